# Optimizing a Trainium2 kernel written in Bass

```python
import math
import jax, jax.numpy as jnp
from jax import lax
import numpy as np

D_MODEL = 1024
BATCH = 8
SEQ = 2048
DEPTH = 4

N_BRANCH = 4
MIX_WIDTH = D_MODEL // N_BRANCH
CONV_WIDTH = 3
SSM_GROUP = 16
SSM_GROUPS = MIX_WIDTH // SSM_GROUP
SSM_STATE = 64
DT_MIN = 1e-3
DT_MAX = 1e-1
LAMBDA_RE_MAX = -1e-4
POOL_WINDOWS = (2, 4, 8, 16)
POOL_GROUP = MIX_WIDTH // len(POOL_WINDOWS)
SB_HEAD_DIM = 64
SB_HEADS = MIX_WIDTH // SB_HEAD_DIM
Q_BLOCK = 128
D_FF = 2816
FFN_RES_WEIGHT = 0.5
N_SUB = 3
EPS = 1e-6

CONV_COLS = 3 * MIX_WIDTH
SSM_COLS = MIX_WIDTH
POOL_COLS = MIX_WIDTH
SB_COLS = 3 * MIX_WIDTH
GATE_COLS = N_BRANCH * D_MODEL
IN_COLS = CONV_COLS + SSM_COLS + POOL_COLS + SB_COLS + GATE_COLS
IN_SPLITS = (CONV_COLS, CONV_COLS + SSM_COLS, CONV_COLS + SSM_COLS + POOL_COLS,
             CONV_COLS + SSM_COLS + POOL_COLS + SB_COLS)

kernel_name = "hybrid_parallel_gated_mixer_trunk"


def rmsnorm(x, g):
    xf = x.astype(jnp.float32)
    y = xf * lax.rsqrt(jnp.mean(xf * xf, axis=-1, keepdims=True) + EPS)
    return (y * g.astype(jnp.float32)).astype(x.dtype)


def modulate(h, shift, scale):
    return h * (1.0 + scale[:, None, :]) + shift[:, None, :]


def swiglu_ffn(h, w_in, w_out):
    a, b = jnp.split(h @ w_in, 2, axis=-1)
    return (jax.nn.silu(a) * b) @ w_out


def short_conv_mixer(p, conv_w, w_out):
    b_g, c_g, v = jnp.split(p, 3, axis=-1)
    u = c_g * v
    L = u.shape[1]
    up = jnp.pad(u, ((0, 0), (CONV_WIDTH - 1, 0), (0, 0)))
    y = conv_w[0] * up[:, 0:L]
    for k in range(1, CONV_WIDTH):
        y = y + conv_w[k] * up[:, k:k + L]
    return (b_g * y) @ w_out


def s5_mixer(u, lam_re, lam_im, log_dt, b_re, b_im, c_re, c_im, d_skip, w_glu):
    f32 = jnp.float32
    Bsz, L, W = u.shape
    uf = u.astype(f32).reshape(Bsz, L, SSM_GROUPS, SSM_GROUP)
    lr = jnp.minimum(lam_re.astype(f32), LAMBDA_RE_MAX)
    li = lam_im.astype(f32)
    dt = jnp.exp(log_dt.astype(f32))[:, None]
    mag = jnp.exp(lr * dt)
    ab_re = mag * jnp.cos(li * dt)
    ab_im = mag * jnp.sin(li * dt)
    den = lr * lr + li * li
    nr = ab_re - 1.0
    f_re = (nr * lr + ab_im * li) / den
    f_im = (ab_im * lr - nr * li) / den
    br = b_re.astype(f32)
    bi = b_im.astype(f32)
    bb_re = f_re[..., None] * br - f_im[..., None] * bi
    bb_im = f_re[..., None] * bi + f_im[..., None] * br
    bu_re = jnp.einsum('blgh,gph->blgp', uf, bb_re)
    bu_im = jnp.einsum('blgh,gph->blgp', uf, bb_im)
    a_re = jnp.broadcast_to(ab_re, bu_re.shape)
    a_im = jnp.broadcast_to(ab_im, bu_im.shape)

    def combine(e1, e2):
        a1r, a1i, b1r, b1i = e1
        a2r, a2i, b2r, b2i = e2
        return (a2r * a1r - a2i * a1i,
                a2r * a1i + a2i * a1r,
                a2r * b1r - a2i * b1i + b2r,
                a2r * b1i + a2i * b1r + b2i)

    _, _, s_re, s_im = lax.associative_scan(combine, (a_re, a_im, bu_re, bu_im), axis=1)
    y = (jnp.einsum('blgp,ghp->blgh', s_re, c_re.astype(f32))
         - jnp.einsum('blgp,ghp->blgh', s_im, c_im.astype(f32)))
    y = y.reshape(Bsz, L, W) + d_skip.astype(f32) * uf.reshape(Bsz, L, W)
    y = jax.nn.gelu(y).astype(u.dtype)
    a, g = jnp.split(y @ w_glu, 2, axis=-1)
    return a * jax.nn.sigmoid(g)


def pool_mixer(u, w_pool, pool_scale, w_out):
    f32 = jnp.float32
    Bsz, L, W = u.shape
    uf = u.astype(f32).reshape(Bsz, L, len(POOL_WINDOWS), POOL_GROUP)
    cs = jnp.cumsum(uf, axis=1)
    pos = jnp.arange(L)
    outs = []
    for gi, w in enumerate(POOL_WINDOWS):
        cg = cs[:, :, gi]
        lag = jnp.pad(cg, ((0, 0), (w, 0), (0, 0)))[:, :L]
        cnt = jnp.minimum(pos + 1, w).astype(f32)[None, :, None]
        outs.append((cg - lag) / cnt - uf[:, :, gi])
    pooled = jnp.stack(outs, axis=2)
    mixed = jnp.einsum('blgc,gcd->blgd', pooled, w_pool.astype(f32)).reshape(Bsz, L, W)
    return (mixed * pool_scale.astype(f32)).astype(u.dtype) @ w_out


def stick_breaking_attention(p, w_out):
    f32 = jnp.float32
    Bsz, L, _ = p.shape
    q, k, v = jnp.split(p, 3, axis=-1)
    q = q.astype(f32).reshape(Bsz, L, SB_HEADS, SB_HEAD_DIM) * (SB_HEAD_DIM ** -0.5)
    k = k.astype(f32).reshape(Bsz, L, SB_HEADS, SB_HEAD_DIM)
    v = v.astype(f32).reshape(Bsz, L, SB_HEADS, SB_HEAD_DIM)
    outs = []
    for start in range(0, L, Q_BLOCK):
        end = start + Q_BLOCK
        z = jnp.einsum('bqhd,bkhd->bhqk', q[:, start:end], k[:, :end])
        t_idx = jnp.arange(start, end)[:, None]
        s_idx = jnp.arange(end)[None, :]
        mask = s_idx < t_idx
        log_keep = jnp.where(mask, jax.nn.log_sigmoid(-z), 0.0)
        log_w = jax.nn.log_sigmoid(z) + lax.cumsum(log_keep, axis=3, reverse=True) - log_keep
        a = jnp.where(mask, jnp.exp(log_w), 0.0)
        outs.append(jnp.einsum('bhqk,bkhd->bqhd', a, v[:, :end]))
    o = jnp.concatenate(outs, axis=1).reshape(Bsz, L, MIX_WIDTH).astype(p.dtype)
    return o @ w_out


def setup_inputs(seed: int = 0) -> dict:
    key = jax.random.key(seed)
    ks = jax.random.split(key, 32)
    f32 = jnp.float32

    def nrm(k, shape, fan_in, gain=1.0):
        return jax.random.normal(k, shape, f32) * (gain * fan_in ** -0.5)

    W = MIX_WIDTH
    x = jax.random.normal(ks[0], (BATCH, SEQ, D_MODEL), f32)
    c = jax.random.normal(ks[1], (BATCH, D_MODEL), f32)
    w_ada = nrm(ks[2], (DEPTH, D_MODEL, N_SUB * 3 * D_MODEL), D_MODEL, 0.1)
    b_ada = 0.01 * jax.random.normal(ks[3], (DEPTH, N_SUB * 3 * D_MODEL), f32)
    g_pre = 1.0 + 0.02 * jax.random.normal(ks[4], (DEPTH, N_SUB, D_MODEL), f32)
    g_post = 1.0 + 0.02 * jax.random.normal(ks[5], (DEPTH, N_SUB, D_MODEL), f32)
    w_ff_in = nrm(ks[6], (DEPTH, 2, D_MODEL, 2 * D_FF), D_MODEL)
    w_ff_out = nrm(ks[7], (DEPTH, 2, D_FF, D_MODEL), D_FF)
    w_in = nrm(ks[8], (DEPTH, D_MODEL, IN_COLS), D_MODEL)
    conv_w = nrm(ks[9], (DEPTH, CONV_WIDTH, W), CONV_WIDTH)
    w_conv_out = nrm(ks[10], (DEPTH, W, D_MODEL), W)
    lam_re = -0.5 + 0.01 * jax.random.normal(ks[11], (DEPTH, SSM_GROUPS, SSM_STATE), f32)
    lam_im = (math.pi * jnp.arange(SSM_STATE, dtype=f32))[None, None, :] \
        + 0.01 * jax.random.normal(ks[12], (DEPTH, SSM_GROUPS, SSM_STATE), f32)
    log_dt = jax.random.uniform(ks[13], (DEPTH, SSM_GROUPS), f32,
                                math.log(DT_MIN), math.log(DT_MAX))
    ssm_b_re = nrm(ks[14], (DEPTH, SSM_GROUPS, SSM_STATE, SSM_GROUP), 2 * SSM_GROUP)
    ssm_b_im = nrm(ks[15], (DEPTH, SSM_GROUPS, SSM_STATE, SSM_GROUP), 2 * SSM_GROUP)
    ssm_c_re = nrm(ks[16], (DEPTH, SSM_GROUPS, SSM_GROUP, SSM_STATE), 2 * SSM_STATE)
    ssm_c_im = nrm(ks[17], (DEPTH, SSM_GROUPS, SSM_GROUP, SSM_STATE), 2 * SSM_STATE)
    ssm_d = jax.random.normal(ks[18], (DEPTH, W), f32)
    w_glu = nrm(ks[19], (DEPTH, W, 2 * D_MODEL), W)
    w_pool = nrm(ks[20], (DEPTH, len(POOL_WINDOWS), POOL_GROUP, POOL_GROUP), POOL_GROUP)
    pool_scale = 1.0 + 0.1 * jax.random.normal(ks[21], (DEPTH, W), f32)
    w_pool_out = nrm(ks[22], (DEPTH, W, D_MODEL), W)
    w_sb_out = nrm(ks[23], (DEPTH, W, D_MODEL), W)
    w_out = nrm(ks[24], (DEPTH, D_MODEL, D_MODEL), D_MODEL)
    return {"x": x, "c": c, "w_ada": w_ada, "b_ada": b_ada, "g_pre": g_pre, "g_post": g_post,
            "w_ff_in": w_ff_in, "w_ff_out": w_ff_out, "w_in": w_in, "conv_w": conv_w,
            "w_conv_out": w_conv_out, "lam_re": lam_re, "lam_im": lam_im, "log_dt": log_dt,
            "ssm_b_re": ssm_b_re, "ssm_b_im": ssm_b_im, "ssm_c_re": ssm_c_re, "ssm_c_im": ssm_c_im,
            "ssm_d": ssm_d, "w_glu": w_glu, "w_pool": w_pool, "pool_scale": pool_scale,
            "w_pool_out": w_pool_out, "w_sb_out": w_sb_out, "w_out": w_out}


def reference(x, c, w_ada, b_ada, g_pre, g_post, w_ff_in, w_ff_out, w_in, conv_w,
              w_conv_out, lam_re, lam_im, log_dt, ssm_b_re, ssm_b_im, ssm_c_re, ssm_c_im,
              ssm_d, w_glu, w_pool, pool_scale, w_pool_out, w_sb_out, w_out):
    Bsz, L, D = x.shape
    c_act = jax.nn.silu(c)
    for l in range(DEPTH):
        ada = (c_act @ w_ada[l] + b_ada[l]).reshape(Bsz, N_SUB, 3, D)

        h = modulate(rmsnorm(x, g_pre[l, 0]), ada[:, 0, 0], ada[:, 0, 1])
        f = swiglu_ffn(h, w_ff_in[l, 0], w_ff_out[l, 0])
        x = x + FFN_RES_WEIGHT * (1.0 + ada[:, 0, 2])[:, None, :] * rmsnorm(f, g_post[l, 0])

        h = modulate(rmsnorm(x, g_pre[l, 1]), ada[:, 1, 0], ada[:, 1, 1])
        p = h @ w_in[l]
        p_conv, p_ssm, p_pool, p_sb, p_gate = jnp.split(p, IN_SPLITS, axis=-1)
        y_a = short_conv_mixer(p_conv, conv_w[l], w_conv_out[l])
        y_b = s5_mixer(p_ssm, lam_re[l], lam_im[l], log_dt[l], ssm_b_re[l], ssm_b_im[l],
                       ssm_c_re[l], ssm_c_im[l], ssm_d[l], w_glu[l])
        y_c = pool_mixer(p_pool, w_pool[l], pool_scale[l], w_pool_out[l])
        y_d = stick_breaking_attention(p_sb, w_sb_out[l])
        gates = jax.nn.sigmoid(p_gate).reshape(Bsz, L, N_BRANCH, D)
        merged = (gates[:, :, 0] * y_a + gates[:, :, 1] * y_b
                  + gates[:, :, 2] * y_c + gates[:, :, 3] * y_d)
        m = merged @ w_out[l]
        x = x + (1.0 + ada[:, 1, 2])[:, None, :] * rmsnorm(m, g_post[l, 1])

        h = modulate(rmsnorm(x, g_pre[l, 2]), ada[:, 2, 0], ada[:, 2, 1])
        f = swiglu_ffn(h, w_ff_in[l, 1], w_ff_out[l, 1])
        x = x + FFN_RES_WEIGHT * (1.0 + ada[:, 2, 2])[:, None, :] * rmsnorm(f, g_post[l, 2])
    return x
```

```python
import math
import os
import numpy as np
import concourse.bass as bass
import concourse.mybir as mybir
from concourse.bass_utils import run_bass_kernel_spmd

F32 = mybir.dt.float32
BF16 = mybir.dt.bfloat16
I32 = mybir.dt.int32
AF = mybir.ActivationFunctionType
ALU = mybir.AluOpType

D = 1024
L = 2048
DEPTH = 4
DFF = 2816
NJ = DFF // 128
EPS = 1e-6
G = 256
ARENA_BYTES = 204 * 1024
POOL_W = (2, 4, 8, 16)
TC = 128


def dsz(dt):
    return 2 if dt == BF16 else 4


class V:
    def __init__(self, ap, lo, hi, shape, es, off):
        self.ap, self.lo, self.hi, self.shape, self.es, self.off = ap, lo, hi, shape, es, off

    def keys(self):
        return [("s", i) for i in range(self.lo // G, (self.hi - 1) // G + 1)]

    def __getitem__(self, key):
        if not isinstance(key, tuple):
            key = (key,)
        sh = self.shape
        if len(sh) == 1:
            (k,) = key
            a, b = (k.start or 0, sh[0] if k.stop is None else k.stop) if isinstance(k, slice) else (k, k + 1)
            ap = self.ap[:, a:b]
            return V(ap, self.off + a * self.es, self.off + b * self.es, (b - a,), self.es, self.off + a * self.es)
        k0 = key[0]
        k1 = key[1] if len(key) > 1 else slice(None)
        if isinstance(k0, slice):
            i0, i1 = k0.start or 0, sh[0] if k0.stop is None else k0.stop
        else:
            i0, i1 = k0, k0 + 1
        c0, c1 = k1.start or 0, sh[1] if k1.stop is None else k1.stop
        lo = self.off + (i0 * sh[1] + c0) * self.es
        hi = self.off + ((i1 - 1) * sh[1] + c1) * self.es
        if isinstance(k0, slice):
            ap = self.ap[:, i0:i1, c0:c1]
            return V(ap, lo, hi, (i1 - i0, c1 - c0), self.es, lo) if (c0 == 0 and c1 == sh[1]) else V(ap, lo, hi, None, self.es, lo)
        ap = self.ap[:, i0, c0:c1]
        return V(ap, lo, hi, (c1 - c0,), self.es, lo)

    def p(self, p0, p1):
        key = (slice(p0, p1),) + (slice(None),) * (len(self.ap.shape) - 1)
        return V(self.ap[key], self.lo, self.hi, self.shape, self.es, self.off)

    def with_ap(self, ap):
        return V(ap, self.lo, self.hi, None, self.es, self.off)


class PV:
    def __init__(self, ap, bank):
        self.ap, self.bank = ap, bank

    def keys(self):
        return [("p", self.bank)]

    def __getitem__(self, key):
        return PV(self.ap[key], self.bank)


class Op:
    __slots__ = ("id", "eng", "fn", "deps", "dma", "marked", "sem", "val", "qidx")


class Sched:
    ENGS = ("pe", "act", "dve", "pool", "sp")
    EPOCH = 20000
    KDMA = 8

    def __init__(self):
        self.ops = []
        self.W = {}
        self.R = {}
        self.by_eng = {e: [] for e in self.ENGS}
        self.ndma = {"pool": 0, "sp": 0}
        self.out_dmas = []

    def add(self, eng, fn, reads=(), writes=(), dma=False, is_out=False):
        op = Op()
        op.id = len(self.ops)
        op.eng, op.fn, op.dma, op.marked = eng, fn, dma, False
        op.sem = op.val = None
        deps = set()
        W, R = self.W, self.R
        for r in reads:
            for k in r.keys():
                w = W.get(k)
                if w is not None:
                    o = self.ops[w]
                    if o.dma or dma or o.eng != eng or eng != "pe":
                        deps.add(w)
        for wv in writes:
            for k in wv.keys():
                w = W.get(k)
                if w is not None:
                    o = self.ops[w]
                    if o.dma or dma or o.eng != eng or eng != "pe":
                        deps.add(w)
                rd = R.get(k)
                if rd:
                    for rk, rid in rd.items():
                        o = self.ops[rid]
                        if o.dma or dma or o.eng != eng or eng != "pe":
                            deps.add(rid)
                W[k] = op.id
                R[k] = {}
        for r in reads:
            for k in r.keys():
                R.setdefault(k, {})[("d", op.id) if dma else eng] = op.id
        deps.discard(op.id)
        if dma:
            q = self.ndma[eng]
            self.ndma[eng] = q + 1
            op.qidx = q
        op.deps = deps
        self.ops.append(op)
        self.by_eng[eng].append(op)
        if is_out:
            self.out_dmas.append(op)
        return op

    def emit(self, nc, block_engines, sems, dsems):
        for op in self.ops:
            for d in op.deps:
                self.ops[d].marked = True
        for e in self.ENGS:
            cnt = 0
            for op in self.by_eng[e]:
                if op.dma:
                    op.sem = dsems[e][op.qidx % self.KDMA]
                    op.val = 16 * (op.qidx // self.KDMA + 1)
                elif op.marked:
                    ep = cnt // self.EPOCH
                    op.sem = sems[e][ep]
                    op.val = cnt % self.EPOCH + 1
                    cnt += 1
        for e in self.ENGS:
            eh = block_engines[e]
            waited = {}

            def wait(sem, val):
                key = id(sem)
                if waited.get(key, 0) < val:
                    eh.wait_ge(sem, val)
                    waited[key] = val

            for op in self.by_eng[e]:
                for d in sorted(op.deps):
                    o = self.ops[d]
                    wait(o.sem, o.val)
                if op.dma and op.qidx >= self.KDMA:
                    wait(op.sem, op.val - 16)
                ins = op.fn(eh)
                if op.dma:
                    ins.then_inc(op.sem, 16)
                elif op.marked:
                    ins.then_inc(op.sem, 1)
            if e == "sp":
                for o in self.out_dmas:
                    wait(o.sem, o.val)


class Builder:
    def __init__(self, nc, n_sub, dbg):
        self.nc = nc
        self.S = Sched()
        self.n_sub = n_sub
        self.dbg = dbg
        self.off = 0
        self.bank_rr = 0

    def alloc(self, shape, dt):
        es = dsz(dt)
        n = int(np.prod(shape))
        nbytes = (n * es + 255) // 256 * 256
        off = self.off
        self.off += nbytes
        assert self.off <= ARENA_BYTES, f"arena overflow {self.off}"
        w0 = off // 4
        ap = self.arena[:, w0:w0 + nbytes // 4]
        if dt != F32:
            ap = ap.bitcast(dt)
        ap = ap[:, 0:n]
        if len(shape) == 2:
            ap = ap.rearrange("p (a b) -> p a b", a=shape[0])
        return V(ap, off, off + n * es, tuple(shape), es, off)

    def mark(self):
        return self.off

    def release(self, m):
        self.off = m

    def bank(self, pool=None):
        pool = pool or self.rot
        b = pool[self.bank_rr % len(pool)]
        self.bank_rr += 1
        return self.banks[b]

    def mm(self, out, lhsT, rhs, start, stop):
        self.S.add("pe", lambda e: e.matmul(out.ap, lhsT=lhsT.ap, rhs=rhs.ap, start=start, stop=stop, skip_group_check=True),
                   reads=[lhsT, rhs], writes=[out])

    def tr(self, out, in_, ident):
        self.S.add("pe", lambda e: e.transpose(out.ap, in_.ap, ident.ap), reads=[in_, ident], writes=[out])

    def act(self, out, in_, func, scale=1.0, bias=None, extra_reads=()):
        kw = {}
        rd = [in_] + list(extra_reads)
        if isinstance(scale, V):
            rd.append(scale)
            sc = scale.ap
        else:
            sc = scale
        if bias is not None:
            rd.append(bias)
            kw["bias"] = bias.ap
        self.S.add("act", lambda e: e.activation(out=out.ap, in_=in_.ap, func=func, scale=sc, **kw), reads=rd, writes=[out])

    def tt(self, out, a, b, op, eng="dve"):
        self.S.add(eng, lambda e: e.tensor_tensor(out=out.ap, in0=a.ap, in1=b.ap, op=op), reads=[a, b], writes=[out])

    def ts(self, out, a, s1, op0, s2=None, op1=None, eng="dve"):
        rd = [a]
        s1a = s1.ap if isinstance(s1, V) else s1
        s2a = s2.ap if isinstance(s2, V) else s2
        if isinstance(s1, V):
            rd.append(s1)
        if isinstance(s2, V):
            rd.append(s2)
        if op1 is None:
            self.S.add(eng, lambda e: e.tensor_scalar(out=out.ap, in0=a.ap, scalar1=s1a, scalar2=None, op0=op0), reads=rd, writes=[out])
        else:
            self.S.add(eng, lambda e: e.tensor_scalar(out=out.ap, in0=a.ap, scalar1=s1a, scalar2=s2a, op0=op0, op1=op1), reads=rd, writes=[out])

    def stt(self, out, a, s, b, op0, op1):
        rd = [a, b]
        sa = s.ap if isinstance(s, V) else s
        if isinstance(s, V):
            rd.append(s)
        self.S.add("dve", lambda e: e.scalar_tensor_tensor(out=out.ap, in0=a.ap, scalar=sa, in1=b.ap, op0=op0, op1=op1), reads=rd, writes=[out])

    def copy(self, out, in_, eng="dve"):
        self.S.add(eng, lambda e: e.tensor_copy(out=out.ap, in_=in_.ap), reads=[in_], writes=[out])

    def scan(self, out, d0, d1, init, extra_reads=()):
        rd = [d0, d1] + list(extra_reads)
        ia = init.ap if isinstance(init, V) else init
        if isinstance(init, V):
            rd.append(init)
        self.S.add("dve", lambda e: e.tensor_tensor_scan(out=out.ap, data0=d0.ap, data1=d1.ap, initial=ia, op0=ALU.mult, op1=ALU.add), reads=rd, writes=[out])

    def recip(self, out, in_):
        self.S.add("dve", lambda e: e.reciprocal(out=out.ap, in_=in_.ap), reads=[in_], writes=[out])

    def memset(self, out, val, eng="dve"):
        self.S.add(eng, lambda e: e.memset(out.ap, val), writes=[out])

    def dma(self, out, in_ap, q="pool", reads=(), **kw):
        self.S.add(q, lambda e: e.dma_start(out=out.ap, in_=in_ap, **kw), reads=list(reads), writes=[out], dma=True)

    def dma_out(self, out_ap, in_):
        self.S.add("sp", lambda e: e.dma_start(out=out_ap, in_=in_.ap), reads=[in_], dma=True, is_out=True)

    def wslot(self, kc, ncols):
        s = self.ring[self.ring_i % len(self.ring)]
        self.ring_i += 1
        ap = s.ap[:, 0:kc * ncols].rearrange("p (k c) -> p k c", k=kc)
        return V(ap, s.lo, s.lo + kc * ncols * 2, (kc, ncols), 2, s.lo)

    def wload(self, w2d, kc, c0, ncols, slot=None, scol=0):
        if slot is None:
            slot = self.wslot(kc, ncols)
            dst = slot
        else:
            dst = slot[0:kc, scol:scol + ncols]
        src = w2d[:, c0:c0 + ncols].rearrange("(k p) c -> p k c", p=128)
        self.dma(dst, src, q="pool")
        return slot

    def sumsq_rstd(self, srcs, rstd):
        ps = self.bank()
        for k in range(8):
            sq = self.sqb[k % 2]
            self.act(sq, srcs[k], AF.Square)
            self.mm(ps, self.ones_bf, sq, k == 0, k == 7)
        self.act(rstd, ps, AF.Sqrt, scale=1.0 / D, bias=self.epsb)
        self.recip(rstd, rstd)

    def pre_norm(self, sub, hT, t0, ntt):
        for tt in range(ntt):
            ts_ = t0 + tt * 512
            rstd = self.rstd[tt % 2]
            self.sumsq_rstd([self.xT[k, ts_:ts_ + 512] for k in range(8)], rstd)
            for k in range(8):
                tmp = self.tmpf[k % 2]
                self.tt(tmp, self.xT[k, ts_:ts_ + 512], rstd, ALU.mult)
                self.act(hT[k, tt * 512:(tt + 1) * 512], tmp, AF.Identity, scale=self.coefA[sub][k:k + 1], bias=self.coefB[sub][k:k + 1])

    def post_norm_res(self, sub, fT, t0, ntt):
        for tt in range(ntt):
            ts_ = t0 + tt * 512
            rstd = self.rstd[tt % 2]
            self.sumsq_rstd([fT[k, tt * 512:(tt + 1) * 512] for k in range(8)], rstd)
            for k in range(8):
                tmp = self.tmpf[k % 2]
                self.tt(tmp, fT[k, tt * 512:(tt + 1) * 512], rstd, ALU.mult)
                xs = self.xT[k, ts_:ts_ + 512]
                self.stt(xs, tmp, self.coefG[sub][k:k + 1], xs, ALU.mult, ALU.add)

    def ffn(self, l, sub, fi):
        w_in = self.w["w_ff_in"][l, fi]
        w_out = self.w["w_ff_out"][l, fi]
        for hf in range(2):
            t0 = hf * 1024
            m = self.mark()
            hT = self.alloc([8, 1024], BF16)
            gT = self.alloc([NJ, 1024], BF16)
            fT = self.alloc([8, 1024], F32)
            self.pre_norm(sub, hT, t0, 2)
            for jg in range(NJ // 2):
                slot = self.wslot(8, 512)
                self.wload(w_in, 8, jg * 256, 256, slot, 0)
                self.wload(w_in, 8, DFF + jg * 256, 256, slot, 256)
                for jj in range(2):
                    j = jg * 2 + jj
                    for tt in range(2):
                        pa = self.bank()
                        pb = self.bank()
                        for k in range(8):
                            self.mm(pa, slot[k, jj * 128:(jj + 1) * 128], hT[k, tt * 512:(tt + 1) * 512], k == 0, k == 7)
                        for k in range(8):
                            self.mm(pb, slot[k, 256 + jj * 128:256 + (jj + 1) * 128], hT[k, tt * 512:(tt + 1) * 512], k == 0, k == 7)
                        sa = self.tmpf[(j * 2 + tt) % 2]
                        self.act(sa, pa, AF.Silu)
                        self.tt(gT[j, tt * 512:(tt + 1) * 512], sa, pb, ALU.mult)
            for n in range(8):
                slot = self.wload(w_out, NJ, n * 128, 128)
                for tt in range(2):
                    pf = self.bank()
                    for k in range(NJ):
                        self.mm(pf, slot[k, 0:128], gT[k, tt * 512:(tt + 1) * 512], k == 0, k == NJ - 1)
                    self.act(fT[n, tt * 512:(tt + 1) * 512], pf, AF.Identity)
            self.post_norm_res(sub, fT, t0, 2)
            self.release(m)

    def layer_params(self, l):
        w = self.w
        vr = self.vrows
        self.dma(vr.p(0, 72), w["b_ada"][l].rearrange("(j p) -> j p", p=128), q="sp")
        self.dma(vr.p(72, 96), w["g_pre"][l].rearrange("s (k p) -> (s k) p", p=128), q="sp")
        self.dma(vr.p(96, 120), w["g_post"][l].rearrange("s (k p) -> (s k) p", p=128), q="sp")
        ps = self.bank()
        self.tr(ps[:, 0:120], vr.p(0, 120), self.ident.p(0, 120)[0:120])
        self.copy(self.vecs[0:120], ps[:, 0:120])
        ada = self.ada
        for cg in range(18):
            slot = self.wload(w["w_ada"][l], 8, cg * 512, 512)
            for jj in range(4):
                j = cg * 4 + jj
                pa = self.bank()
                for k in range(8):
                    self.mm(pa[:, 0:2], slot[k, jj * 128:(jj + 1) * 128], self.cT2[k * 2:k * 2 + 2], k == 0, k == 7)
                self.tt(ada[j:j + 1], pa[:, 0:1], self.vecs[j:j + 1], ALU.add)
        for s in range(3):
            resw = 1.0 if s == 1 else 0.5
            self.stt(self.coefA[s], ada[s * 24 + 8:s * 24 + 16], 1.0, self.vecs[72 + s * 8:72 + s * 8 + 8], ALU.add, ALU.mult)
            self.copy(self.coefB[s], ada[s * 24:s * 24 + 8])
            self.ts(self.tmp8, ada[s * 24 + 16:s * 24 + 24], 1.0, ALU.add, resw, ALU.mult)
            self.tt(self.coefG[s], self.tmp8, self.vecs[96 + s * 8:96 + s * 8 + 8], ALU.mult)

    def setup(self):
        nc = self.nc
        al = self.alloc
        self.xT = al([8, L], F32)
        self.ident = al([128], F32)
        self.ones_bf = al([128], BF16)
        self.onesf = al([128], F32)
        self.zerosf = al([512], F32)
        self.epsb = al([1], F32)
        self.one1 = al([1], F32)
        self.cT = al([8], BF16)
        self.cT2 = al([16], BF16)
        self.cTf = al([8], F32)
        self.vrows = al([128], F32)
        self.vecs = al([128], F32)
        self.ada = al([72], F32)
        self.coefA = [al([8], F32) for _ in range(3)]
        self.coefB = [al([8], F32) for _ in range(3)]
        self.coefG = [al([8], F32) for _ in range(3)]
        self.tmp8 = al([8], F32)
        self.ring = [al([4096], BF16) for _ in range(3)]
        self.ring_i = 0
        self.sqb = [al([512], BF16) for _ in range(2)]
        self.rstd = [al([512], F32) for _ in range(2)]
        self.tmpf = [al([512], F32) for _ in range(2)]
        self.memset(self.onesf, 1.0, "pool")
        self.memset(self.zerosf, 0.0, "pool")
        self.memset(self.epsb, EPS, "pool")
        self.memset(self.one1, 1.0, "pool")
        self.memset(self.vrows, 0.0, "pool")
        self.S.add("pool", lambda e: e.affine_select(out=self.ident.ap, in_=self.onesf.ap, pattern=[[-1, 128]], compare_op=ALU.is_equal,
                                                     fill=0.0, base=0, channel_multiplier=1), reads=[self.onesf], writes=[self.ident])
        self.copy(self.ones_bf, self.onesf)
        import os
        crow = self.vrows
        if not os.environ.get("SKIPC"):
            self.dma(crow.p(0, 8), self.w["c"].rearrange("(k p) -> k p", p=128), q="sp")
            ps = self.bank()
            self.tr(ps[:, 0:8], crow.p(0, 8), self.ident.p(0, 8)[0:8])
            self.act(self.cTf, ps[:, 0:8], AF.Silu)
            self.copy(self.cT, self.cTf)
            self.copy(self.cT2.with_ap(self.cT2.ap.rearrange("p (k t) -> p k t", t=2)), self.cTf.with_ap(self.cTf.ap.unsqueeze(2).to_broadcast([128, 8, 2])))
        mk = self.mark()
        self.xrow = [self.alloc([1024], F32) for _ in range(2)]
        xin = self.w["x"]
        for tb in range(16):
            xr = self.xrow[tb % 2]
            self.dma(xr, xin[tb * 128:(tb + 1) * 128, :], q="sp")
            for k in range(8):
                ps = self.bank()
                self.tr(ps[:, 0:128], xr[k * 128:(k + 1) * 128], self.ident)
                if k % 2 == 0:
                    self.copy(self.xT[k, tb * 128:(tb + 1) * 128], ps[:, 0:128])
                else:
                    self.act(self.xT[k, tb * 128:(tb + 1) * 128], ps[:, 0:128], AF.Identity)
        self.release(mk)

    def store_out(self):
        out = self.out_ap
        self.xrow = [self.alloc([1024], F32) for _ in range(2)]
        for tb in range(16):
            xr = self.xrow[tb % 2]
            for k in range(8):
                ps = self.bank()
                self.tr(ps[:, 0:128], self.xT[k, tb * 128:(tb + 1) * 128], self.ident)
                if k % 2 == 0:
                    self.copy(xr[k * 128:(k + 1) * 128], ps[:, 0:128])
                else:
                    self.act(xr[k * 128:(k + 1) * 128], ps[:, 0:128], AF.Identity)
            self.dma_out(out[tb * 128:(tb + 1) * 128, :], xr)

    def bc3(self, v, n, m):
        return v.with_ap(v.ap.unsqueeze(2).to_broadcast([128, n, m]))

    def v3(self, v, b):
        if isinstance(v, PV):
            return PV(v.ap.rearrange("p (a b) -> p a b", b=b), v.bank)
        return v.with_ap(v.ap.rearrange("p (a b) -> p a b", b=b))

    def tb3(self, v, n, b):
        return v.with_ap(v.ap.unsqueeze(1).to_broadcast([128, n, b]))

    def pool_op(self, fn, reads, writes):
        self.S.add("pool", fn, reads=reads, writes=writes)

    def mixer_consts(self, l):
        al = self.alloc
        w = self.w
        mc = {}
        mc["ident_bf"] = al([128], BF16)
        self.copy(mc["ident_bf"], self.ident)
        mc["tri"] = al([128], BF16)
        mc["negones"] = al([128], BF16)
        negf = self.tmpf[0][0:128]
        trif = self.tmpf[1][0:128]
        self.memset(negf, -1.0)
        self.copy(mc["negones"], negf)
        self.pool_op(lambda e: e.affine_select(out=trif.ap, in_=negf.ap, pattern=[[-1, 128]], compare_op=ALU.is_ge, fill=0.0, base=0,
                                               channel_multiplier=1), [negf], [trif])
        self.copy(mc["tri"], trif)
        mc["maskb"] = []
        for i in range(4):
            mb = al([512], BF16)
            tmp = self.rstd[i % 2]
            self.pool_op((lambda tmp, i: lambda e: e.affine_select(out=tmp.ap, in_=self.zerosf.ap, pattern=[[1, 512]], compare_op=ALU.is_gt,
                                                                  fill=-30000.0, base=-128 * i, channel_multiplier=-1))(tmp, i),
                         [self.zerosf], [tmp])
            self.copy(mb, tmp)
            mc["maskb"].append(mb)
        vr = self.vrows
        self.dma(vr.p(0, 8), w["lam_re"][l].rearrange("(t g) p -> t (g p)", g=2), q="sp")
        self.dma(vr.p(8, 16), w["lam_im"][l].rearrange("(t g) p -> t (g p)", g=2), q="sp")
        self.dma(vr.p(16, 22), w["conv_w"][l].rearrange("k (c p) -> (k c) p", p=128), q="sp")
        self.dma(vr.p(22, 24), w["ssm_d"][l].rearrange("(c p) -> c p", p=128), q="sp")
        self.dma(vr.p(24, 26), w["pool_scale"][l].rearrange("(c p) -> c p", p=128), q="sp")
        ps = self.bank()
        self.tr(ps[:, 0:26], vr.p(0, 26), self.ident.p(0, 26)[0:26])
        mc["vecs2"] = al([32], F32)
        self.copy(mc["vecs2"][0:26], ps[:, 0:26])
        return mc

    def sincos(self, dsin, dcos, y, yi, ta, tb):
        self.copy(yi, y)
        self.copy(ta, yi)
        self.tt(tb, y, ta, ALU.subtract)
        self.act(dsin, tb, AF.Sin, scale=6.283185)
        self.ts(tb, y, 0.25, ALU.add)
        self.copy(yi, tb)
        self.copy(ta, yi)
        self.tt(tb, tb, ta, ALU.subtract)
        self.act(dcos, tb, AF.Sin, scale=6.283185)

    def ssm_branch(self, l, hT, ub, mc):
        w = self.w
        al = self.alloc
        MUL, ADD, SUB = ALU.mult, ALU.add, ALU.subtract
        m = self.mark()
        v2 = mc["vecs2"]
        lrT, liT = v2[0:8], v2[8:16]
        dcol = v2[22:24]
        sp = al([20 * 8], F32)
        c8 = [sp[i * 8:(i + 1) * 8] for i in range(20)]
        lr, dt, a_, th, mag, thn, ta8, tb8, sn, cs, abr, abi, den, nr, fre, fim, t8, nEi, inre, inim = c8
        yi8 = al([8], I32)
        NT = TC + 1
        lb = al([16, 128], BF16)
        lc = al([16, 128], BF16)
        sinT = al([8, NT], F32)
        cosT = al([8, NT], F32)
        ms_ = self.mark()
        rows = al([272], F32)
        E0 = rows[0:128].p(0, 1)
        E1 = rows[128:256].p(0, 1)
        ld = rows[256:272].p(0, 1)
        self.memset(rows.p(0, 1), 0.0)
        self.memset(rows[0:64].p(0, 1), 1.0)
        self.memset(rows[192:256].p(0, 1), 1.0)
        self.dma(ld, w["log_dt"][l].unsqueeze(0), q="sp")
        ps = self.bank()
        self.mm(ps[:, 0:8], E0, ld.with_ap(ld.ap[:, 0:16:2]), True, False)
        self.mm(ps[:, 0:8], E1, ld.with_ap(ld.ap[:, 1:16:2]), False, True)
        self.act(dt, ps[:, 0:8], AF.Exp)
        self.ts(lr, lrT, -1e-4, ALU.min)
        self.tt(a_, lr, dt, MUL)
        self.tt(th, liT, dt, MUL)
        self.act(mag, a_, AF.Exp)
        self.ts(thn, th, 1.0 / (2 * math.pi), MUL)
        self.sincos(sn, cs, thn, yi8, ta8, tb8)
        self.tt(abr, mag, cs, MUL)
        self.tt(abi, mag, sn, MUL)
        self.tt(den, lr, lr, MUL)
        self.tt(t8, liT, liT, MUL)
        self.tt(den, den, t8, ADD)
        self.recip(den, den)
        self.ts(nr, abr, -1.0, ADD)
        self.tt(fre, nr, lr, MUL)
        self.tt(t8, abi, liT, MUL)
        self.tt(fre, fre, t8, ADD)
        self.tt(fre, fre, den, MUL)
        self.tt(fim, abi, lr, MUL)
        self.tt(t8, nr, liT, MUL)
        self.tt(fim, fim, t8, SUB)
        self.tt(fim, fim, den, MUL)
        if int(os.environ.get("SSMSTOP", "9")) <= 1:
            self.memset(ub, 0.0)
            self.release(m)
            return
        pidx = al([1], I32)
        pj = al([1], I32)
        pf = al([1], F32)
        rm = al([4], F32)
        par = al([2], F32)
        self.pool_op(lambda e: e.iota(pidx.ap, pattern=[[0, 1]], base=0, channel_multiplier=1), [], [pidx])
        self.S.add("dve", lambda e: e.tensor_scalar(out=pj.ap, in0=pidx.ap, scalar1=5, scalar2=None, op0=ALU.logical_shift_right), reads=[pidx], writes=[pj])
        self.copy(pf, pj)
        for q in range(4):
            self.ts(rm[q:q + 1], pf, float(q), ALU.is_equal)
        self.S.add("dve", lambda e: e.tensor_scalar(out=pj.ap, in0=pidx.ap, scalar1=4, scalar2=1, op0=ALU.logical_shift_right, op1=ALU.bitwise_and),
                   reads=[pidx], writes=[pj])
        self.copy(par[1:2], pj)
        self.ts(par[0:1], par[1:2], -1.0, MUL, 1.0, ADD)
        Bre = al([8, 32], F32)
        Bim = al([8, 32], F32)
        bbr = al([8, 32], F32)
        bbi = al([8, 32], F32)
        tb1 = al([8, 32], F32)
        for Bt, nm in ((Bre, "ssm_b_re"), (Bim, "ssm_b_im")):
            self.memset(Bt, 0.0)
            src = w[nm][l].rearrange("(t g) p h -> g p t h", g=2)
            self.dma(Bt.with_ap(Bt.ap[0:64, :, 0:16]), src[0], q="sp")
            self.dma(Bt.with_ap(Bt.ap[64:128, :, 16:32]), src[1], q="sp")
        fr3 = self.bc3(fre, 8, 32)
        fi3 = self.bc3(fim, 8, 32)
        self.tt(bbr, Bre, fr3, MUL)
        self.tt(tb1, Bim, fi3, MUL)
        self.tt(bbr, bbr, tb1, SUB)
        self.tt(bbi, Bim, fr3, MUL)
        self.tt(tb1, Bre, fi3, MUL)
        self.tt(bbi, bbi, tb1, ADD)
        for ri, bb in enumerate((bbr, bbi)):
            for c in range(2):
                ps = self.bank()
                src = bb.with_ap(bb.ap[:, 4 * c:4 * c + 4, :].rearrange("p a b -> p (a b)"))
                self.tr(ps[:, 0:128], src, self.ident)
                for q in range(4):
                    self.ts(lb[ri * 8 + 4 * c + q], ps[:, 0:128], rm[q:q + 1], MUL)
        if int(os.environ.get("SSMSTOP", "9")) <= 2:
            self.memset(ub, 0.0)
            self.release(m)
            return
        self.memset(lc, 0.0)
        InC = al([128], F32)
        for ri, nm in enumerate(("ssm_c_re", "ssm_c_im")):
            for c in range(2):
                src = w[nm][l].rearrange("g h p -> (g h) p")[c * 128:(c + 1) * 128, :]
                self.dma(InC[0:64], src, q="sp")
                self.dma(InC[64:128], src, q="sp")
                self.ts(InC[0:64], InC[0:64], par[0:1], MUL)
                self.ts(InC[64:128], InC[64:128], par[1:2], MUL)
                ps = self.bank()
                self.tr(ps[:, 0:128], InC, self.ident)
                for q in range(4):
                    dst = lc[ri * 8 + 4 * c + q][32 * q:32 * q + 32]
                    if ri == 0:
                        self.copy(dst, ps[:, 32 * q:32 * q + 32])
                    else:
                        self.ts(dst, ps[:, 32 * q:32 * q + 32], -1.0, MUL)
        if int(os.environ.get("SSMSTOP", "9")) <= 3:
            self.memset(ub, 0.0)
            self.release(m)
            return
        iot_i = al([NT], I32)
        iot_f = al([NT], F32)
        self.pool_op(lambda e: e.iota(iot_i.ap, pattern=[[1, NT]], base=0, channel_multiplier=0), [], [iot_i])
        self.copy(iot_f, iot_i)
        mt = self.mark()
        yt = al([8, NT], F32)
        yi = al([8, NT], I32)
        ta = al([8, NT], F32)
        tb = al([8, NT], F32)
        for T in range(8):
            self.ts(yt[T], iot_f, thn[T:T + 1], MUL)
        self.sincos(sinT, cosT, yt, yi, ta, tb)
        self.release(mt)
        for T in range(8):
            self.ts(nEi[T:T + 1], sinT[T, TC:TC + 1], -1.0, MUL)
        if int(os.environ.get("SSMSTOP", "9")) <= 4:
            self.memset(ub, 0.0)
            self.release(m)
            return
        self.release(ms_)
        uS = al([1024], F32)
        uSb = al([1024], BF16)
        sre = al([1024], F32)
        sim = al([1024], F32)
        sbre = al([1024], BF16)
        sbim = al([1024], BF16)
        sre2 = al([1024], F32)
        sim2 = al([1024], F32)
        tA = al([512], F32)
        tB = al([512], F32)
        yv = tA
        y2 = tB
        t1c = al([1], F32)
        wsl = al([8, 256], BF16)
        self.wload(w["w_in"][l], 8, 768, 256, wsl, 0)
        ybanks = [self.banks[6], self.banks[7]]
        rot = [0, 1, 2, 3, 4, 5]
        nb = 512 // TC
        NCH = 1024 // TC
        for c in range(2):
            for hf in range(2):
                t0 = hf * 1024
                for tt in range(2):
                    lt = slice(tt * 512, (tt + 1) * 512)
                    ps = self.bank(rot)
                    for k in range(8):
                        self.mm(ps, wsl[k, c * 128:(c + 1) * 128], hT[k, t0 + tt * 512:t0 + (tt + 1) * 512], k == 0, k == 7)
                    self.act(uS[lt], ps, AF.Identity)
                    self.copy(uSb[lt], ps)
                for q in range(4):
                    T = 4 * c + q
                    cb = self.tb3(cosT[T, 0:TC], nb, TC)
                    sb_ = self.tb3(sinT[T, 0:TC], nb, TC)
                    Er = cosT[T, TC:TC + 1]
                    Ei = sinT[T, TC:TC + 1]
                    magb = mag[T:T + 1].with_ap(mag[T:T + 1].ap.to_broadcast([128, TC]))
                    tA3, tB3 = self.v3(tA, TC), self.v3(tB, TC)
                    for tt in range(2):
                        lt = slice(tt * 512, (tt + 1) * 512)
                        pr = self.bank(rot)
                        pi = self.bank(rot)
                        self.mm(pr, lb[T], uSb[lt], True, True)
                        self.mm(pi, lb[8 + T], uSb[lt], True, True)
                        self.act(sre[lt], pr, AF.Identity)
                        self.act(sim[lt], pi, AF.Identity)
                        r3, i3 = self.v3(sre[lt], TC), self.v3(sim[lt], TC)
                        self.tt(tA3, r3, cb, MUL)
                        self.tt(tB3, i3, sb_, MUL)
                        self.tt(sre2[lt], tA, tB, ADD)
                        self.tt(tA3, i3, cb, MUL)
                        self.tt(tB3, r3, sb_, MUL)
                        self.tt(sim2[lt], tA, tB, SUB)
                    for ch in range(NCH):
                        gch = hf * NCH + ch
                        seg = slice(ch * TC, (ch + 1) * TC)
                        self.scan(sre[seg], magb, sre2[seg], 0.0 if gch == 0 else inre[T:T + 1])
                        self.scan(sim[seg], magb, sim2[seg], 0.0 if gch == 0 else inim[T:T + 1])
                        if gch < 2 * NCH - 1:
                            lre = sre[(ch + 1) * TC - 1:(ch + 1) * TC]
                            lim = sim[(ch + 1) * TC - 1:(ch + 1) * TC]
                            self.ts(t1c, lre, Er, MUL)
                            self.ts(inre[T:T + 1], lim, nEi[T:T + 1], MUL, t1c, ADD)
                            self.ts(t1c, lim, Er, MUL)
                            self.ts(inim[T:T + 1], lre, Ei, MUL, t1c, ADD)
                    for tt in range(2):
                        lt = slice(tt * 512, (tt + 1) * 512)
                        sr3, si3 = self.v3(sre[lt], TC), self.v3(sim[lt], TC)
                        self.tt(tA3, sr3, cb, MUL)
                        self.tt(tB3, si3, sb_, MUL)
                        self.tt(sbre[lt], tA, tB, SUB)
                        self.tt(tA3, si3, cb, MUL)
                        self.tt(tB3, sr3, sb_, MUL)
                        self.tt(sbim[lt], tA, tB, ADD)
                        self.mm(ybanks[tt], lc[T], sbre[lt], q == 0, False)
                        self.mm(ybanks[tt], lc[8 + T], sbim[lt], False, q == 3)
                for tt in range(2):
                    lt = slice(tt * 512, (tt + 1) * 512)
                    self.ts(yv, uS[lt], dcol[c:c + 1], MUL)
                    self.tt(yv, yv, ybanks[tt], ADD)
                    self.tt(y2, yv, yv, MUL)
                    self.ts(y2, y2, 1.5957691216 * 0.044715, MUL, 1.5957691216, ADD)
                    self.tt(y2, y2, yv, MUL)
                    self.act(y2, y2, AF.Sigmoid)
                    self.tt(ub[c, t0 + tt * 512:t0 + (tt + 1) * 512], yv, y2, MUL)
        self.release(m)

    def conv_branch(self, l, hT, ua, mc):
        al = self.alloc
        win = self.w["w_in"][l]
        MUL, ADD = ALU.mult, ALU.add
        m = self.mark()
        v2 = mc["vecs2"]
        upad = al([2 + L], F32)
        csb = al([512], F32)
        yt = al([512], F32)
        s1 = self.wload(win, 8, 0, 512)
        s2 = self.wload(win, 8, 512, 256)
        for i in range(2):
            self.memset(upad[0:2], 0.0)
            for tt in range(4):
                tok = slice(tt * 512, (tt + 1) * 512)
                pc = self.bank()
                pv = self.bank()
                pb = self.bank()
                for k in range(8):
                    self.mm(pc, s1[k, 256 + i * 128:256 + (i + 1) * 128], hT[k, tok], k == 0, k == 7)
                for k in range(8):
                    self.mm(pv, s2[k, i * 128:(i + 1) * 128], hT[k, tok], k == 0, k == 7)
                for k in range(8):
                    self.mm(pb, s1[k, i * 128:(i + 1) * 128], hT[k, tok], k == 0, k == 7)
                self.act(csb, pc, AF.Identity)
                b0 = tt * 512
                self.tt(upad[2 + b0:2 + b0 + 512], csb, pv, MUL)
                self.ts(yt, upad[b0:b0 + 512], v2[16 + i:17 + i], MUL)
                self.stt(yt, upad[1 + b0:1 + b0 + 512], v2[18 + i:19 + i], yt, MUL, ADD)
                self.stt(yt, upad[2 + b0:2 + b0 + 512], v2[20 + i:21 + i], yt, MUL, ADD)
                self.tt(ua[i, tok], yt, pb, MUL)
        self.release(m)

    def pool_branch(self, l, hT, uc, mc):
        al = self.alloc
        w = self.w
        win = w["w_in"][l]
        MUL, ADD, SUB = ALU.mult, ALU.add, ALU.subtract
        m = self.mark()
        v2 = mc["vecs2"]
        uP = al([L], F32)
        csp = al([16 + L], F32)
        dP = al([L], F32)
        pl = al([L], BF16)
        lp = al([2, 128], BF16)
        invc = al([2, 16], F32)
        t16 = al([16], F32)
        io_i = al([16], I32)
        io_f = al([16], F32)
        self.pool_op(lambda e: e.iota(io_i.ap, pattern=[[1, 16]], base=1, channel_multiplier=0), [], [io_i])
        self.copy(io_f, io_i)
        for c in range(2):
            for h2 in range(2):
                self.ts(invc[c].p(64 * h2, 64 * h2 + 64), io_f.p(64 * h2, 64 * h2 + 64), float(POOL_W[2 * c + h2]), ALU.min)
        self.recip(invc, invc)
        self.memset(lp, 0.0)
        for c in range(2):
            for h2 in range(2):
                self.dma(lp[c].p(64 * h2, 64 * h2 + 64)[64 * h2:64 * h2 + 64], w["w_pool"][l, 2 * c + h2], q="pool")
        sl = self.wload(win, 8, 1024, 256)
        onesb = self.onesf[0:1].with_ap(self.onesf[0:1].ap.to_broadcast([128, 512]))
        self.memset(csp[0:16], 0.0)
        for c in range(2):
            for tt in range(4):
                tok = slice(tt * 512, (tt + 1) * 512)
                ps = self.bank()
                for k in range(8):
                    self.mm(ps, sl[k, c * 128:(c + 1) * 128], hT[k, tok], k == 0, k == 7)
                self.act(uP[tok], ps, AF.Identity)
                self.scan(csp[16 + tt * 512:16 + (tt + 1) * 512], onesb, uP[tok], csp[15 + tt * 512:16 + tt * 512])
            for h2 in range(2):
                w_ = POOL_W[2 * c + h2]
                P = (64 * h2, 64 * h2 + 64)
                self.tt(dP.p(*P), csp[16:16 + L].p(*P), csp[16 - w_:16 - w_ + L].p(*P), SUB)
                self.stt(pl.p(*P), dP.p(*P), 1.0 / w_, uP.p(*P), MUL, SUB)
            self.tt(t16, dP[0:16], invc[c], MUL)
            self.tt(pl[0:16], t16, uP[0:16], SUB)
            for tt in range(4):
                tok = slice(tt * 512, (tt + 1) * 512)
                ps = self.bank()
                self.mm(ps, lp[c], pl[tok], True, True)
                self.act(uc[c, tok], ps, AF.Identity, scale=v2[24 + c:25 + c])
        self.release(m)

    def attn_branch(self, l, hT, ud, mc):
        al = self.alloc
        win = self.w["w_in"][l]
        m = self.mark()
        qT = al([L], BF16)
        kT = al([L], BF16)
        vS = al([16, 128], BF16)
        e_ = al([512], F32)
        spb = [al([512], BF16) for _ in range(2)]
        At = [al([512], F32) for _ in range(2)]
        Gt = al([512], F32)
        Ab = [al([512], BF16) for _ in range(2)]
        rot = [0, 1, 2, 3]
        ui = 0
        for c in range(2):
            sq = self.wload(win, 8, 1280 + c * 128, 128)
            sk = self.wload(win, 8, 1536 + c * 128, 128)
            sv = self.wload(win, 8, 1792 + c * 128, 128)
            for tt in range(4):
                tok = slice(tt * 512, (tt + 1) * 512)
                ps = self.bank(rot)
                for k in range(8):
                    self.mm(ps, sq[k, 0:128], hT[k, tok], k == 0, k == 7)
                self.act(qT[tok], ps, AF.Identity, scale=0.125)
                ps = self.bank(rot)
                for k in range(8):
                    self.mm(ps, sk[k, 0:128], hT[k, tok], k == 0, k == 7)
                self.copy(kT[tok], ps)
            for tb in range(16):
                ps = self.bank(rot)
                for k in range(8):
                    self.mm(ps[:, 0:128], hT[k, tb * 128:(tb + 1) * 128], sv[k, 0:128], k == 0, k == 7)
                if tb % 2 == 0:
                    self.copy(vS[tb], ps[:, 0:128])
                else:
                    self.act(vS[tb], ps[:, 0:128], AF.Identity)
            for hh in range(2):
                P = (64 * hh, 64 * hh + 64)
                for qt in range(4):
                    chain = hh * 4 + qt
                    Ob, Rb = (self.banks[4], self.banks[5]) if chain % 2 == 0 else (self.banks[6], self.banks[7])
                    ObP = PV(Ob.ap[P[0]:P[1], :], Ob.bank)
                    nkb = 4 * (qt + 1)
                    qv = qT[qt * 512:(qt + 1) * 512].p(*P)
                    first = True
                    for kb in range(nkb - 1, -1, -1):
                        zb = self.bank(rot)
                        diag = kb >= 4 * qt
                        sp_ = spb[ui % 2]
                        A = Ab[ui % 2]
                        at = At[ui % 2]
                        ui += 1
                        self.mm(zb, kT[kb * 128:(kb + 1) * 128].p(*P), qv, True, not diag)
                        if diag:
                            self.mm(zb, mc["ident_bf"], mc["maskb"][kb - 4 * qt], False, True)
                        self.act(e_, zb, AF.Exp)
                        self.act(sp_, e_, AF.Ln, bias=self.one1)
                        if not first:
                            self.act(Gt, Rb, AF.Exp)
                        self.mm(zb, mc["tri"], sp_, False, True)
                        if kb > 0:
                            self.mm(Rb, mc["negones"], sp_, first, True)
                        if first:
                            self.act(A, zb, AF.Exp)
                        else:
                            self.act(at, zb, AF.Exp)
                            self.tt(A, at, Gt, ALU.mult)
                        self.mm(ObP, vS[kb, 64 * hh:64 * hh + 64], A, first, kb == 0)
                        first = False
                    self.act(ud[c, qt * 512:(qt + 1) * 512].p(*P), ObP, AF.Identity)
        self.release(m)

    def merge_out(self, l, hT, us):
        al = self.alloc
        w = self.w
        win = w["w_in"][l]
        MUL, ADD = ALU.mult, ALU.add
        ynames = ["w_conv_out", None, "w_pool_out", "w_sb_out"]
        rot = [0, 1, 2, 3, 4, 5, 6]
        pss = self.banks[7]
        for hf in range(2):
            t0 = hf * 1024
            m = self.mark()
            mg = al([8, 1024], BF16)
            acc = al([1024], F32)
            sg = al([512], F32)
            sg2 = al([512], F32)
            yt = al([512], F32)
            pr = al([512], F32)
            for n in range(8):
                for b in range(4):
                    sgw = self.wload(win, 8, 2048 + b * 1024 + n * 128, 128)
                    if b == 1:
                        sy = self.wslot(2, 256)
                        self.wload(w["w_glu"][l], 2, n * 128, 128, sy, 0)
                        self.wload(w["w_glu"][l], 2, 1024 + n * 128, 128, sy, 128)
                    else:
                        sy = self.wload(w[ynames[b]][l], 2, n * 128, 128)
                    u = us[b]
                    for tt in range(2):
                        tok = slice(t0 + tt * 512, t0 + (tt + 1) * 512)
                        lt = slice(tt * 512, (tt + 1) * 512)
                        pg = self.bank(rot)
                        for k in range(8):
                            self.mm(pg, sgw[k, 0:128], hT[k, tok], k == 0, k == 7)
                        py = self.bank(rot)
                        for k in range(2):
                            self.mm(py, sy[k, 0:128], u[k, tok], k == 0, k == 1)
                        self.act(sg, pg, AF.Sigmoid)
                        dst = acc[lt] if b == 0 else pr
                        if b == 1:
                            pyg = self.bank(rot)
                            for k in range(2):
                                self.mm(pyg, sy[k, 128:256], u[k, tok], k == 0, k == 1)
                            self.act(sg2, pyg, AF.Sigmoid)
                            self.tt(yt, sg2, py, MUL)
                            self.tt(dst, sg, yt, MUL)
                        else:
                            self.tt(dst, sg, py, MUL)
                        if b in (1, 2):
                            self.tt(acc[lt], acc[lt], pr, ADD)
                        elif b == 3:
                            self.tt(mg[n, lt], acc[lt], pr, ADD)
            so = [self.wload(w["w_out"][l], 8, 0, 512), self.wload(w["w_out"][l], 8, 512, 512)]
            for tt in range(2):
                lt = slice(tt * 512, (tt + 1) * 512)
                ts_ = t0 + tt * 512
                rstd = self.rstd[tt % 2]
                for n in range(8):
                    ps = self.bank(rot)
                    for k in range(8):
                        self.mm(ps, so[n // 4][k, (n % 4) * 128:(n % 4 + 1) * 128], mg[k, lt], k == 0, k == 7)
                    sq = self.sqb[n % 2]
                    self.act(sq, ps, AF.Square)
                    self.mm(pss, self.ones_bf, sq, n == 0, n == 7)
                self.act(rstd, pss, AF.Sqrt, scale=1.0 / D, bias=self.epsb)
                self.recip(rstd, rstd)
                for n in range(8):
                    ps = self.bank(rot)
                    for k in range(8):
                        self.mm(ps, so[n // 4][k, (n % 4) * 128:(n % 4 + 1) * 128], mg[k, lt], k == 0, k == 7)
                    tmp = self.tmpf[n % 2]
                    self.tt(tmp, ps, rstd, MUL)
                    xs = self.xT[n, ts_:ts_ + 512]
                    self.stt(xs, tmp, self.coefG[1][n:n + 1], xs, MUL, ADD)
            self.release(m)

    def dump_u(self, i, u):
        if self.dbg:
            dst = self.dbg_ap[i]
            self.S.add("pool", lambda e: e.dma_start(out=dst, in_=u.ap), reads=[u], dma=True, is_out=True)

    def mixer(self, l):
        al = self.alloc
        m0 = self.mark()
        hT = al([8, L], BF16)
        self.pre_norm(1, hT, 0, 4)
        mc = self.mixer_consts(l)
        br = os.environ.get("BR", "scpam")
        ub = al([2, L], BF16)
        if "s" in br:
            self.ssm_branch(l, hT, ub, mc)
        else:
            self.memset(ub, 0.0)
        ua = al([2, L], BF16)
        if "c" in br:
            self.conv_branch(l, hT, ua, mc)
        else:
            self.memset(ua, 0.0)
        uc = al([2, L], BF16)
        if "p" in br:
            self.pool_branch(l, hT, uc, mc)
        else:
            self.memset(uc, 0.0)
        ud = al([2, L], BF16)
        if "a" in br:
            self.attn_branch(l, hT, ud, mc)
        else:
            self.memset(ud, 0.0)
        for i, u in enumerate((ua, ub, uc, ud)):
            self.dump_u(i, u)
        if "m" in br:
            self.merge_out(l, hT, (ua, ub, uc, ud))
        self.release(m0)


    def build(self):
        nc = self.nc
        w = {}
        shapes = dict(x=[L, D], c=[D], w_ada=[DEPTH, D, 9 * D], b_ada=[DEPTH, 9 * D], g_pre=[DEPTH, 3, D], g_post=[DEPTH, 3, D],
                      w_ff_in=[DEPTH, 2, D, 2 * DFF], w_ff_out=[DEPTH, 2, DFF, D], w_in=[DEPTH, D, 6144], conv_w=[DEPTH, 3, 256],
                      w_conv_out=[DEPTH, 256, D], lam_re=[DEPTH, 16, 64], lam_im=[DEPTH, 16, 64], log_dt=[DEPTH, 16],
                      ssm_b_re=[DEPTH, 16, 64, 16], ssm_b_im=[DEPTH, 16, 64, 16], ssm_c_re=[DEPTH, 16, 16, 64], ssm_c_im=[DEPTH, 16, 16, 64],
                      ssm_d=[DEPTH, 256], w_glu=[DEPTH, 256, 2 * D], w_pool=[DEPTH, 4, 64, 64], pool_scale=[DEPTH, 256],
                      w_pool_out=[DEPTH, 256, D], w_sb_out=[DEPTH, 256, D], w_out=[DEPTH, D, D])
        import os
        for k, s in shapes.items():
            if os.environ.get("ONLYX") and k not in ("x", "c"):
                continue
            w[k] = nc.dram_tensor(k, s, F32, kind="ExternalInput").ap()
        self.w = w
        self.out_ap = nc.dram_tensor("out", [L, D], F32, kind="ExternalOutput").ap()
        if self.dbg:
            self.dbg_ap = nc.dram_tensor("dbg_u", [4, 128, 2, L], F32, kind="ExternalOutput").ap()
        from contextlib import ExitStack
        with ExitStack() as es:
            arena = es.enter_context(nc.sbuf_tensor("arena", [128, ARENA_BYTES // 4], F32))
            self.arena = arena[:]
            self.banks = []
            for b in range(8):
                pt = es.enter_context(nc.psum_tensor(f"ps{b}", [128, 512], F32))
                self.banks.append(PV(pt[:], b))
            self.rot = list(range(8))
            sems = {e: [es.enter_context(nc.semaphore(f"s_{e}_{i}")) for i in range(4)] for e in ("pe", "act", "dve", "pool")}
            sems["sp"] = sems["pool"]
            dsems = {q: [es.enter_context(nc.semaphore(f"d_{q}_{i}")) for i in range(Sched.KDMA)] for q in ("pool", "sp")}
            m0 = self.mark()
            self.xrow = None
            self.xT_only = True
            self.setup_alloc_xrow = True
            self._setup_with_xrow()
            done = 0
            for l in range(DEPTH):
                if done >= self.n_sub:
                    break
                self.layer_params(l)
                self.ffn(l, 0, 0)
                done += 1
                if done >= self.n_sub:
                    break
                self.mixer(l)
                done += 1
                if done >= self.n_sub:
                    break
                self.ffn(l, 2, 1)
                done += 1
            self.store_out()
            block = es.enter_context(nc.Block())
            S = self.S

            @block.tensor
            def _(e):
                S._cur = e
                S.emit_one(nc, "pe", e, sems, dsems)

            @block.scalar
            def _(e):
                S.emit_one(nc, "act", e, sems, dsems)

            @block.vector
            def _(e):
                S.emit_one(nc, "dve", e, sems, dsems)

            @block.gpsimd
            def _(e):
                S.emit_one(nc, "pool", e, sems, dsems)

            @block.sync
            def _(e):
                S.emit_one(nc, "sp", e, sems, dsems)
        return nc

    def _setup_with_xrow(self):
        orig_alloc = self.alloc
        self.xrow = None
        self.setup()


def _emit_one(self, nc, e, eh, sems, dsems):
    if not getattr(self, "_prepared", False):
        for op in self.ops:
            for d in op.deps:
                self.ops[d].marked = True
        for en in self.ENGS:
            cnt = 0
            for op in self.by_eng[en]:
                if op.dma:
                    op.sem = dsems[en][op.qidx % self.KDMA]
                    op.val = 16 * (op.qidx // self.KDMA + 1)
                elif op.marked:
                    ep = cnt // self.EPOCH
                    op.sem = sems[en][ep] if en != "sp" else None
                    op.val = cnt % self.EPOCH + 1
                    cnt += 1
        self._prepared = True
    waited = {}

    def wait(sem, val):
        key = id(sem)
        if waited.get(key, 0) < val:
            eh.wait_ge(sem, val)
            waited[key] = val

    for op in self.by_eng[e]:
        for d in sorted(op.deps):
            o = self.ops[d]
            wait(o.sem, o.val)
        if op.dma and op.qidx >= self.KDMA:
            wait(op.sem, op.val - 16)
        ins = op.fn(eh)
        if op.dma:
            ins.then_inc(op.sem, 16)
        elif op.marked:
            ins.then_inc(op.sem, 1)
    if e == "sp":
        for o in self.out_dmas:
            wait(o.sem, o.val)


Sched.emit_one = _emit_one

_CACHE = {}


def build_nc(n_sub=12, dbg=False):
    nc = bass.Bass("TRN2", target_bir_lowering=False)
    b = Builder(nc, n_sub, dbg)
    b.build()
    return nc


def kernel(n_sub=12, **inputs):
    x = np.ascontiguousarray(inputs["x"], dtype=np.float32)
    c = np.ascontiguousarray(inputs["c"], dtype=np.float32)
    nc = build_nc(n_sub)
    shared = {k: np.ascontiguousarray(v, dtype=np.float32) for k, v in inputs.items() if k not in ("x", "c")}
    in_maps = []
    for b in range(8):
        m = dict(shared)
        m["x"] = x[b]
        m["c"] = c[b]
        in_maps.append(m)
    res = run_bass_kernel_spmd(nc, in_maps, core_ids=list(range(8)))
    out = np.stack([res.results[b]["out"] for b in range(8)], axis=0)
    return out.astype(np.float32)
```

```python
import math
import os
import numpy as np
import concourse.bass as bass
import concourse.mybir as mybir
from concourse.bass_utils import run_bass_kernel_spmd

F32 = mybir.dt.float32
BF16 = mybir.dt.bfloat16
I32 = mybir.dt.int32
AF = mybir.ActivationFunctionType
ALU = mybir.AluOpType

D = 1024
L = 2048
DEPTH = 4
DFF = 2816
NJ = DFF // 128
EPS = 1e-6
G = 256
ARENA_BYTES = 204 * 1024
POOL_W = (2, 4, 8, 16)
TC = 128


def dsz(dt):
    return 2 if dt == BF16 else 4


class V:
    def __init__(self, ap, lo, hi, shape, es, off):
        self.ap, self.lo, self.hi, self.shape, self.es, self.off = ap, lo, hi, shape, es, off

    def keys(self):
        return [("s", i) for i in range(self.lo // G, (self.hi - 1) // G + 1)]

    def __getitem__(self, key):
        if not isinstance(key, tuple):
            key = (key,)
        sh = self.shape
        if len(sh) == 1:
            (k,) = key
            a, b = (k.start or 0, sh[0] if k.stop is None else k.stop) if isinstance(k, slice) else (k, k + 1)
            ap = self.ap[:, a:b]
            return V(ap, self.off + a * self.es, self.off + b * self.es, (b - a,), self.es, self.off + a * self.es)
        k0 = key[0]
        k1 = key[1] if len(key) > 1 else slice(None)
        if isinstance(k0, slice):
            i0, i1 = k0.start or 0, sh[0] if k0.stop is None else k0.stop
        else:
            i0, i1 = k0, k0 + 1
        c0, c1 = k1.start or 0, sh[1] if k1.stop is None else k1.stop
        lo = self.off + (i0 * sh[1] + c0) * self.es
        hi = self.off + ((i1 - 1) * sh[1] + c1) * self.es
        if isinstance(k0, slice):
            ap = self.ap[:, i0:i1, c0:c1]
            return V(ap, lo, hi, (i1 - i0, c1 - c0), self.es, lo) if (c0 == 0 and c1 == sh[1]) else V(ap, lo, hi, None, self.es, lo)
        ap = self.ap[:, i0, c0:c1]
        return V(ap, lo, hi, (c1 - c0,), self.es, lo)

    def p(self, p0, p1):
        key = (slice(p0, p1),) + (slice(None),) * (len(self.ap.shape) - 1)
        return V(self.ap[key], self.lo, self.hi, self.shape, self.es, self.off)

    def with_ap(self, ap):
        return V(ap, self.lo, self.hi, None, self.es, self.off)


class PV:
    def __init__(self, ap, bank):
        self.ap, self.bank = ap, bank

    def keys(self):
        return [("p", self.bank)]

    def __getitem__(self, key):
        return PV(self.ap[key], self.bank)


class Op:
    __slots__ = ("id", "eng", "fn", "deps", "dma", "marked", "sem", "val", "qidx")


class Sched:
    ENGS = ("pe", "act", "dve", "pool", "sp")
    EPOCH = 20000
    KDMA = 8

    def __init__(self):
        self.ops = []
        self.W = {}
        self.R = {}
        self.by_eng = {e: [] for e in self.ENGS}
        self.ndma = {"pool": 0, "sp": 0}
        self.out_dmas = []

    def add(self, eng, fn, reads=(), writes=(), dma=False, is_out=False):
        op = Op()
        op.id = len(self.ops)
        op.eng, op.fn, op.dma, op.marked = eng, fn, dma, False
        op.sem = op.val = None
        deps = set()
        W, R = self.W, self.R
        for r in reads:
            for k in r.keys():
                w = W.get(k)
                if w is not None:
                    o = self.ops[w]
                    if o.dma or dma or o.eng != eng or eng != "pe":
                        deps.add(w)
        for wv in writes:
            for k in wv.keys():
                w = W.get(k)
                if w is not None:
                    o = self.ops[w]
                    if o.dma or dma or o.eng != eng or eng != "pe":
                        deps.add(w)
                rd = R.get(k)
                if rd:
                    for rk, rid in rd.items():
                        o = self.ops[rid]
                        if o.dma or dma or o.eng != eng or eng != "pe":
                            deps.add(rid)
                W[k] = op.id
                R[k] = {}
        for r in reads:
            for k in r.keys():
                R.setdefault(k, {})[("d", op.id) if dma else eng] = op.id
        deps.discard(op.id)
        if dma:
            q = self.ndma[eng]
            self.ndma[eng] = q + 1
            op.qidx = q
        op.deps = deps
        self.ops.append(op)
        self.by_eng[eng].append(op)
        if is_out:
            self.out_dmas.append(op)
        return op

    def emit(self, nc, block_engines, sems, dsems):
        for op in self.ops:
            for d in op.deps:
                self.ops[d].marked = True
        for e in self.ENGS:
            cnt = 0
            for op in self.by_eng[e]:
                if op.dma:
                    op.sem = dsems[e][op.qidx % self.KDMA]
                    op.val = 16 * (op.qidx // self.KDMA + 1)
                elif op.marked:
                    ep = cnt // self.EPOCH
                    op.sem = sems[e][ep]
                    op.val = cnt % self.EPOCH + 1
                    cnt += 1
        for e in self.ENGS:
            eh = block_engines[e]
            waited = {}

            def wait(sem, val):
                key = id(sem)
                if waited.get(key, 0) < val:
                    eh.wait_ge(sem, val)
                    waited[key] = val

            for op in self.by_eng[e]:
                for d in sorted(op.deps):
                    o = self.ops[d]
                    wait(o.sem, o.val)
                if op.dma and op.qidx >= self.KDMA:
                    wait(op.sem, op.val - 16)
                ins = op.fn(eh)
                if op.dma:
                    ins.then_inc(op.sem, 16)
                elif op.marked:
                    ins.then_inc(op.sem, 1)
            if e == "sp":
                for o in self.out_dmas:
                    wait(o.sem, o.val)


class Builder:
    def __init__(self, nc, n_sub, dbg):
        self.nc = nc
        self.S = Sched()
        self.n_sub = n_sub
        self.dbg = dbg
        self.off = 0
        self.bank_rr = 0

    def alloc(self, shape, dt):
        es = dsz(dt)
        n = int(np.prod(shape))
        nbytes = (n * es + 255) // 256 * 256
        off = self.off
        self.off += nbytes
        assert self.off <= ARENA_BYTES, f"arena overflow {self.off}"
        w0 = off // 4
        ap = self.arena[:, w0:w0 + nbytes // 4]
        if dt != F32:
            ap = ap.bitcast(dt)
        ap = ap[:, 0:n]
        if len(shape) == 2:
            ap = ap.rearrange("p (a b) -> p a b", a=shape[0])
        return V(ap, off, off + n * es, tuple(shape), es, off)

    def mark(self):
        return self.off

    def release(self, m):
        self.off = m

    def bank(self, pool=None):
        pool = pool or self.rot
        b = pool[self.bank_rr % len(pool)]
        self.bank_rr += 1
        return self.banks[b]

    def mm(self, out, lhsT, rhs, start, stop):
        self.S.add("pe", lambda e: e.matmul(out.ap, lhsT=lhsT.ap, rhs=rhs.ap, start=start, stop=stop, skip_group_check=True),
                   reads=[lhsT, rhs], writes=[out])

    def tr(self, out, in_, ident):
        self.S.add("pe", lambda e: e.transpose(out.ap, in_.ap, ident.ap), reads=[in_, ident], writes=[out])

    def act(self, out, in_, func, scale=1.0, bias=None, extra_reads=()):
        kw = {}
        rd = [in_] + list(extra_reads)
        if isinstance(scale, V):
            rd.append(scale)
            sc = scale.ap
        else:
            sc = scale
        if bias is not None:
            rd.append(bias)
            kw["bias"] = bias.ap
        self.S.add("act", lambda e: e.activation(out=out.ap, in_=in_.ap, func=func, scale=sc, **kw), reads=rd, writes=[out])

    def tt(self, out, a, b, op, eng="dve"):
        self.S.add(eng, lambda e: e.tensor_tensor(out=out.ap, in0=a.ap, in1=b.ap, op=op), reads=[a, b], writes=[out])

    def ts(self, out, a, s1, op0, s2=None, op1=None, eng="dve"):
        rd = [a]
        s1a = s1.ap if isinstance(s1, V) else s1
        s2a = s2.ap if isinstance(s2, V) else s2
        if isinstance(s1, V):
            rd.append(s1)
        if isinstance(s2, V):
            rd.append(s2)
        if op1 is None:
            self.S.add(eng, lambda e: e.tensor_scalar(out=out.ap, in0=a.ap, scalar1=s1a, scalar2=None, op0=op0), reads=rd, writes=[out])
        else:
            self.S.add(eng, lambda e: e.tensor_scalar(out=out.ap, in0=a.ap, scalar1=s1a, scalar2=s2a, op0=op0, op1=op1), reads=rd, writes=[out])

    def stt(self, out, a, s, b, op0, op1):
        rd = [a, b]
        sa = s.ap if isinstance(s, V) else s
        if isinstance(s, V):
            rd.append(s)
        self.S.add("dve", lambda e: e.scalar_tensor_tensor(out=out.ap, in0=a.ap, scalar=sa, in1=b.ap, op0=op0, op1=op1), reads=rd, writes=[out])

    def copy(self, out, in_, eng="dve"):
        self.S.add(eng, lambda e: e.tensor_copy(out=out.ap, in_=in_.ap), reads=[in_], writes=[out])

    def scan(self, out, d0, d1, init, extra_reads=()):
        rd = [d0, d1] + list(extra_reads)
        ia = init.ap if isinstance(init, V) else init
        if isinstance(init, V):
            rd.append(init)
        self.S.add("dve", lambda e: e.tensor_tensor_scan(out=out.ap, data0=d0.ap, data1=d1.ap, initial=ia, op0=ALU.mult, op1=ALU.add), reads=rd, writes=[out])

    def recip(self, out, in_):
        self.S.add("dve", lambda e: e.reciprocal(out=out.ap, in_=in_.ap), reads=[in_], writes=[out])

    def memset(self, out, val, eng="dve"):
        self.S.add(eng, lambda e: e.memset(out.ap, val), writes=[out])

    def dma(self, out, in_ap, q="pool", reads=(), **kw):
        self.S.add(q, lambda e: e.dma_start(out=out.ap, in_=in_ap, **kw), reads=list(reads), writes=[out], dma=True)

    def dma_out(self, out_ap, in_):
        self.S.add("sp", lambda e: e.dma_start(out=out_ap, in_=in_.ap), reads=[in_], dma=True, is_out=True)

    def wslot(self, kc, ncols):
        s = self.ring[self.ring_i % len(self.ring)]
        self.ring_i += 1
        ap = s.ap[:, 0:kc * ncols].rearrange("p (k c) -> p k c", k=kc)
        return V(ap, s.lo, s.lo + kc * ncols * 2, (kc, ncols), 2, s.lo)

    def wload(self, w2d, kc, c0, ncols, slot=None, scol=0):
        if slot is None:
            slot = self.wslot(kc, ncols)
            dst = slot
        else:
            dst = slot[0:kc, scol:scol + ncols]
        src = w2d[:, c0:c0 + ncols].rearrange("(k p) c -> p k c", p=128)
        self.dma(dst, src, q="pool")
        return slot

    def sumsq_rstd(self, srcs, rstd):
        ps = self.bank()
        for k in range(8):
            sq = self.sqb[k % 2]
            self.act(sq, srcs[k], AF.Square)
            self.mm(ps, self.ones_bf, sq, k == 0, k == 7)
        self.act(rstd, ps, AF.Sqrt, scale=1.0 / D, bias=self.epsb)
        self.recip(rstd, rstd)

    def pre_norm(self, sub, hT, t0, ntt):
        for tt in range(ntt):
            ts_ = t0 + tt * 512
            rstd = self.rstd[tt % 2]
            self.sumsq_rstd([self.xT[k, ts_:ts_ + 512] for k in range(8)], rstd)
            for k in range(8):
                tmp = self.tmpf[k % 2]
                self.tt(tmp, self.xT[k, ts_:ts_ + 512], rstd, ALU.mult)
                self.act(hT[k, tt * 512:(tt + 1) * 512], tmp, AF.Identity, scale=self.coefA[sub][k:k + 1], bias=self.coefB[sub][k:k + 1])

    def post_norm_res(self, sub, fT, t0, ntt):
        for tt in range(ntt):
            ts_ = t0 + tt * 512
            rstd = self.rstd[tt % 2]
            self.sumsq_rstd([fT[k, tt * 512:(tt + 1) * 512] for k in range(8)], rstd)
            for k in range(8):
                tmp = self.tmpf[k % 2]
                self.tt(tmp, fT[k, tt * 512:(tt + 1) * 512], rstd, ALU.mult)
                xs = self.xT[k, ts_:ts_ + 512]
                self.stt(xs, tmp, self.coefG[sub][k:k + 1], xs, ALU.mult, ALU.add)

    def ffn(self, l, sub, fi):
        w_in = self.w["w_ff_in"][l, fi]
        w_out = self.w["w_ff_out"][l, fi]
        for hf in range(2):
            t0 = hf * 1024
            m = self.mark()
            hT = self.alloc([8, 1024], BF16)
            gT = self.alloc([NJ, 1024], BF16)
            fT = self.alloc([8, 1024], F32)
            self.pre_norm(sub, hT, t0, 2)
            for jg in range(NJ // 2):
                slot = self.wslot(8, 512)
                self.wload(w_in, 8, jg * 256, 256, slot, 0)
                self.wload(w_in, 8, DFF + jg * 256, 256, slot, 256)
                for jj in range(2):
                    j = jg * 2 + jj
                    for tt in range(2):
                        pa = self.bank()
                        pb = self.bank()
                        for k in range(8):
                            self.mm(pa, slot[k, jj * 128:(jj + 1) * 128], hT[k, tt * 512:(tt + 1) * 512], k == 0, k == 7)
                        for k in range(8):
                            self.mm(pb, slot[k, 256 + jj * 128:256 + (jj + 1) * 128], hT[k, tt * 512:(tt + 1) * 512], k == 0, k == 7)
                        sa = self.tmpf[(j * 2 + tt) % 2]
                        self.act(sa, pa, AF.Silu)
                        self.tt(gT[j, tt * 512:(tt + 1) * 512], sa, pb, ALU.mult)
            for n in range(8):
                slot = self.wload(w_out, NJ, n * 128, 128)
                for tt in range(2):
                    pf = self.bank()
                    for k in range(NJ):
                        self.mm(pf, slot[k, 0:128], gT[k, tt * 512:(tt + 1) * 512], k == 0, k == NJ - 1)
                    self.act(fT[n, tt * 512:(tt + 1) * 512], pf, AF.Identity)
            self.post_norm_res(sub, fT, t0, 2)
            self.release(m)

    def layer_params(self, l):
        w = self.w
        vr = self.vrows
        self.dma(vr.p(0, 72), w["b_ada"][l].rearrange("(j p) -> j p", p=128), q="sp")
        self.dma(vr.p(72, 96), w["g_pre"][l].rearrange("s (k p) -> (s k) p", p=128), q="sp")
        self.dma(vr.p(96, 120), w["g_post"][l].rearrange("s (k p) -> (s k) p", p=128), q="sp")
        ps = self.bank()
        self.tr(ps[:, 0:120], vr.p(0, 120), self.ident.p(0, 120)[0:120])
        self.copy(self.vecs[0:120], ps[:, 0:120])
        ada = self.ada
        for cg in range(18):
            slot = self.wload(w["w_ada"][l], 8, cg * 512, 512)
            for jj in range(4):
                j = cg * 4 + jj
                pa = self.bank()
                for k in range(8):
                    self.mm(pa[:, 0:2], slot[k, jj * 128:(jj + 1) * 128], self.cT2[k * 2:k * 2 + 2], k == 0, k == 7)
                self.tt(ada[j:j + 1], pa[:, 0:1], self.vecs[j:j + 1], ALU.add)
        for s in range(3):
            resw = 1.0 if s == 1 else 0.5
            self.stt(self.coefA[s], ada[s * 24 + 8:s * 24 + 16], 1.0, self.vecs[72 + s * 8:72 + s * 8 + 8], ALU.add, ALU.mult)
            self.copy(self.coefB[s], ada[s * 24:s * 24 + 8])
            self.ts(self.tmp8, ada[s * 24 + 16:s * 24 + 24], 1.0, ALU.add, resw, ALU.mult)
            self.tt(self.coefG[s], self.tmp8, self.vecs[96 + s * 8:96 + s * 8 + 8], ALU.mult)

    def setup(self):
        nc = self.nc
        al = self.alloc
        self.xT = al([8, L], F32)
        self.ident = al([128], F32)
        self.ones_bf = al([128], BF16)
        self.onesf = al([128], F32)
        self.zerosf = al([512], F32)
        self.epsb = al([1], F32)
        self.one1 = al([1], F32)
        self.cT = al([8], BF16)
        self.cT2 = al([16], BF16)
        self.cTf = al([8], F32)
        self.vrows = al([128], F32)
        self.vecs = al([128], F32)
        self.ada = al([72], F32)
        self.coefA = [al([8], F32) for _ in range(3)]
        self.coefB = [al([8], F32) for _ in range(3)]
        self.coefG = [al([8], F32) for _ in range(3)]
        self.tmp8 = al([8], F32)
        self.ring = [al([4096], BF16) for _ in range(3)]
        self.ring_i = 0
        self.sqb = [al([512], BF16) for _ in range(2)]
        self.rstd = [al([512], F32) for _ in range(2)]
        self.tmpf = [al([512], F32) for _ in range(2)]
        self.memset(self.onesf, 1.0, "pool")
        self.memset(self.zerosf, 0.0, "pool")
        self.memset(self.epsb, EPS, "pool")
        self.memset(self.one1, 1.0, "pool")
        self.memset(self.vrows, 0.0, "pool")
        self.S.add("pool", lambda e: e.affine_select(out=self.ident.ap, in_=self.onesf.ap, pattern=[[-1, 128]], compare_op=ALU.is_equal,
                                                     fill=0.0, base=0, channel_multiplier=1), reads=[self.onesf], writes=[self.ident])
        self.copy(self.ones_bf, self.onesf)
        import os
        crow = self.vrows
        if not os.environ.get("SKIPC"):
            self.dma(crow.p(0, 8), self.w["c"].rearrange("(k p) -> k p", p=128), q="sp")
            ps = self.bank()
            self.tr(ps[:, 0:8], crow.p(0, 8), self.ident.p(0, 8)[0:8])
            self.act(self.cTf, ps[:, 0:8], AF.Silu)
            self.copy(self.cT, self.cTf)
            self.copy(self.cT2.with_ap(self.cT2.ap.rearrange("p (k t) -> p k t", t=2)), self.cTf.with_ap(self.cTf.ap.unsqueeze(2).to_broadcast([128, 8, 2])))
        mk = self.mark()
        self.xrow = [self.alloc([1024], F32) for _ in range(2)]
        xin = self.w["x"]
        for tb in range(16):
            xr = self.xrow[tb % 2]
            self.dma(xr, xin[tb * 128:(tb + 1) * 128, :], q="sp")
            for k in range(8):
                ps = self.bank()
                self.tr(ps[:, 0:128], xr[k * 128:(k + 1) * 128], self.ident)
                if k % 2 == 0:
                    self.copy(self.xT[k, tb * 128:(tb + 1) * 128], ps[:, 0:128])
                else:
                    self.act(self.xT[k, tb * 128:(tb + 1) * 128], ps[:, 0:128], AF.Identity)
        self.release(mk)

    def store_out(self):
        out = self.out_ap
        self.xrow = [self.alloc([1024], F32) for _ in range(2)]
        for tb in range(16):
            xr = self.xrow[tb % 2]
            for k in range(8):
                ps = self.bank()
                self.tr(ps[:, 0:128], self.xT[k, tb * 128:(tb + 1) * 128], self.ident)
                if k % 2 == 0:
                    self.copy(xr[k * 128:(k + 1) * 128], ps[:, 0:128])
                else:
                    self.act(xr[k * 128:(k + 1) * 128], ps[:, 0:128], AF.Identity)
            self.dma_out(out[tb * 128:(tb + 1) * 128, :], xr)

    def bc3(self, v, n, m):
        return v.with_ap(v.ap.unsqueeze(2).to_broadcast([128, n, m]))

    def v3(self, v, b):
        if isinstance(v, PV):
            return PV(v.ap.rearrange("p (a b) -> p a b", b=b), v.bank)
        return v.with_ap(v.ap.rearrange("p (a b) -> p a b", b=b))

    def tb3(self, v, n, b):
        return v.with_ap(v.ap.unsqueeze(1).to_broadcast([128, n, b]))

    def pool_op(self, fn, reads, writes):
        self.S.add("pool", fn, reads=reads, writes=writes)

    def mixer_consts(self, l):
        al = self.alloc
        w = self.w
        mc = {}
        mc["ident_bf"] = al([128], BF16)
        self.copy(mc["ident_bf"], self.ident)
        mc["tri"] = al([128], BF16)
        mc["negones"] = al([128], BF16)
        negf = self.tmpf[0][0:128]
        trif = self.tmpf[1][0:128]
        self.memset(negf, -1.0)
        self.copy(mc["negones"], negf)
        self.pool_op(lambda e: e.affine_select(out=trif.ap, in_=negf.ap, pattern=[[-1, 128]], compare_op=ALU.is_ge, fill=0.0, base=0,
                                               channel_multiplier=1), [negf], [trif])
        self.copy(mc["tri"], trif)
        mc["maskb"] = []
        for i in range(4):
            mb = al([512], BF16)
            tmp = self.rstd[i % 2]
            self.pool_op((lambda tmp, i: lambda e: e.affine_select(out=tmp.ap, in_=self.zerosf.ap, pattern=[[1, 512]], compare_op=ALU.is_gt,
                                                                  fill=-30000.0, base=-128 * i, channel_multiplier=-1))(tmp, i),
                         [self.zerosf], [tmp])
            self.copy(mb, tmp)
            mc["maskb"].append(mb)
        vr = self.vrows
        self.dma(vr.p(0, 8), w["lam_re"][l].rearrange("(t g) p -> t (g p)", g=2), q="sp")
        self.dma(vr.p(8, 16), w["lam_im"][l].rearrange("(t g) p -> t (g p)", g=2), q="sp")
        self.dma(vr.p(16, 22), w["conv_w"][l].rearrange("k (c p) -> (k c) p", p=128), q="sp")
        self.dma(vr.p(22, 24), w["ssm_d"][l].rearrange("(c p) -> c p", p=128), q="sp")
        self.dma(vr.p(24, 26), w["pool_scale"][l].rearrange("(c p) -> c p", p=128), q="sp")
        ps = self.bank()
        self.tr(ps[:, 0:26], vr.p(0, 26), self.ident.p(0, 26)[0:26])
        mc["vecs2"] = al([32], F32)
        self.copy(mc["vecs2"][0:26], ps[:, 0:26])
        return mc

    def sincos(self, dsin, dcos, y, yi, ta, tb):
        self.copy(yi, y)
        self.copy(ta, yi)
        self.tt(tb, y, ta, ALU.subtract)
        self.act(dsin, tb, AF.Sin, scale=6.283185)
        self.ts(tb, y, 0.25, ALU.add)
        self.copy(yi, tb)
        self.copy(ta, yi)
        self.tt(tb, tb, ta, ALU.subtract)
        self.act(dcos, tb, AF.Sin, scale=6.283185)

    def ssm_branch(self, l, hT, ub, mc):
        w = self.w
        al = self.alloc
        MUL, ADD, SUB = ALU.mult, ALU.add, ALU.subtract
        m = self.mark()
        v2 = mc["vecs2"]
        lrT, liT = v2[0:8], v2[8:16]
        dcol = v2[22:24]
        sp = al([20 * 8], F32)
        c8 = [sp[i * 8:(i + 1) * 8] for i in range(20)]
        lr, dt, a_, th, mag, thn, ta8, tb8, sn, cs, abr, abi, den, nr, fre, fim, t8, nEi, inre, inim = c8
        yi8 = al([8], I32)
        NT = TC + 1
        lb = al([16, 128], BF16)
        lc = al([16, 128], BF16)
        sinT = al([8, NT], F32)
        cosT = al([8, NT], F32)
        ms_ = self.mark()
        rows = al([272], F32)
        E0 = rows[0:128].p(0, 1)
        E1 = rows[128:256].p(0, 1)
        ld = rows[256:272].p(0, 1)
        self.memset(rows.p(0, 1), 0.0)
        self.memset(rows[0:64].p(0, 1), 1.0)
        self.memset(rows[192:256].p(0, 1), 1.0)
        self.dma(ld, w["log_dt"][l].unsqueeze(0), q="sp")
        ps = self.bank()
        self.mm(ps[:, 0:8], E0, ld.with_ap(ld.ap[:, 0:16:2]), True, False)
        self.mm(ps[:, 0:8], E1, ld.with_ap(ld.ap[:, 1:16:2]), False, True)
        self.act(dt, ps[:, 0:8], AF.Exp)
        self.ts(lr, lrT, -1e-4, ALU.min)
        self.tt(a_, lr, dt, MUL)
        self.tt(th, liT, dt, MUL)
        self.act(mag, a_, AF.Exp)
        self.ts(thn, th, 1.0 / (2 * math.pi), MUL)
        self.sincos(sn, cs, thn, yi8, ta8, tb8)
        self.tt(abr, mag, cs, MUL)
        self.tt(abi, mag, sn, MUL)
        self.tt(den, lr, lr, MUL)
        self.tt(t8, liT, liT, MUL)
        self.tt(den, den, t8, ADD)
        self.recip(den, den)
        self.ts(nr, abr, -1.0, ADD)
        self.tt(fre, nr, lr, MUL)
        self.tt(t8, abi, liT, MUL)
        self.tt(fre, fre, t8, ADD)
        self.tt(fre, fre, den, MUL)
        self.tt(fim, abi, lr, MUL)
        self.tt(t8, nr, liT, MUL)
        self.tt(fim, fim, t8, SUB)
        self.tt(fim, fim, den, MUL)
        if int(os.environ.get("SSMSTOP", "9")) <= 1:
            self.memset(ub, 0.0)
            self.release(m)
            return
        pidx = al([1], I32)
        pj = al([1], I32)
        pf = al([1], F32)
        rm = al([4], F32)
        par = al([2], F32)
        self.pool_op(lambda e: e.iota(pidx.ap, pattern=[[0, 1]], base=0, channel_multiplier=1), [], [pidx])
        self.S.add("dve", lambda e: e.tensor_scalar(out=pj.ap, in0=pidx.ap, scalar1=5, scalar2=None, op0=ALU.logical_shift_right), reads=[pidx], writes=[pj])
        self.copy(pf, pj)
        for q in range(4):
            self.ts(rm[q:q + 1], pf, float(q), ALU.is_equal)
        self.S.add("dve", lambda e: e.tensor_scalar(out=pj.ap, in0=pidx.ap, scalar1=4, scalar2=1, op0=ALU.logical_shift_right, op1=ALU.bitwise_and),
                   reads=[pidx], writes=[pj])
        self.copy(par[1:2], pj)
        self.ts(par[0:1], par[1:2], -1.0, MUL, 1.0, ADD)
        Bre = al([8, 32], F32)
        Bim = al([8, 32], F32)
        bbr = al([8, 32], F32)
        bbi = al([8, 32], F32)
        tb1 = al([8, 32], F32)
        for Bt, nm in ((Bre, "ssm_b_re"), (Bim, "ssm_b_im")):
            self.memset(Bt, 0.0)
            src = w[nm][l].rearrange("(t g) p h -> g p t h", g=2)
            self.dma(Bt.with_ap(Bt.ap[0:64, :, 0:16]), src[0], q="sp")
            self.dma(Bt.with_ap(Bt.ap[64:128, :, 16:32]), src[1], q="sp")
        fr3 = self.bc3(fre, 8, 32)
        fi3 = self.bc3(fim, 8, 32)
        self.tt(bbr, Bre, fr3, MUL)
        self.tt(tb1, Bim, fi3, MUL)
        self.tt(bbr, bbr, tb1, SUB)
        self.tt(bbi, Bim, fr3, MUL)
        self.tt(tb1, Bre, fi3, MUL)
        self.tt(bbi, bbi, tb1, ADD)
        for ri, bb in enumerate((bbr, bbi)):
            for c in range(2):
                ps = self.bank()
                src = bb.with_ap(bb.ap[:, 4 * c:4 * c + 4, :].rearrange("p a b -> p (a b)"))
                self.tr(ps[:, 0:128], src, self.ident)
                for q in range(4):
                    self.ts(lb[ri * 8 + 4 * c + q], ps[:, 0:128], rm[q:q + 1], MUL)
        if int(os.environ.get("SSMSTOP", "9")) <= 2:
            self.memset(ub, 0.0)
            self.release(m)
            return
        self.memset(lc, 0.0)
        InC = al([128], F32)
        for ri, nm in enumerate(("ssm_c_re", "ssm_c_im")):
            for c in range(2):
                src = w[nm][l].rearrange("g h p -> (g h) p")[c * 128:(c + 1) * 128, :]
                self.dma(InC[0:64], src, q="sp")
                self.dma(InC[64:128], src, q="sp")
                self.ts(InC[0:64], InC[0:64], par[0:1], MUL)
                self.ts(InC[64:128], InC[64:128], par[1:2], MUL)
                ps = self.bank()
                self.tr(ps[:, 0:128], InC, self.ident)
                for q in range(4):
                    dst = lc[ri * 8 + 4 * c + q][32 * q:32 * q + 32]
                    if ri == 0:
                        self.copy(dst, ps[:, 32 * q:32 * q + 32])
                    else:
                        self.ts(dst, ps[:, 32 * q:32 * q + 32], -1.0, MUL)
        if int(os.environ.get("SSMSTOP", "9")) <= 3:
            self.memset(ub, 0.0)
            self.release(m)
            return
        iot_i = al([NT], I32)
        iot_f = al([NT], F32)
        self.pool_op(lambda e: e.iota(iot_i.ap, pattern=[[1, NT]], base=0, channel_multiplier=0), [], [iot_i])
        self.copy(iot_f, iot_i)
        mt = self.mark()
        yt = al([8, NT], F32)
        yi = al([8, NT], I32)
        ta = al([8, NT], F32)
        tb = al([8, NT], F32)
        for T in range(8):
            self.ts(yt[T], iot_f, thn[T:T + 1], MUL)
        self.sincos(sinT, cosT, yt, yi, ta, tb)
        self.release(mt)
        for T in range(8):
            self.ts(nEi[T:T + 1], sinT[T, TC:TC + 1], -1.0, MUL)
        if int(os.environ.get("SSMSTOP", "9")) <= 4:
            self.memset(ub, 0.0)
            self.release(m)
            return
        self.release(ms_)
        uS = al([1024], F32)
        uSb = al([1024], BF16)
        sre = al([1024], F32)
        sim = al([1024], F32)
        sbre = al([1024], BF16)
        sbim = al([1024], BF16)
        sre2 = al([1024], F32)
        sim2 = al([1024], F32)
        tA = al([512], F32)
        tB = al([512], F32)
        yv = tA
        y2 = tB
        t1c = al([1], F32)
        wsl = al([8, 256], BF16)
        self.wload(w["w_in"][l], 8, 768, 256, wsl, 0)
        ybanks = [self.banks[6], self.banks[7]]
        rot = [0, 1, 2, 3, 4, 5]
        nb = 512 // TC
        NCH = 1024 // TC
        for c in range(2):
            for hf in range(2):
                t0 = hf * 1024
                for tt in range(2):
                    lt = slice(tt * 512, (tt + 1) * 512)
                    ps = self.bank(rot)
                    for k in range(8):
                        self.mm(ps, wsl[k, c * 128:(c + 1) * 128], hT[k, t0 + tt * 512:t0 + (tt + 1) * 512], k == 0, k == 7)
                    self.act(uS[lt], ps, AF.Identity)
                    self.copy(uSb[lt], ps)
                for q in range(4):
                    T = 4 * c + q
                    cb = self.tb3(cosT[T, 0:TC], nb, TC)
                    sb_ = self.tb3(sinT[T, 0:TC], nb, TC)
                    Er = cosT[T, TC:TC + 1]
                    Ei = sinT[T, TC:TC + 1]
                    magb = mag[T:T + 1].with_ap(mag[T:T + 1].ap.to_broadcast([128, TC]))
                    tA3, tB3 = self.v3(tA, TC), self.v3(tB, TC)
                    for tt in range(2):
                        lt = slice(tt * 512, (tt + 1) * 512)
                        pr = self.bank(rot)
                        pi = self.bank(rot)
                        self.mm(pr, lb[T], uSb[lt], True, True)
                        self.mm(pi, lb[8 + T], uSb[lt], True, True)
                        self.act(sre[lt], pr, AF.Identity)
                        self.act(sim[lt], pi, AF.Identity)
                        r3, i3 = self.v3(sre[lt], TC), self.v3(sim[lt], TC)
                        self.tt(tA3, r3, cb, MUL)
                        self.tt(tB3, i3, sb_, MUL)
                        self.tt(sre2[lt], tA, tB, ADD)
                        self.tt(tA3, i3, cb, MUL)
                        self.tt(tB3, r3, sb_, MUL)
                        self.tt(sim2[lt], tA, tB, SUB)
                    for ch in range(NCH):
                        gch = hf * NCH + ch
                        seg = slice(ch * TC, (ch + 1) * TC)
                        self.scan(sre[seg], magb, sre2[seg], 0.0 if gch == 0 else inre[T:T + 1])
                        self.scan(sim[seg], magb, sim2[seg], 0.0 if gch == 0 else inim[T:T + 1])
                        if gch < 2 * NCH - 1:
                            lre = sre[(ch + 1) * TC - 1:(ch + 1) * TC]
                            lim = sim[(ch + 1) * TC - 1:(ch + 1) * TC]
                            self.ts(t1c, lre, Er, MUL)
                            self.ts(inre[T:T + 1], lim, nEi[T:T + 1], MUL, t1c, ADD)
                            self.ts(t1c, lim, Er, MUL)
                            self.ts(inim[T:T + 1], lre, Ei, MUL, t1c, ADD)
                    for tt in range(2):
                        lt = slice(tt * 512, (tt + 1) * 512)
                        sr3, si3 = self.v3(sre[lt], TC), self.v3(sim[lt], TC)
                        self.tt(tA3, sr3, cb, MUL)
                        self.tt(tB3, si3, sb_, MUL)
                        self.tt(sbre[lt], tA, tB, SUB)
                        self.tt(tA3, si3, cb, MUL)
                        self.tt(tB3, sr3, sb_, MUL)
                        self.tt(sbim[lt], tA, tB, ADD)
                        self.mm(ybanks[tt], lc[T], sbre[lt], q == 0, False)
                        self.mm(ybanks[tt], lc[8 + T], sbim[lt], False, q == 3)
                for tt in range(2):
                    lt = slice(tt * 512, (tt + 1) * 512)
                    self.ts(yv, uS[lt], dcol[c:c + 1], MUL)
                    self.tt(yv, yv, ybanks[tt], ADD)
                    self.tt(y2, yv, yv, MUL)
                    self.ts(y2, y2, 1.5957691216 * 0.044715, MUL, 1.5957691216, ADD)
                    self.tt(y2, y2, yv, MUL)
                    self.act(y2, y2, AF.Sigmoid)
                    self.tt(ub[c, t0 + tt * 512:t0 + (tt + 1) * 512], yv, y2, MUL)
        self.release(m)

    def conv_branch(self, l, hT, ua, mc):
        al = self.alloc
        win = self.w["w_in"][l]
        MUL, ADD = ALU.mult, ALU.add
        m = self.mark()
        v2 = mc["vecs2"]
        upad = al([2 + L], F32)
        csb = al([512], F32)
        yt = al([512], F32)
        s1 = self.wload(win, 8, 0, 512)
        s2 = self.wload(win, 8, 512, 256)
        for i in range(2):
            self.memset(upad[0:2], 0.0)
            for tt in range(4):
                tok = slice(tt * 512, (tt + 1) * 512)
                pc = self.bank()
                pv = self.bank()
                pb = self.bank()
                for k in range(8):
                    self.mm(pc, s1[k, 256 + i * 128:256 + (i + 1) * 128], hT[k, tok], k == 0, k == 7)
                for k in range(8):
                    self.mm(pv, s2[k, i * 128:(i + 1) * 128], hT[k, tok], k == 0, k == 7)
                for k in range(8):
                    self.mm(pb, s1[k, i * 128:(i + 1) * 128], hT[k, tok], k == 0, k == 7)
                self.act(csb, pc, AF.Identity)
                b0 = tt * 512
                self.tt(upad[2 + b0:2 + b0 + 512], csb, pv, MUL)
                self.ts(yt, upad[b0:b0 + 512], v2[16 + i:17 + i], MUL)
                self.stt(yt, upad[1 + b0:1 + b0 + 512], v2[18 + i:19 + i], yt, MUL, ADD)
                self.stt(yt, upad[2 + b0:2 + b0 + 512], v2[20 + i:21 + i], yt, MUL, ADD)
                self.tt(ua[i, tok], yt, pb, MUL)
        self.release(m)

    def pool_branch(self, l, hT, uc, mc):
        al = self.alloc
        w = self.w
        win = w["w_in"][l]
        MUL, ADD, SUB = ALU.mult, ALU.add, ALU.subtract
        m = self.mark()
        v2 = mc["vecs2"]
        uP = al([L], F32)
        csp = al([16 + L], F32)
        dP = al([L], F32)
        pl = al([L], BF16)
        lp = al([2, 128], BF16)
        invc = al([2, 16], F32)
        t16 = al([16], F32)
        io_i = al([16], I32)
        io_f = al([16], F32)
        self.pool_op(lambda e: e.iota(io_i.ap, pattern=[[1, 16]], base=1, channel_multiplier=0), [], [io_i])
        self.copy(io_f, io_i)
        for c in range(2):
            for h2 in range(2):
                self.ts(invc[c].p(64 * h2, 64 * h2 + 64), io_f.p(64 * h2, 64 * h2 + 64), float(POOL_W[2 * c + h2]), ALU.min)
        self.recip(invc, invc)
        self.memset(lp, 0.0)
        for c in range(2):
            for h2 in range(2):
                self.dma(lp[c].p(64 * h2, 64 * h2 + 64)[64 * h2:64 * h2 + 64], w["w_pool"][l, 2 * c + h2], q="pool")
        sl = self.wload(win, 8, 1024, 256)
        onesb = self.onesf[0:1].with_ap(self.onesf[0:1].ap.to_broadcast([128, 512]))
        self.memset(csp[0:16], 0.0)
        for c in range(2):
            for tt in range(4):
                tok = slice(tt * 512, (tt + 1) * 512)
                ps = self.bank()
                for k in range(8):
                    self.mm(ps, sl[k, c * 128:(c + 1) * 128], hT[k, tok], k == 0, k == 7)
                self.act(uP[tok], ps, AF.Identity)
                self.scan(csp[16 + tt * 512:16 + (tt + 1) * 512], onesb, uP[tok], csp[15 + tt * 512:16 + tt * 512])
            for h2 in range(2):
                w_ = POOL_W[2 * c + h2]
                P = (64 * h2, 64 * h2 + 64)
                self.tt(dP.p(*P), csp[16:16 + L].p(*P), csp[16 - w_:16 - w_ + L].p(*P), SUB)
                self.stt(pl.p(*P), dP.p(*P), 1.0 / w_, uP.p(*P), MUL, SUB)
            self.tt(t16, dP[0:16], invc[c], MUL)
            self.tt(pl[0:16], t16, uP[0:16], SUB)
            for tt in range(4):
                tok = slice(tt * 512, (tt + 1) * 512)
                ps = self.bank()
                self.mm(ps, lp[c], pl[tok], True, True)
                self.act(uc[c, tok], ps, AF.Identity, scale=v2[24 + c:25 + c])
        self.release(m)

    def attn_branch(self, l, hT, ud, mc):
        al = self.alloc
        win = self.w["w_in"][l]
        m = self.mark()
        qT = al([L], BF16)
        kT = al([L], BF16)
        vS = al([16, 128], BF16)
        e_ = al([512], F32)
        spb = [al([512], BF16) for _ in range(2)]
        At = [al([512], F32) for _ in range(2)]
        Gts = [al([512], F32) for _ in range(2)]
        Ab = [al([512], BF16) for _ in range(2)]
        rot = [0, 1, 2, 3]
        ui = 0
        for c in range(2):
            sq = self.wload(win, 8, 1280 + c * 128, 128)
            sk = self.wload(win, 8, 1536 + c * 128, 128)
            sv = self.wload(win, 8, 1792 + c * 128, 128)
            for tt in range(4):
                tok = slice(tt * 512, (tt + 1) * 512)
                ps = self.bank(rot)
                for k in range(8):
                    self.mm(ps, sq[k, 0:128], hT[k, tok], k == 0, k == 7)
                self.act(qT[tok], ps, AF.Identity, scale=0.125)
                ps = self.bank(rot)
                for k in range(8):
                    self.mm(ps, sk[k, 0:128], hT[k, tok], k == 0, k == 7)
                self.copy(kT[tok], ps)
            for tb in range(16):
                ps = self.bank(rot)
                for k in range(8):
                    self.mm(ps[:, 0:128], hT[k, tb * 128:(tb + 1) * 128], sv[k, 0:128], k == 0, k == 7)
                if tb % 2 == 0:
                    self.copy(vS[tb], ps[:, 0:128])
                else:
                    self.act(vS[tb], ps[:, 0:128], AF.Identity)
            units = []
            for hh in range(2):
                P = (64 * hh, 64 * hh + 64)
                for qt in range(4):
                    chain = hh * 4 + qt
                    Ob, Rb = (self.banks[4], self.banks[5]) if chain % 2 == 0 else (self.banks[6], self.banks[7])
                    ObP = PV(Ob.ap[P[0]:P[1], :], Ob.bank)
                    nkb = 4 * (qt + 1)
                    qv = qT[qt * 512:(qt + 1) * 512].p(*P)
                    for kb in range(nkb - 1, -1, -1):
                        units.append(dict(hh=hh, qt=qt, kb=kb, first=(kb == nkb - 1), last=(kb == 0), Rb=Rb, ObP=ObP, qv=qv, P=P,
                                          diag=(kb >= 4 * qt)))

            def stageA(u, i):
                zb = self.bank(rot)
                u["zb"] = zb
                sp_ = spb[i % 2]
                G = Gts[i % 2]
                u["G"] = G
                kb, P = u["kb"], u["P"]
                self.mm(zb, kT[kb * 128:(kb + 1) * 128].p(*P), u["qv"], True, not u["diag"])
                if u["diag"]:
                    self.mm(zb, mc["ident_bf"], mc["maskb"][kb - 4 * u["qt"]], False, True)
                self.act(e_, zb, AF.Exp)
                self.act(sp_, e_, AF.Ln, bias=self.one1)
                if not u["first"]:
                    self.act(G, u["Rb"], AF.Exp)
                self.mm(zb, mc["tri"], sp_, False, True)
                if kb > 0:
                    self.mm(u["Rb"], mc["negones"], sp_, u["first"], True)

            def stageB(u, i):
                A = Ab[i % 2]
                at = At[i % 2]
                if u["first"]:
                    self.act(A, u["zb"], AF.Exp)
                else:
                    self.act(at, u["zb"], AF.Exp)
                    self.tt(A, at, u["G"], ALU.mult)
                hh, kb = u["hh"], u["kb"]
                self.mm(u["ObP"], vS[kb, 64 * hh:64 * hh + 64], A, u["first"], u["last"])
                if u["last"]:
                    qt = u["qt"]
                    self.act(ud[c, qt * 512:(qt + 1) * 512].p(*u["P"]), u["ObP"], AF.Identity)

            stageA(units[0], 0)
            for i in range(1, len(units)):
                stageA(units[i], i)
                stageB(units[i - 1], i - 1)
            stageB(units[-1], len(units) - 1)
        self.release(m)

    def merge_out(self, l, hT, us):
        al = self.alloc
        w = self.w
        win = w["w_in"][l]
        MUL, ADD = ALU.mult, ALU.add
        ynames = ["w_conv_out", None, "w_pool_out", "w_sb_out"]
        rot = [0, 1, 2, 3, 4, 5, 6]
        pss = self.banks[7]
        for hf in range(2):
            t0 = hf * 1024
            m = self.mark()
            mg = al([8, 1024], BF16)
            acc = al([1024], F32)
            sg = al([512], F32)
            sg2 = al([512], F32)
            yt = al([512], F32)
            pr = al([512], F32)
            for n in range(8):
                for b in range(4):
                    sgw = self.wload(win, 8, 2048 + b * 1024 + n * 128, 128)
                    if b == 1:
                        sy = self.wslot(2, 256)
                        self.wload(w["w_glu"][l], 2, n * 128, 128, sy, 0)
                        self.wload(w["w_glu"][l], 2, 1024 + n * 128, 128, sy, 128)
                    else:
                        sy = self.wload(w[ynames[b]][l], 2, n * 128, 128)
                    u = us[b]
                    for tt in range(2):
                        tok = slice(t0 + tt * 512, t0 + (tt + 1) * 512)
                        lt = slice(tt * 512, (tt + 1) * 512)
                        pg = self.bank(rot)
                        for k in range(8):
                            self.mm(pg, sgw[k, 0:128], hT[k, tok], k == 0, k == 7)
                        py = self.bank(rot)
                        for k in range(2):
                            self.mm(py, sy[k, 0:128], u[k, tok], k == 0, k == 1)
                        self.act(sg, pg, AF.Sigmoid)
                        dst = acc[lt] if b == 0 else pr
                        if b == 1:
                            pyg = self.bank(rot)
                            for k in range(2):
                                self.mm(pyg, sy[k, 128:256], u[k, tok], k == 0, k == 1)
                            self.act(sg2, pyg, AF.Sigmoid)
                            self.tt(yt, sg2, py, MUL)
                            self.tt(dst, sg, yt, MUL)
                        else:
                            self.tt(dst, sg, py, MUL)
                        if b in (1, 2):
                            self.tt(acc[lt], acc[lt], pr, ADD)
                        elif b == 3:
                            self.tt(mg[n, lt], acc[lt], pr, ADD)
            so = [self.wload(w["w_out"][l], 8, 0, 512), self.wload(w["w_out"][l], 8, 512, 512)]
            for tt in range(2):
                lt = slice(tt * 512, (tt + 1) * 512)
                ts_ = t0 + tt * 512
                rstd = self.rstd[tt % 2]
                for n in range(8):
                    ps = self.bank(rot)
                    for k in range(8):
                        self.mm(ps, so[n // 4][k, (n % 4) * 128:(n % 4 + 1) * 128], mg[k, lt], k == 0, k == 7)
                    sq = self.sqb[n % 2]
                    self.act(sq, ps, AF.Square)
                    self.mm(pss, self.ones_bf, sq, n == 0, n == 7)
                self.act(rstd, pss, AF.Sqrt, scale=1.0 / D, bias=self.epsb)
                self.recip(rstd, rstd)
                for n in range(8):
                    ps = self.bank(rot)
                    for k in range(8):
                        self.mm(ps, so[n // 4][k, (n % 4) * 128:(n % 4 + 1) * 128], mg[k, lt], k == 0, k == 7)
                    tmp = self.tmpf[n % 2]
                    self.tt(tmp, ps, rstd, MUL)
                    xs = self.xT[n, ts_:ts_ + 512]
                    self.stt(xs, tmp, self.coefG[1][n:n + 1], xs, MUL, ADD)
            self.release(m)

    def dump_u(self, i, u):
        if self.dbg:
            dst = self.dbg_ap[i]
            self.S.add("pool", lambda e: e.dma_start(out=dst, in_=u.ap), reads=[u], dma=True, is_out=True)

    def mixer(self, l):
        al = self.alloc
        m0 = self.mark()
        hT = al([8, L], BF16)
        self.pre_norm(1, hT, 0, 4)
        mc = self.mixer_consts(l)
        br = os.environ.get("BR", "scpam")
        ub = al([2, L], BF16)
        if "s" in br:
            self.ssm_branch(l, hT, ub, mc)
        else:
            self.memset(ub, 0.0)
        ua = al([2, L], BF16)
        if "c" in br:
            self.conv_branch(l, hT, ua, mc)
        else:
            self.memset(ua, 0.0)
        uc = al([2, L], BF16)
        if "p" in br:
            self.pool_branch(l, hT, uc, mc)
        else:
            self.memset(uc, 0.0)
        ud = al([2, L], BF16)
        if "a" in br:
            self.attn_branch(l, hT, ud, mc)
        else:
            self.memset(ud, 0.0)
        for i, u in enumerate((ua, ub, uc, ud)):
            self.dump_u(i, u)
        if "m" in br:
            self.merge_out(l, hT, (ua, ub, uc, ud))
        self.release(m0)


    def build(self):
        nc = self.nc
        w = {}
        shapes = dict(x=[L, D], c=[D], w_ada=[DEPTH, D, 9 * D], b_ada=[DEPTH, 9 * D], g_pre=[DEPTH, 3, D], g_post=[DEPTH, 3, D],
                      w_ff_in=[DEPTH, 2, D, 2 * DFF], w_ff_out=[DEPTH, 2, DFF, D], w_in=[DEPTH, D, 6144], conv_w=[DEPTH, 3, 256],
                      w_conv_out=[DEPTH, 256, D], lam_re=[DEPTH, 16, 64], lam_im=[DEPTH, 16, 64], log_dt=[DEPTH, 16],
                      ssm_b_re=[DEPTH, 16, 64, 16], ssm_b_im=[DEPTH, 16, 64, 16], ssm_c_re=[DEPTH, 16, 16, 64], ssm_c_im=[DEPTH, 16, 16, 64],
                      ssm_d=[DEPTH, 256], w_glu=[DEPTH, 256, 2 * D], w_pool=[DEPTH, 4, 64, 64], pool_scale=[DEPTH, 256],
                      w_pool_out=[DEPTH, 256, D], w_sb_out=[DEPTH, 256, D], w_out=[DEPTH, D, D])
        import os
        for k, s in shapes.items():
            if os.environ.get("ONLYX") and k not in ("x", "c"):
                continue
            w[k] = nc.dram_tensor(k, s, F32, kind="ExternalInput").ap()
        self.w = w
        self.out_ap = nc.dram_tensor("out", [L, D], F32, kind="ExternalOutput").ap()
        if self.dbg:
            self.dbg_ap = nc.dram_tensor("dbg_u", [4, 128, 2, L], F32, kind="ExternalOutput").ap()
        from contextlib import ExitStack
        with ExitStack() as es:
            arena = es.enter_context(nc.sbuf_tensor("arena", [128, ARENA_BYTES // 4], F32))
            self.arena = arena[:]
            self.banks = []
            for b in range(8):
                pt = es.enter_context(nc.psum_tensor(f"ps{b}", [128, 512], F32))
                self.banks.append(PV(pt[:], b))
            self.rot = list(range(8))
            sems = {e: [es.enter_context(nc.semaphore(f"s_{e}_{i}")) for i in range(4)] for e in ("pe", "act", "dve", "pool")}
            sems["sp"] = sems["pool"]
            dsems = {q: [es.enter_context(nc.semaphore(f"d_{q}_{i}")) for i in range(Sched.KDMA)] for q in ("pool", "sp")}
            m0 = self.mark()
            self.xrow = None
            self.xT_only = True
            self.setup_alloc_xrow = True
            self._setup_with_xrow()
            done = 0
            for l in range(DEPTH):
                if done >= self.n_sub:
                    break
                self.layer_params(l)
                self.ffn(l, 0, 0)
                done += 1
                if done >= self.n_sub:
                    break
                self.mixer(l)
                done += 1
                if done >= self.n_sub:
                    break
                self.ffn(l, 2, 1)
                done += 1
            self.store_out()
            block = es.enter_context(nc.Block())
            S = self.S

            @block.tensor
            def _(e):
                S._cur = e
                S.emit_one(nc, "pe", e, sems, dsems)

            @block.scalar
            def _(e):
                S.emit_one(nc, "act", e, sems, dsems)

            @block.vector
            def _(e):
                S.emit_one(nc, "dve", e, sems, dsems)

            @block.gpsimd
            def _(e):
                S.emit_one(nc, "pool", e, sems, dsems)

            @block.sync
            def _(e):
                S.emit_one(nc, "sp", e, sems, dsems)
        return nc

    def _setup_with_xrow(self):
        orig_alloc = self.alloc
        self.xrow = None
        self.setup()


def _emit_one(self, nc, e, eh, sems, dsems):
    if not getattr(self, "_prepared", False):
        for op in self.ops:
            for d in op.deps:
                self.ops[d].marked = True
        for en in self.ENGS:
            cnt = 0
            for op in self.by_eng[en]:
                if op.dma:
                    op.sem = dsems[en][op.qidx % self.KDMA]
                    op.val = 16 * (op.qidx // self.KDMA + 1)
                elif op.marked:
                    ep = cnt // self.EPOCH
                    op.sem = sems[en][ep] if en != "sp" else None
                    op.val = cnt % self.EPOCH + 1
                    cnt += 1
        self._prepared = True
    waited = {}

    def wait(sem, val):
        key = id(sem)
        if waited.get(key, 0) < val:
            eh.wait_ge(sem, val)
            waited[key] = val

    for op in self.by_eng[e]:
        for d in sorted(op.deps):
            o = self.ops[d]
            wait(o.sem, o.val)
        if op.dma and op.qidx >= self.KDMA:
            wait(op.sem, op.val - 16)
        ins = op.fn(eh)
        if op.dma:
            ins.then_inc(op.sem, 16)
        elif op.marked:
            ins.then_inc(op.sem, 1)
    if e == "sp":
        for o in self.out_dmas:
            wait(o.sem, o.val)


Sched.emit_one = _emit_one

_CACHE = {}


def build_nc(n_sub=12, dbg=False):
    nc = bass.Bass("TRN2", target_bir_lowering=False)
    b = Builder(nc, n_sub, dbg)
    b.build()
    return nc


def kernel(n_sub=12, **inputs):
    x = np.ascontiguousarray(inputs["x"], dtype=np.float32)
    c = np.ascontiguousarray(inputs["c"], dtype=np.float32)
    nc = build_nc(n_sub)
    shared = {k: np.ascontiguousarray(v, dtype=np.float32) for k, v in inputs.items() if k not in ("x", "c")}
    in_maps = []
    for b in range(8):
        m = dict(shared)
        m["x"] = x[b]
        m["c"] = c[b]
        in_maps.append(m)
    res = run_bass_kernel_spmd(nc, in_maps, core_ids=list(range(8)))
    out = np.stack([res.results[b]["out"] for b in range(8)], axis=0)
    return out.astype(np.float32)
```

```python
import math
import os
import numpy as np
import concourse.bass as bass
import concourse.mybir as mybir
from concourse.bass_utils import run_bass_kernel_spmd

F32 = mybir.dt.float32
BF16 = mybir.dt.bfloat16
I32 = mybir.dt.int32
AF = mybir.ActivationFunctionType
ALU = mybir.AluOpType

D = 1024
L = 2048
DEPTH = 4
DFF = 2816
NJ = DFF // 128
EPS = 1e-6
G = 256
ARENA_BYTES = 204 * 1024
POOL_W = (2, 4, 8, 16)
TC = 128


def dsz(dt):
    return 2 if dt == BF16 else 4


class V:
    def __init__(self, ap, lo, hi, shape, es, off):
        self.ap, self.lo, self.hi, self.shape, self.es, self.off = ap, lo, hi, shape, es, off

    def keys(self):
        return [("s", i) for i in range(self.lo // G, (self.hi - 1) // G + 1)]

    def __getitem__(self, key):
        if not isinstance(key, tuple):
            key = (key,)
        sh = self.shape
        if len(sh) == 1:
            (k,) = key
            a, b = (k.start or 0, sh[0] if k.stop is None else k.stop) if isinstance(k, slice) else (k, k + 1)
            ap = self.ap[:, a:b]
            return V(ap, self.off + a * self.es, self.off + b * self.es, (b - a,), self.es, self.off + a * self.es)
        k0 = key[0]
        k1 = key[1] if len(key) > 1 else slice(None)
        if isinstance(k0, slice):
            i0, i1 = k0.start or 0, sh[0] if k0.stop is None else k0.stop
        else:
            i0, i1 = k0, k0 + 1
        c0, c1 = k1.start or 0, sh[1] if k1.stop is None else k1.stop
        lo = self.off + (i0 * sh[1] + c0) * self.es
        hi = self.off + ((i1 - 1) * sh[1] + c1) * self.es
        if isinstance(k0, slice):
            ap = self.ap[:, i0:i1, c0:c1]
            return V(ap, lo, hi, (i1 - i0, c1 - c0), self.es, lo) if (c0 == 0 and c1 == sh[1]) else V(ap, lo, hi, None, self.es, lo)
        ap = self.ap[:, i0, c0:c1]
        return V(ap, lo, hi, (c1 - c0,), self.es, lo)

    def p(self, p0, p1):
        key = (slice(p0, p1),) + (slice(None),) * (len(self.ap.shape) - 1)
        return V(self.ap[key], self.lo, self.hi, self.shape, self.es, self.off)

    def with_ap(self, ap):
        return V(ap, self.lo, self.hi, None, self.es, self.off)


class PV:
    def __init__(self, ap, bank):
        self.ap, self.bank = ap, bank

    def keys(self):
        return [("p", self.bank)]

    def __getitem__(self, key):
        return PV(self.ap[key], self.bank)


class Op:
    __slots__ = ("id", "eng", "fn", "deps", "dma", "marked", "sem", "val", "qidx")


class Sched:
    ENGS = ("pe", "act", "dve", "pool", "sp")
    EPOCH = 20000
    KDMA = 8

    def __init__(self):
        self.ops = []
        self.W = {}
        self.R = {}
        self.by_eng = {e: [] for e in self.ENGS}
        self.ndma = {"pool": 0, "sp": 0}
        self.out_dmas = []

    def add(self, eng, fn, reads=(), writes=(), dma=False, is_out=False):
        op = Op()
        op.id = len(self.ops)
        op.eng, op.fn, op.dma, op.marked = eng, fn, dma, False
        op.sem = op.val = None
        deps = set()
        W, R = self.W, self.R
        for r in reads:
            for k in r.keys():
                w = W.get(k)
                if w is not None:
                    o = self.ops[w]
                    if o.dma or dma or o.eng != eng or eng != "pe":
                        deps.add(w)
        for wv in writes:
            for k in wv.keys():
                w = W.get(k)
                if w is not None:
                    o = self.ops[w]
                    if o.dma or dma or o.eng != eng or eng != "pe":
                        deps.add(w)
                rd = R.get(k)
                if rd:
                    for rk, rid in rd.items():
                        o = self.ops[rid]
                        if o.dma or dma or o.eng != eng or eng != "pe":
                            deps.add(rid)
                W[k] = op.id
                R[k] = {}
        for r in reads:
            for k in r.keys():
                R.setdefault(k, {})[("d", op.id) if dma else eng] = op.id
        deps.discard(op.id)
        if dma:
            q = self.ndma[eng]
            self.ndma[eng] = q + 1
            op.qidx = q
        op.deps = deps
        self.ops.append(op)
        self.by_eng[eng].append(op)
        if is_out:
            self.out_dmas.append(op)
        return op

    def emit(self, nc, block_engines, sems, dsems):
        for op in self.ops:
            for d in op.deps:
                self.ops[d].marked = True
        for e in self.ENGS:
            cnt = 0
            for op in self.by_eng[e]:
                if op.dma:
                    op.sem = dsems[e][op.qidx % self.KDMA]
                    op.val = 16 * (op.qidx // self.KDMA + 1)
                elif op.marked:
                    ep = cnt // self.EPOCH
                    op.sem = sems[e][ep]
                    op.val = cnt % self.EPOCH + 1
                    cnt += 1
        for e in self.ENGS:
            eh = block_engines[e]
            waited = {}

            def wait(sem, val):
                key = id(sem)
                if waited.get(key, 0) < val:
                    eh.wait_ge(sem, val)
                    waited[key] = val

            for op in self.by_eng[e]:
                for d in sorted(op.deps):
                    o = self.ops[d]
                    wait(o.sem, o.val)
                if op.dma and op.qidx >= self.KDMA:
                    wait(op.sem, op.val - 16)
                ins = op.fn(eh)
                if op.dma:
                    ins.then_inc(op.sem, 16)
                elif op.marked:
                    ins.then_inc(op.sem, 1)
            if e == "sp":
                for o in self.out_dmas:
                    wait(o.sem, o.val)


class Builder:
    def __init__(self, nc, n_sub, dbg):
        self.nc = nc
        self.S = Sched()
        self.n_sub = n_sub
        self.dbg = dbg
        self.off = 0
        self.bank_rr = 0

    def alloc(self, shape, dt):
        es = dsz(dt)
        n = int(np.prod(shape))
        nbytes = (n * es + 255) // 256 * 256
        off = self.off
        self.off += nbytes
        assert self.off <= ARENA_BYTES, f"arena overflow {self.off}"
        w0 = off // 4
        ap = self.arena[:, w0:w0 + nbytes // 4]
        if dt != F32:
            ap = ap.bitcast(dt)
        ap = ap[:, 0:n]
        if len(shape) == 2:
            ap = ap.rearrange("p (a b) -> p a b", a=shape[0])
        return V(ap, off, off + n * es, tuple(shape), es, off)

    def mark(self):
        return self.off

    def release(self, m):
        self.off = m

    def bank(self, pool=None):
        pool = pool or self.rot
        b = pool[self.bank_rr % len(pool)]
        self.bank_rr += 1
        return self.banks[b]

    def mm(self, out, lhsT, rhs, start, stop):
        self.S.add("pe", lambda e: e.matmul(out.ap, lhsT=lhsT.ap, rhs=rhs.ap, start=start, stop=stop, skip_group_check=True),
                   reads=[lhsT, rhs], writes=[out])

    def tr(self, out, in_, ident):
        self.S.add("pe", lambda e: e.transpose(out.ap, in_.ap, ident.ap), reads=[in_, ident], writes=[out])

    def act(self, out, in_, func, scale=1.0, bias=None, extra_reads=()):
        kw = {}
        rd = [in_] + list(extra_reads)
        if isinstance(scale, V):
            rd.append(scale)
            sc = scale.ap
        else:
            sc = scale
        if bias is not None:
            rd.append(bias)
            kw["bias"] = bias.ap
        self.S.add("act", lambda e: e.activation(out=out.ap, in_=in_.ap, func=func, scale=sc, **kw), reads=rd, writes=[out])

    def tt(self, out, a, b, op, eng="dve"):
        self.S.add(eng, lambda e: e.tensor_tensor(out=out.ap, in0=a.ap, in1=b.ap, op=op), reads=[a, b], writes=[out])

    def ts(self, out, a, s1, op0, s2=None, op1=None, eng="dve"):
        rd = [a]
        s1a = s1.ap if isinstance(s1, V) else s1
        s2a = s2.ap if isinstance(s2, V) else s2
        if isinstance(s1, V):
            rd.append(s1)
        if isinstance(s2, V):
            rd.append(s2)
        if op1 is None:
            self.S.add(eng, lambda e: e.tensor_scalar(out=out.ap, in0=a.ap, scalar1=s1a, scalar2=None, op0=op0), reads=rd, writes=[out])
        else:
            self.S.add(eng, lambda e: e.tensor_scalar(out=out.ap, in0=a.ap, scalar1=s1a, scalar2=s2a, op0=op0, op1=op1), reads=rd, writes=[out])

    def stt(self, out, a, s, b, op0, op1):
        rd = [a, b]
        sa = s.ap if isinstance(s, V) else s
        if isinstance(s, V):
            rd.append(s)
        self.S.add("dve", lambda e: e.scalar_tensor_tensor(out=out.ap, in0=a.ap, scalar=sa, in1=b.ap, op0=op0, op1=op1), reads=rd, writes=[out])

    def copy(self, out, in_, eng="dve"):
        self.S.add(eng, lambda e: e.tensor_copy(out=out.ap, in_=in_.ap), reads=[in_], writes=[out])

    def scan(self, out, d0, d1, init, extra_reads=()):
        rd = [d0, d1] + list(extra_reads)
        ia = init.ap if isinstance(init, V) else init
        if isinstance(init, V):
            rd.append(init)
        self.S.add("dve", lambda e: e.tensor_tensor_scan(out=out.ap, data0=d0.ap, data1=d1.ap, initial=ia, op0=ALU.mult, op1=ALU.add), reads=rd, writes=[out])

    def recip(self, out, in_):
        self.S.add("dve", lambda e: e.reciprocal(out=out.ap, in_=in_.ap), reads=[in_], writes=[out])

    def memset(self, out, val, eng="dve"):
        self.S.add(eng, lambda e: e.memset(out.ap, val), writes=[out])

    def dma(self, out, in_ap, q="pool", reads=(), **kw):
        self.S.add(q, lambda e: e.dma_start(out=out.ap, in_=in_ap, **kw), reads=list(reads), writes=[out], dma=True)

    def dma_out(self, out_ap, in_):
        self.S.add("sp", lambda e: e.dma_start(out=out_ap, in_=in_.ap), reads=[in_], dma=True, is_out=True)

    def wslot(self, kc, ncols):
        s = self.ring[self.ring_i % len(self.ring)]
        self.ring_i += 1
        ap = s.ap[:, 0:kc * ncols].rearrange("p (k c) -> p k c", k=kc)
        return V(ap, s.lo, s.lo + kc * ncols * 2, (kc, ncols), 2, s.lo)

    def wload(self, w2d, kc, c0, ncols, slot=None, scol=0):
        if slot is None:
            slot = self.wslot(kc, ncols)
            dst = slot
        else:
            dst = slot[0:kc, scol:scol + ncols]
        src = w2d[:, c0:c0 + ncols].rearrange("(k p) c -> p k c", p=128)
        self.dma(dst, src, q="pool")
        return slot

    def sumsq_rstd(self, srcs, rstd):
        ps = self.bank()
        for k in range(8):
            sq = self.sqb[k % 2]
            self.act(sq, srcs[k], AF.Square)
            self.mm(ps, self.ones_bf, sq, k == 0, k == 7)
        self.act(rstd, ps, AF.Sqrt, scale=1.0 / D, bias=self.epsb)
        self.recip(rstd, rstd)

    def pre_norm(self, sub, hT, t0, ntt):
        for tt in range(ntt):
            ts_ = t0 + tt * 512
            rstd = self.rstd[tt % 2]
            self.sumsq_rstd([self.xT[k, ts_:ts_ + 512] for k in range(8)], rstd)
            for k in range(8):
                tmp = self.tmpf[k % 2]
                self.tt(tmp, self.xT[k, ts_:ts_ + 512], rstd, ALU.mult)
                self.act(hT[k, tt * 512:(tt + 1) * 512], tmp, AF.Identity, scale=self.coefA[sub][k:k + 1], bias=self.coefB[sub][k:k + 1])

    def post_norm_res(self, sub, fT, t0, ntt):
        for tt in range(ntt):
            ts_ = t0 + tt * 512
            rstd = self.rstd[tt % 2]
            self.sumsq_rstd([fT[k, tt * 512:(tt + 1) * 512] for k in range(8)], rstd)
            for k in range(8):
                tmp = self.tmpf[k % 2]
                self.tt(tmp, fT[k, tt * 512:(tt + 1) * 512], rstd, ALU.mult)
                xs = self.xT[k, ts_:ts_ + 512]
                self.stt(xs, tmp, self.coefG[sub][k:k + 1], xs, ALU.mult, ALU.add)

    def ffn(self, l, sub, fi):
        w_in = self.w["w_ff_in"][l, fi]
        w_out = self.w["w_ff_out"][l, fi]
        for hf in range(2):
            t0 = hf * 1024
            m = self.mark()
            hT = self.alloc([8, 1024], BF16)
            gT = self.alloc([NJ, 1024], BF16)
            fT = self.alloc([8, 1024], F32)
            self.pre_norm(sub, hT, t0, 2)
            for jg in range(NJ // 2):
                slot = self.wslot(8, 512)
                self.wload(w_in, 8, jg * 256, 256, slot, 0)
                self.wload(w_in, 8, DFF + jg * 256, 256, slot, 256)
                for jj in range(2):
                    j = jg * 2 + jj
                    for tt in range(2):
                        pa = self.bank()
                        pb = self.bank()
                        for k in range(8):
                            self.mm(pa, slot[k, jj * 128:(jj + 1) * 128], hT[k, tt * 512:(tt + 1) * 512], k == 0, k == 7)
                        for k in range(8):
                            self.mm(pb, slot[k, 256 + jj * 128:256 + (jj + 1) * 128], hT[k, tt * 512:(tt + 1) * 512], k == 0, k == 7)
                        sa = self.tmpf[(j * 2 + tt) % 2]
                        self.act(sa, pa, AF.Silu)
                        self.tt(gT[j, tt * 512:(tt + 1) * 512], sa, pb, ALU.mult)
            for n in range(8):
                slot = self.wload(w_out, NJ, n * 128, 128)
                for tt in range(2):
                    pf = self.bank()
                    for k in range(NJ):
                        self.mm(pf, slot[k, 0:128], gT[k, tt * 512:(tt + 1) * 512], k == 0, k == NJ - 1)
                    self.act(fT[n, tt * 512:(tt + 1) * 512], pf, AF.Identity)
            self.post_norm_res(sub, fT, t0, 2)
            self.release(m)

    def layer_params(self, l):
        w = self.w
        vr = self.vrows
        self.dma(vr.p(0, 72), w["b_ada"][l].rearrange("(j p) -> j p", p=128), q="sp")
        self.dma(vr.p(72, 96), w["g_pre"][l].rearrange("s (k p) -> (s k) p", p=128), q="sp")
        self.dma(vr.p(96, 120), w["g_post"][l].rearrange("s (k p) -> (s k) p", p=128), q="sp")
        ps = self.bank()
        self.tr(ps[:, 0:120], vr.p(0, 120), self.ident.p(0, 120)[0:120])
        self.copy(self.vecs[0:120], ps[:, 0:120])
        ada = self.ada
        for cg in range(18):
            slot = self.wload(w["w_ada"][l], 8, cg * 512, 512)
            for jj in range(4):
                j = cg * 4 + jj
                pa = self.bank()
                for k in range(8):
                    self.mm(pa[:, 0:2], slot[k, jj * 128:(jj + 1) * 128], self.cT2[k * 2:k * 2 + 2], k == 0, k == 7)
                self.tt(ada[j:j + 1], pa[:, 0:1], self.vecs[j:j + 1], ALU.add)
        for s in range(3):
            resw = 1.0 if s == 1 else 0.5
            self.stt(self.coefA[s], ada[s * 24 + 8:s * 24 + 16], 1.0, self.vecs[72 + s * 8:72 + s * 8 + 8], ALU.add, ALU.mult)
            self.copy(self.coefB[s], ada[s * 24:s * 24 + 8])
            self.ts(self.tmp8, ada[s * 24 + 16:s * 24 + 24], 1.0, ALU.add, resw, ALU.mult)
            self.tt(self.coefG[s], self.tmp8, self.vecs[96 + s * 8:96 + s * 8 + 8], ALU.mult)

    def setup(self):
        nc = self.nc
        al = self.alloc
        self.xT = al([8, L], F32)
        self.ident = al([128], F32)
        self.ones_bf = al([128], BF16)
        self.onesf = al([128], F32)
        self.zerosf = al([512], F32)
        self.epsb = al([1], F32)
        self.one1 = al([1], F32)
        self.cT = al([8], BF16)
        self.cT2 = al([16], BF16)
        self.cTf = al([8], F32)
        self.vrows = al([128], F32)
        self.vecs = al([128], F32)
        self.ada = al([72], F32)
        self.coefA = [al([8], F32) for _ in range(3)]
        self.coefB = [al([8], F32) for _ in range(3)]
        self.coefG = [al([8], F32) for _ in range(3)]
        self.tmp8 = al([8], F32)
        self.ring = [al([4096], BF16) for _ in range(3)]
        self.ring_i = 0
        self.sqb = [al([512], BF16) for _ in range(2)]
        self.rstd = [al([512], F32) for _ in range(2)]
        self.tmpf = [al([512], F32) for _ in range(2)]
        self.memset(self.onesf, 1.0, "pool")
        self.memset(self.zerosf, 0.0, "pool")
        self.memset(self.epsb, EPS, "pool")
        self.memset(self.one1, 1.0, "pool")
        self.memset(self.vrows, 0.0, "pool")
        self.S.add("pool", lambda e: e.affine_select(out=self.ident.ap, in_=self.onesf.ap, pattern=[[-1, 128]], compare_op=ALU.is_equal,
                                                     fill=0.0, base=0, channel_multiplier=1), reads=[self.onesf], writes=[self.ident])
        self.copy(self.ones_bf, self.onesf)
        import os
        crow = self.vrows
        if not os.environ.get("SKIPC"):
            self.dma(crow.p(0, 8), self.w["c"].rearrange("(k p) -> k p", p=128), q="sp")
            ps = self.bank()
            self.tr(ps[:, 0:8], crow.p(0, 8), self.ident.p(0, 8)[0:8])
            self.act(self.cTf, ps[:, 0:8], AF.Silu)
            self.copy(self.cT, self.cTf)
            self.copy(self.cT2.with_ap(self.cT2.ap.rearrange("p (k t) -> p k t", t=2)), self.cTf.with_ap(self.cTf.ap.unsqueeze(2).to_broadcast([128, 8, 2])))
        mk = self.mark()
        self.xrow = [self.alloc([1024], F32) for _ in range(2)]
        xin = self.w["x"]
        for tb in range(16):
            xr = self.xrow[tb % 2]
            self.dma(xr, xin[tb * 128:(tb + 1) * 128, :], q="sp")
            for k in range(8):
                ps = self.bank()
                self.tr(ps[:, 0:128], xr[k * 128:(k + 1) * 128], self.ident)
                if k % 2 == 0:
                    self.copy(self.xT[k, tb * 128:(tb + 1) * 128], ps[:, 0:128])
                else:
                    self.act(self.xT[k, tb * 128:(tb + 1) * 128], ps[:, 0:128], AF.Identity)
        self.release(mk)

    def store_out(self):
        out = self.out_ap
        self.xrow = [self.alloc([1024], F32) for _ in range(2)]
        for tb in range(16):
            xr = self.xrow[tb % 2]
            for k in range(8):
                ps = self.bank()
                self.tr(ps[:, 0:128], self.xT[k, tb * 128:(tb + 1) * 128], self.ident)
                if k % 2 == 0:
                    self.copy(xr[k * 128:(k + 1) * 128], ps[:, 0:128])
                else:
                    self.act(xr[k * 128:(k + 1) * 128], ps[:, 0:128], AF.Identity)
            self.dma_out(out[tb * 128:(tb + 1) * 128, :], xr)

    def bc3(self, v, n, m):
        return v.with_ap(v.ap.unsqueeze(2).to_broadcast([128, n, m]))

    def v3(self, v, b):
        if isinstance(v, PV):
            return PV(v.ap.rearrange("p (a b) -> p a b", b=b), v.bank)
        return v.with_ap(v.ap.rearrange("p (a b) -> p a b", b=b))

    def tb3(self, v, n, b):
        return v.with_ap(v.ap.unsqueeze(1).to_broadcast([128, n, b]))

    def pool_op(self, fn, reads, writes):
        self.S.add("pool", fn, reads=reads, writes=writes)

    def mixer_consts(self, l):
        al = self.alloc
        w = self.w
        mc = {}
        mc["ident_bf"] = al([128], BF16)
        self.copy(mc["ident_bf"], self.ident)
        mc["tri"] = al([128], BF16)
        mc["negones"] = al([128], BF16)
        negf = self.tmpf[0][0:128]
        trif = self.tmpf[1][0:128]
        self.memset(negf, -1.0)
        self.copy(mc["negones"], negf)
        self.pool_op(lambda e: e.affine_select(out=trif.ap, in_=negf.ap, pattern=[[-1, 128]], compare_op=ALU.is_ge, fill=0.0, base=0,
                                               channel_multiplier=1), [negf], [trif])
        self.copy(mc["tri"], trif)
        mc["maskb"] = []
        for i in range(4):
            mb = al([512], BF16)
            tmp = self.rstd[i % 2]
            self.pool_op((lambda tmp, i: lambda e: e.affine_select(out=tmp.ap, in_=self.zerosf.ap, pattern=[[1, 512]], compare_op=ALU.is_gt,
                                                                  fill=-30000.0, base=-128 * i, channel_multiplier=-1))(tmp, i),
                         [self.zerosf], [tmp])
            self.copy(mb, tmp)
            mc["maskb"].append(mb)
        vr = self.vrows
        self.dma(vr.p(0, 8), w["lam_re"][l].rearrange("(t g) p -> t (g p)", g=2), q="sp")
        self.dma(vr.p(8, 16), w["lam_im"][l].rearrange("(t g) p -> t (g p)", g=2), q="sp")
        self.dma(vr.p(16, 22), w["conv_w"][l].rearrange("k (c p) -> (k c) p", p=128), q="sp")
        self.dma(vr.p(22, 24), w["ssm_d"][l].rearrange("(c p) -> c p", p=128), q="sp")
        self.dma(vr.p(24, 26), w["pool_scale"][l].rearrange("(c p) -> c p", p=128), q="sp")
        ps = self.bank()
        self.tr(ps[:, 0:26], vr.p(0, 26), self.ident.p(0, 26)[0:26])
        mc["vecs2"] = al([32], F32)
        self.copy(mc["vecs2"][0:26], ps[:, 0:26])
        return mc

    def sincos(self, dsin, dcos, y, yi, ta, tb):
        self.copy(yi, y)
        self.copy(ta, yi)
        self.tt(tb, y, ta, ALU.subtract)
        self.act(dsin, tb, AF.Sin, scale=6.283185)
        self.ts(tb, y, 0.25, ALU.add)
        self.copy(yi, tb)
        self.copy(ta, yi)
        self.tt(tb, tb, ta, ALU.subtract)
        self.act(dcos, tb, AF.Sin, scale=6.283185)

    def ssm_branch(self, l, hT, ub, mc):
        w = self.w
        al = self.alloc
        MUL, ADD, SUB = ALU.mult, ALU.add, ALU.subtract
        m = self.mark()
        v2 = mc["vecs2"]
        lrT, liT = v2[0:8], v2[8:16]
        dcol = v2[22:24]
        sp = al([20 * 8], F32)
        c8 = [sp[i * 8:(i + 1) * 8] for i in range(20)]
        lr, dt, a_, th, mag, thn, ta8, tb8, sn, cs, abr, abi, den, nr, fre, fim, t8, nEi, inre, inim = c8
        yi8 = al([8], I32)
        NT = TC + 1
        lb = al([16, 128], BF16)
        lc = al([16, 128], BF16)
        sinT = al([8, NT], F32)
        cosT = al([8, NT], F32)
        ms_ = self.mark()
        rows = al([272], F32)
        E0 = rows[0:128].p(0, 1)
        E1 = rows[128:256].p(0, 1)
        ld = rows[256:272].p(0, 1)
        self.memset(rows.p(0, 1), 0.0)
        self.memset(rows[0:64].p(0, 1), 1.0)
        self.memset(rows[192:256].p(0, 1), 1.0)
        self.dma(ld, w["log_dt"][l].unsqueeze(0), q="sp")
        ps = self.bank()
        self.mm(ps[:, 0:8], E0, ld.with_ap(ld.ap[:, 0:16:2]), True, False)
        self.mm(ps[:, 0:8], E1, ld.with_ap(ld.ap[:, 1:16:2]), False, True)
        self.act(dt, ps[:, 0:8], AF.Exp)
        self.ts(lr, lrT, -1e-4, ALU.min)
        self.tt(a_, lr, dt, MUL)
        self.tt(th, liT, dt, MUL)
        self.act(mag, a_, AF.Exp)
        self.ts(thn, th, 1.0 / (2 * math.pi), MUL)
        self.sincos(sn, cs, thn, yi8, ta8, tb8)
        self.tt(abr, mag, cs, MUL)
        self.tt(abi, mag, sn, MUL)
        self.tt(den, lr, lr, MUL)
        self.tt(t8, liT, liT, MUL)
        self.tt(den, den, t8, ADD)
        self.recip(den, den)
        self.ts(nr, abr, -1.0, ADD)
        self.tt(fre, nr, lr, MUL)
        self.tt(t8, abi, liT, MUL)
        self.tt(fre, fre, t8, ADD)
        self.tt(fre, fre, den, MUL)
        self.tt(fim, abi, lr, MUL)
        self.tt(t8, nr, liT, MUL)
        self.tt(fim, fim, t8, SUB)
        self.tt(fim, fim, den, MUL)
        if int(os.environ.get("SSMSTOP", "9")) <= 1:
            self.memset(ub, 0.0)
            self.release(m)
            return
        pidx = al([1], I32)
        pj = al([1], I32)
        pf = al([1], F32)
        rm = al([4], F32)
        par = al([2], F32)
        self.pool_op(lambda e: e.iota(pidx.ap, pattern=[[0, 1]], base=0, channel_multiplier=1), [], [pidx])
        self.S.add("dve", lambda e: e.tensor_scalar(out=pj.ap, in0=pidx.ap, scalar1=5, scalar2=None, op0=ALU.logical_shift_right), reads=[pidx], writes=[pj])
        self.copy(pf, pj)
        for q in range(4):
            self.ts(rm[q:q + 1], pf, float(q), ALU.is_equal)
        self.S.add("dve", lambda e: e.tensor_scalar(out=pj.ap, in0=pidx.ap, scalar1=4, scalar2=1, op0=ALU.logical_shift_right, op1=ALU.bitwise_and),
                   reads=[pidx], writes=[pj])
        self.copy(par[1:2], pj)
        self.ts(par[0:1], par[1:2], -1.0, MUL, 1.0, ADD)
        Bre = al([8, 32], F32)
        Bim = al([8, 32], F32)
        bbr = al([8, 32], F32)
        bbi = al([8, 32], F32)
        tb1 = al([8, 32], F32)
        for Bt, nm in ((Bre, "ssm_b_re"), (Bim, "ssm_b_im")):
            self.memset(Bt, 0.0)
            src = w[nm][l].rearrange("(t g) p h -> g p t h", g=2)
            self.dma(Bt.with_ap(Bt.ap[0:64, :, 0:16]), src[0], q="sp")
            self.dma(Bt.with_ap(Bt.ap[64:128, :, 16:32]), src[1], q="sp")
        fr3 = self.bc3(fre, 8, 32)
        fi3 = self.bc3(fim, 8, 32)
        self.tt(bbr, Bre, fr3, MUL)
        self.tt(tb1, Bim, fi3, MUL)
        self.tt(bbr, bbr, tb1, SUB)
        self.tt(bbi, Bim, fr3, MUL)
        self.tt(tb1, Bre, fi3, MUL)
        self.tt(bbi, bbi, tb1, ADD)
        for ri, bb in enumerate((bbr, bbi)):
            for c in range(2):
                ps = self.bank()
                src = bb.with_ap(bb.ap[:, 4 * c:4 * c + 4, :].rearrange("p a b -> p (a b)"))
                self.tr(ps[:, 0:128], src, self.ident)
                for q in range(4):
                    self.ts(lb[ri * 8 + 4 * c + q], ps[:, 0:128], rm[q:q + 1], MUL)
        if int(os.environ.get("SSMSTOP", "9")) <= 2:
            self.memset(ub, 0.0)
            self.release(m)
            return
        self.memset(lc, 0.0)
        InC = al([128], F32)
        for ri, nm in enumerate(("ssm_c_re", "ssm_c_im")):
            for c in range(2):
                src = w[nm][l].rearrange("g h p -> (g h) p")[c * 128:(c + 1) * 128, :]
                self.dma(InC[0:64], src, q="sp")
                self.dma(InC[64:128], src, q="sp")
                self.ts(InC[0:64], InC[0:64], par[0:1], MUL)
                self.ts(InC[64:128], InC[64:128], par[1:2], MUL)
                ps = self.bank()
                self.tr(ps[:, 0:128], InC, self.ident)
                for q in range(4):
                    dst = lc[ri * 8 + 4 * c + q][32 * q:32 * q + 32]
                    if ri == 0:
                        self.copy(dst, ps[:, 32 * q:32 * q + 32])
                    else:
                        self.ts(dst, ps[:, 32 * q:32 * q + 32], -1.0, MUL)
        if int(os.environ.get("SSMSTOP", "9")) <= 3:
            self.memset(ub, 0.0)
            self.release(m)
            return
        iot_i = al([NT], I32)
        iot_f = al([NT], F32)
        self.pool_op(lambda e: e.iota(iot_i.ap, pattern=[[1, NT]], base=0, channel_multiplier=0), [], [iot_i])
        self.copy(iot_f, iot_i)
        mt = self.mark()
        yt = al([8, NT], F32)
        yi = al([8, NT], I32)
        ta = al([8, NT], F32)
        tb = al([8, NT], F32)
        for T in range(8):
            self.ts(yt[T], iot_f, thn[T:T + 1], MUL)
        self.sincos(sinT, cosT, yt, yi, ta, tb)
        self.release(mt)
        for T in range(8):
            self.ts(nEi[T:T + 1], sinT[T, TC:TC + 1], -1.0, MUL)
        if int(os.environ.get("SSMSTOP", "9")) <= 4:
            self.memset(ub, 0.0)
            self.release(m)
            return
        self.release(ms_)
        uS = al([1024], F32)
        uSb = al([1024], BF16)
        sre = al([1024], F32)
        sim = al([1024], F32)
        sbre = al([1024], BF16)
        sbim = al([1024], BF16)
        sre2 = al([1024], F32)
        sim2 = al([1024], F32)
        tA = al([512], F32)
        tB = al([512], F32)
        yv = tA
        y2 = tB
        t1c = al([1], F32)
        wsl = al([8, 256], BF16)
        self.wload(w["w_in"][l], 8, 768, 256, wsl, 0)
        ybanks = [self.banks[6], self.banks[7]]
        rot = [0, 1, 2, 3, 4, 5]
        nb = 512 // TC
        NCH = 1024 // TC
        for c in range(2):
            for hf in range(2):
                t0 = hf * 1024
                for tt in range(2):
                    lt = slice(tt * 512, (tt + 1) * 512)
                    ps = self.bank(rot)
                    for k in range(8):
                        self.mm(ps, wsl[k, c * 128:(c + 1) * 128], hT[k, t0 + tt * 512:t0 + (tt + 1) * 512], k == 0, k == 7)
                    self.act(uS[lt], ps, AF.Identity)
                    self.copy(uSb[lt], ps)
                for q in range(4):
                    T = 4 * c + q
                    cb = self.tb3(cosT[T, 0:TC], nb, TC)
                    sb_ = self.tb3(sinT[T, 0:TC], nb, TC)
                    Er = cosT[T, TC:TC + 1]
                    Ei = sinT[T, TC:TC + 1]
                    magb = mag[T:T + 1].with_ap(mag[T:T + 1].ap.to_broadcast([128, TC]))
                    tA3, tB3 = self.v3(tA, TC), self.v3(tB, TC)
                    for tt in range(2):
                        lt = slice(tt * 512, (tt + 1) * 512)
                        pr = self.bank(rot)
                        pi = self.bank(rot)
                        self.mm(pr, lb[T], uSb[lt], True, True)
                        self.mm(pi, lb[8 + T], uSb[lt], True, True)
                        self.act(sre[lt], pr, AF.Identity)
                        self.act(sim[lt], pi, AF.Identity)
                        r3, i3 = self.v3(sre[lt], TC), self.v3(sim[lt], TC)
                        self.tt(tA3, r3, cb, MUL)
                        self.tt(tB3, i3, sb_, MUL)
                        self.tt(sre2[lt], tA, tB, ADD)
                        self.tt(tA3, i3, cb, MUL)
                        self.tt(tB3, r3, sb_, MUL)
                        self.tt(sim2[lt], tA, tB, SUB)
                    for ch in range(NCH):
                        gch = hf * NCH + ch
                        seg = slice(ch * TC, (ch + 1) * TC)
                        self.scan(sre[seg], magb, sre2[seg], 0.0 if gch == 0 else inre[T:T + 1])
                        self.scan(sim[seg], magb, sim2[seg], 0.0 if gch == 0 else inim[T:T + 1])
                        if gch < 2 * NCH - 1:
                            lre = sre[(ch + 1) * TC - 1:(ch + 1) * TC]
                            lim = sim[(ch + 1) * TC - 1:(ch + 1) * TC]
                            self.ts(t1c, lre, Er, MUL)
                            self.ts(inre[T:T + 1], lim, nEi[T:T + 1], MUL, t1c, ADD)
                            self.ts(t1c, lim, Er, MUL)
                            self.ts(inim[T:T + 1], lre, Ei, MUL, t1c, ADD)
                    for tt in range(2):
                        lt = slice(tt * 512, (tt + 1) * 512)
                        sr3, si3 = self.v3(sre[lt], TC), self.v3(sim[lt], TC)
                        self.tt(tA3, sr3, cb, MUL)
                        self.tt(tB3, si3, sb_, MUL)
                        self.tt(sbre[lt], tA, tB, SUB)
                        self.tt(tA3, si3, cb, MUL)
                        self.tt(tB3, sr3, sb_, MUL)
                        self.tt(sbim[lt], tA, tB, ADD)
                        self.mm(ybanks[tt], lc[T], sbre[lt], q == 0, False)
                        self.mm(ybanks[tt], lc[8 + T], sbim[lt], False, q == 3)
                for tt in range(2):
                    lt = slice(tt * 512, (tt + 1) * 512)
                    self.ts(yv, uS[lt], dcol[c:c + 1], MUL)
                    self.tt(yv, yv, ybanks[tt], ADD)
                    self.tt(y2, yv, yv, MUL)
                    self.ts(y2, y2, 1.5957691216 * 0.044715, MUL, 1.5957691216, ADD)
                    self.tt(y2, y2, yv, MUL)
                    self.act(y2, y2, AF.Sigmoid)
                    self.tt(ub[c, t0 + tt * 512:t0 + (tt + 1) * 512], yv, y2, MUL)
        self.release(m)

    def conv_branch(self, l, hT, ua, mc):
        al = self.alloc
        win = self.w["w_in"][l]
        MUL, ADD = ALU.mult, ALU.add
        m = self.mark()
        v2 = mc["vecs2"]
        upad = al([2 + L], F32)
        csb = al([512], F32)
        yt = al([512], F32)
        s1 = self.wload(win, 8, 0, 512)
        s2 = self.wload(win, 8, 512, 256)
        for i in range(2):
            self.memset(upad[0:2], 0.0)
            for tt in range(4):
                tok = slice(tt * 512, (tt + 1) * 512)
                pc = self.bank()
                pv = self.bank()
                pb = self.bank()
                for k in range(8):
                    self.mm(pc, s1[k, 256 + i * 128:256 + (i + 1) * 128], hT[k, tok], k == 0, k == 7)
                for k in range(8):
                    self.mm(pv, s2[k, i * 128:(i + 1) * 128], hT[k, tok], k == 0, k == 7)
                for k in range(8):
                    self.mm(pb, s1[k, i * 128:(i + 1) * 128], hT[k, tok], k == 0, k == 7)
                self.act(csb, pc, AF.Identity)
                b0 = tt * 512
                self.tt(upad[2 + b0:2 + b0 + 512], csb, pv, MUL)
                self.ts(yt, upad[b0:b0 + 512], v2[16 + i:17 + i], MUL)
                self.stt(yt, upad[1 + b0:1 + b0 + 512], v2[18 + i:19 + i], yt, MUL, ADD)
                self.stt(yt, upad[2 + b0:2 + b0 + 512], v2[20 + i:21 + i], yt, MUL, ADD)
                self.tt(ua[i, tok], yt, pb, MUL)
        self.release(m)

    def pool_branch(self, l, hT, uc, mc):
        al = self.alloc
        w = self.w
        win = w["w_in"][l]
        MUL, ADD, SUB = ALU.mult, ALU.add, ALU.subtract
        m = self.mark()
        v2 = mc["vecs2"]
        uP = al([L], F32)
        csp = al([16 + L], F32)
        dP = al([L], F32)
        pl = al([L], BF16)
        lp = al([2, 128], BF16)
        invc = al([2, 16], F32)
        t16 = al([16], F32)
        io_i = al([16], I32)
        io_f = al([16], F32)
        self.pool_op(lambda e: e.iota(io_i.ap, pattern=[[1, 16]], base=1, channel_multiplier=0), [], [io_i])
        self.copy(io_f, io_i)
        for c in range(2):
            for h2 in range(2):
                self.ts(invc[c].p(64 * h2, 64 * h2 + 64), io_f.p(64 * h2, 64 * h2 + 64), float(POOL_W[2 * c + h2]), ALU.min)
        self.recip(invc, invc)
        self.memset(lp, 0.0)
        for c in range(2):
            for h2 in range(2):
                self.dma(lp[c].p(64 * h2, 64 * h2 + 64)[64 * h2:64 * h2 + 64], w["w_pool"][l, 2 * c + h2], q="pool")
        sl = self.wload(win, 8, 1024, 256)
        onesb = self.onesf[0:1].with_ap(self.onesf[0:1].ap.to_broadcast([128, 512]))
        self.memset(csp[0:16], 0.0)
        for c in range(2):
            for tt in range(4):
                tok = slice(tt * 512, (tt + 1) * 512)
                ps = self.bank()
                for k in range(8):
                    self.mm(ps, sl[k, c * 128:(c + 1) * 128], hT[k, tok], k == 0, k == 7)
                self.act(uP[tok], ps, AF.Identity)
                self.scan(csp[16 + tt * 512:16 + (tt + 1) * 512], onesb, uP[tok], csp[15 + tt * 512:16 + tt * 512])
            for h2 in range(2):
                w_ = POOL_W[2 * c + h2]
                P = (64 * h2, 64 * h2 + 64)
                self.tt(dP.p(*P), csp[16:16 + L].p(*P), csp[16 - w_:16 - w_ + L].p(*P), SUB)
                self.stt(pl.p(*P), dP.p(*P), 1.0 / w_, uP.p(*P), MUL, SUB)
            self.tt(t16, dP[0:16], invc[c], MUL)
            self.tt(pl[0:16], t16, uP[0:16], SUB)
            for tt in range(4):
                tok = slice(tt * 512, (tt + 1) * 512)
                ps = self.bank()
                self.mm(ps, lp[c], pl[tok], True, True)
                self.act(uc[c, tok], ps, AF.Identity, scale=v2[24 + c:25 + c])
        self.release(m)

    def attn_branch(self, l, hT, ud, mc):
        al = self.alloc
        win = self.w["w_in"][l]
        m = self.mark()
        qT = al([L], BF16)
        kT = al([L], BF16)
        vS = al([16, 128], BF16)
        e_ = al([512], F32)
        spb = [al([512], BF16) for _ in range(2)]
        At = [al([512], F32) for _ in range(2)]
        Gts = [al([512], F32) for _ in range(2)]
        Ab = [al([512], BF16) for _ in range(2)]
        rot = [0, 1, 2, 3]
        ui = 0
        for c in range(2):
            sq = self.wload(win, 8, 1280 + c * 128, 128)
            sk = self.wload(win, 8, 1536 + c * 128, 128)
            sv = self.wload(win, 8, 1792 + c * 128, 128)
            for tt in range(4):
                tok = slice(tt * 512, (tt + 1) * 512)
                ps = self.bank(rot)
                for k in range(8):
                    self.mm(ps, sq[k, 0:128], hT[k, tok], k == 0, k == 7)
                self.act(qT[tok], ps, AF.Identity, scale=0.125)
                ps = self.bank(rot)
                for k in range(8):
                    self.mm(ps, sk[k, 0:128], hT[k, tok], k == 0, k == 7)
                self.copy(kT[tok], ps)
            for tb in range(16):
                ps = self.bank(rot)
                for k in range(8):
                    self.mm(ps[:, 0:128], hT[k, tb * 128:(tb + 1) * 128], sv[k, 0:128], k == 0, k == 7)
                if tb % 2 == 0:
                    self.copy(vS[tb], ps[:, 0:128])
                else:
                    self.act(vS[tb], ps[:, 0:128], AF.Identity)
            units = []
            for hh in range(2):
                P = (64 * hh, 64 * hh + 64)
                for qt in range(4):
                    chain = hh * 4 + qt
                    Ob, Rb = (self.banks[4], self.banks[5]) if chain % 2 == 0 else (self.banks[6], self.banks[7])
                    ObP = PV(Ob.ap[P[0]:P[1], :], Ob.bank)
                    nkb = 4 * (qt + 1)
                    qv = qT[qt * 512:(qt + 1) * 512].p(*P)
                    for kb in range(nkb - 1, -1, -1):
                        units.append(dict(hh=hh, qt=qt, kb=kb, first=(kb == nkb - 1), last=(kb == 0), Rb=Rb, ObP=ObP, qv=qv, P=P,
                                          diag=(kb >= 4 * qt)))

            def stageA0(u, i):
                zb = self.bank(rot)
                u["zb"] = zb
                kb, P = u["kb"], u["P"]
                self.mm(zb, kT[kb * 128:(kb + 1) * 128].p(*P), u["qv"], True, not u["diag"])
                if u["diag"]:
                    self.mm(zb, mc["ident_bf"], mc["maskb"][kb - 4 * u["qt"]], False, True)

            def stageA1(u, i):
                zb = u["zb"]
                sp_ = spb[i % 2]
                G = Gts[i % 2]
                u["G"] = G
                kb = u["kb"]
                self.act(e_, zb, AF.Exp)
                self.act(sp_, e_, AF.Ln, bias=self.one1)
                if not u["first"]:
                    self.act(G, u["Rb"], AF.Exp)
                self.mm(zb, mc["tri"], sp_, False, True)
                if kb > 0:
                    self.mm(u["Rb"], mc["negones"], sp_, u["first"], True)

            def stageB(u, i):
                A = Ab[i % 2]
                at = At[i % 2]
                if u["first"]:
                    self.act(A, u["zb"], AF.Exp)
                else:
                    self.act(at, u["zb"], AF.Exp)
                    self.tt(A, at, u["G"], ALU.mult)
                hh, kb = u["hh"], u["kb"]
                self.mm(u["ObP"], vS[kb, 64 * hh:64 * hh + 64], A, u["first"], u["last"])
                if u["last"]:
                    qt = u["qt"]
                    self.act(ud[c, qt * 512:(qt + 1) * 512].p(*u["P"]), u["ObP"], AF.Identity)

            nu = len(units)
            stageA0(units[0], 0)
            for i in range(nu):
                if i + 1 < nu:
                    stageA0(units[i + 1], i + 1)
                stageA1(units[i], i)
                if i >= 1:
                    stageB(units[i - 1], i - 1)
            stageB(units[-1], nu - 1)
        self.release(m)

    def merge_out(self, l, hT, us):
        al = self.alloc
        w = self.w
        win = w["w_in"][l]
        MUL, ADD = ALU.mult, ALU.add
        ynames = ["w_conv_out", None, "w_pool_out", "w_sb_out"]
        rot = [0, 1, 2, 3, 4, 5, 6]
        pss = self.banks[7]
        for hf in range(2):
            t0 = hf * 1024
            m = self.mark()
            mg = al([8, 1024], BF16)
            acc = al([1024], F32)
            sg = al([512], F32)
            sg2 = al([512], F32)
            yt = al([512], F32)
            pr = al([512], F32)
            for n in range(8):
                for b in range(4):
                    sgw = self.wload(win, 8, 2048 + b * 1024 + n * 128, 128)
                    if b == 1:
                        sy = self.wslot(2, 256)
                        self.wload(w["w_glu"][l], 2, n * 128, 128, sy, 0)
                        self.wload(w["w_glu"][l], 2, 1024 + n * 128, 128, sy, 128)
                    else:
                        sy = self.wload(w[ynames[b]][l], 2, n * 128, 128)
                    u = us[b]
                    for tt in range(2):
                        tok = slice(t0 + tt * 512, t0 + (tt + 1) * 512)
                        lt = slice(tt * 512, (tt + 1) * 512)
                        pg = self.bank(rot)
                        for k in range(8):
                            self.mm(pg, sgw[k, 0:128], hT[k, tok], k == 0, k == 7)
                        py = self.bank(rot)
                        for k in range(2):
                            self.mm(py, sy[k, 0:128], u[k, tok], k == 0, k == 1)
                        self.act(sg, pg, AF.Sigmoid)
                        dst = acc[lt] if b == 0 else pr
                        if b == 1:
                            pyg = self.bank(rot)
                            for k in range(2):
                                self.mm(pyg, sy[k, 128:256], u[k, tok], k == 0, k == 1)
                            self.act(sg2, pyg, AF.Sigmoid)
                            self.tt(yt, sg2, py, MUL)
                            self.tt(dst, sg, yt, MUL)
                        else:
                            self.tt(dst, sg, py, MUL)
                        if b in (1, 2):
                            self.tt(acc[lt], acc[lt], pr, ADD)
                        elif b == 3:
                            self.tt(mg[n, lt], acc[lt], pr, ADD)
            so = [self.wload(w["w_out"][l], 8, 0, 512), self.wload(w["w_out"][l], 8, 512, 512)]
            for tt in range(2):
                lt = slice(tt * 512, (tt + 1) * 512)
                ts_ = t0 + tt * 512
                rstd = self.rstd[tt % 2]
                for n in range(8):
                    ps = self.bank(rot)
                    for k in range(8):
                        self.mm(ps, so[n // 4][k, (n % 4) * 128:(n % 4 + 1) * 128], mg[k, lt], k == 0, k == 7)
                    sq = self.sqb[n % 2]
                    self.act(sq, ps, AF.Square)
                    self.mm(pss, self.ones_bf, sq, n == 0, n == 7)
                self.act(rstd, pss, AF.Sqrt, scale=1.0 / D, bias=self.epsb)
                self.recip(rstd, rstd)
                for n in range(8):
                    ps = self.bank(rot)
                    for k in range(8):
                        self.mm(ps, so[n // 4][k, (n % 4) * 128:(n % 4 + 1) * 128], mg[k, lt], k == 0, k == 7)
                    tmp = self.tmpf[n % 2]
                    self.tt(tmp, ps, rstd, MUL)
                    xs = self.xT[n, ts_:ts_ + 512]
                    self.stt(xs, tmp, self.coefG[1][n:n + 1], xs, MUL, ADD)
            self.release(m)

    def dump_u(self, i, u):
        if self.dbg:
            dst = self.dbg_ap[i]
            self.S.add("pool", lambda e: e.dma_start(out=dst, in_=u.ap), reads=[u], dma=True, is_out=True)

    def mixer(self, l):
        al = self.alloc
        m0 = self.mark()
        hT = al([8, L], BF16)
        self.pre_norm(1, hT, 0, 4)
        mc = self.mixer_consts(l)
        br = os.environ.get("BR", "scpam")
        ub = al([2, L], BF16)
        if "s" in br:
            self.ssm_branch(l, hT, ub, mc)
        else:
            self.memset(ub, 0.0)
        ua = al([2, L], BF16)
        if "c" in br:
            self.conv_branch(l, hT, ua, mc)
        else:
            self.memset(ua, 0.0)
        uc = al([2, L], BF16)
        if "p" in br:
            self.pool_branch(l, hT, uc, mc)
        else:
            self.memset(uc, 0.0)
        ud = al([2, L], BF16)
        if "a" in br:
            self.attn_branch(l, hT, ud, mc)
        else:
            self.memset(ud, 0.0)
        for i, u in enumerate((ua, ub, uc, ud)):
            self.dump_u(i, u)
        if "m" in br:
            self.merge_out(l, hT, (ua, ub, uc, ud))
        self.release(m0)


    def build(self):
        nc = self.nc
        w = {}
        shapes = dict(x=[L, D], c=[D], w_ada=[DEPTH, D, 9 * D], b_ada=[DEPTH, 9 * D], g_pre=[DEPTH, 3, D], g_post=[DEPTH, 3, D],
                      w_ff_in=[DEPTH, 2, D, 2 * DFF], w_ff_out=[DEPTH, 2, DFF, D], w_in=[DEPTH, D, 6144], conv_w=[DEPTH, 3, 256],
                      w_conv_out=[DEPTH, 256, D], lam_re=[DEPTH, 16, 64], lam_im=[DEPTH, 16, 64], log_dt=[DEPTH, 16],
                      ssm_b_re=[DEPTH, 16, 64, 16], ssm_b_im=[DEPTH, 16, 64, 16], ssm_c_re=[DEPTH, 16, 16, 64], ssm_c_im=[DEPTH, 16, 16, 64],
                      ssm_d=[DEPTH, 256], w_glu=[DEPTH, 256, 2 * D], w_pool=[DEPTH, 4, 64, 64], pool_scale=[DEPTH, 256],
                      w_pool_out=[DEPTH, 256, D], w_sb_out=[DEPTH, 256, D], w_out=[DEPTH, D, D])
        import os
        for k, s in shapes.items():
            if os.environ.get("ONLYX") and k not in ("x", "c"):
                continue
            w[k] = nc.dram_tensor(k, s, F32, kind="ExternalInput").ap()
        self.w = w
        self.out_ap = nc.dram_tensor("out", [L, D], F32, kind="ExternalOutput").ap()
        if self.dbg:
            self.dbg_ap = nc.dram_tensor("dbg_u", [4, 128, 2, L], F32, kind="ExternalOutput").ap()
        from contextlib import ExitStack
        with ExitStack() as es:
            arena = es.enter_context(nc.sbuf_tensor("arena", [128, ARENA_BYTES // 4], F32))
            self.arena = arena[:]
            self.banks = []
            for b in range(8):
                pt = es.enter_context(nc.psum_tensor(f"ps{b}", [128, 512], F32))
                self.banks.append(PV(pt[:], b))
            self.rot = list(range(8))
            sems = {e: [es.enter_context(nc.semaphore(f"s_{e}_{i}")) for i in range(4)] for e in ("pe", "act", "dve", "pool")}
            sems["sp"] = sems["pool"]
            dsems = {q: [es.enter_context(nc.semaphore(f"d_{q}_{i}")) for i in range(Sched.KDMA)] for q in ("pool", "sp")}
            m0 = self.mark()
            self.xrow = None
            self.xT_only = True
            self.setup_alloc_xrow = True
            self._setup_with_xrow()
            done = 0
            for l in range(DEPTH):
                if done >= self.n_sub:
                    break
                self.layer_params(l)
                self.ffn(l, 0, 0)
                done += 1
                if done >= self.n_sub:
                    break
                self.mixer(l)
                done += 1
                if done >= self.n_sub:
                    break
                self.ffn(l, 2, 1)
                done += 1
            self.store_out()
            block = es.enter_context(nc.Block())
            S = self.S

            @block.tensor
            def _(e):
                S._cur = e
                S.emit_one(nc, "pe", e, sems, dsems)

            @block.scalar
            def _(e):
                S.emit_one(nc, "act", e, sems, dsems)

            @block.vector
            def _(e):
                S.emit_one(nc, "dve", e, sems, dsems)

            @block.gpsimd
            def _(e):
                S.emit_one(nc, "pool", e, sems, dsems)

            @block.sync
            def _(e):
                S.emit_one(nc, "sp", e, sems, dsems)
        return nc

    def _setup_with_xrow(self):
        orig_alloc = self.alloc
        self.xrow = None
        self.setup()


def _emit_one(self, nc, e, eh, sems, dsems):
    if not getattr(self, "_prepared", False):
        for op in self.ops:
            for d in op.deps:
                self.ops[d].marked = True
        for en in self.ENGS:
            cnt = 0
            for op in self.by_eng[en]:
                if op.dma:
                    op.sem = dsems[en][op.qidx % self.KDMA]
                    op.val = 16 * (op.qidx // self.KDMA + 1)
                elif op.marked:
                    ep = cnt // self.EPOCH
                    op.sem = sems[en][ep] if en != "sp" else None
                    op.val = cnt % self.EPOCH + 1
                    cnt += 1
        self._prepared = True
    waited = {}

    def wait(sem, val):
        key = id(sem)
        if waited.get(key, 0) < val:
            eh.wait_ge(sem, val)
            waited[key] = val

    for op in self.by_eng[e]:
        for d in sorted(op.deps):
            o = self.ops[d]
            wait(o.sem, o.val)
        if op.dma and op.qidx >= self.KDMA:
            wait(op.sem, op.val - 16)
        ins = op.fn(eh)
        if op.dma:
            ins.then_inc(op.sem, 16)
        elif op.marked:
            ins.then_inc(op.sem, 1)
    if e == "sp":
        for o in self.out_dmas:
            wait(o.sem, o.val)


Sched.emit_one = _emit_one

_CACHE = {}


def build_nc(n_sub=12, dbg=False):
    nc = bass.Bass("TRN2", target_bir_lowering=False)
    b = Builder(nc, n_sub, dbg)
    b.build()
    return nc


def kernel(n_sub=12, **inputs):
    x = np.ascontiguousarray(inputs["x"], dtype=np.float32)
    c = np.ascontiguousarray(inputs["c"], dtype=np.float32)
    nc = build_nc(n_sub)
    shared = {k: np.ascontiguousarray(v, dtype=np.float32) for k, v in inputs.items() if k not in ("x", "c")}
    in_maps = []
    for b in range(8):
        m = dict(shared)
        m["x"] = x[b]
        m["c"] = c[b]
        in_maps.append(m)
    res = run_bass_kernel_spmd(nc, in_maps, core_ids=list(range(8)))
    out = np.stack([res.results[b]["out"] for b in range(8)], axis=0)
    return out.astype(np.float32)
```

```python
import math
import os
import numpy as np
import concourse.bass as bass
import concourse.mybir as mybir
from concourse.bass_utils import run_bass_kernel_spmd

F32 = mybir.dt.float32
BF16 = mybir.dt.bfloat16
I32 = mybir.dt.int32
AF = mybir.ActivationFunctionType
ALU = mybir.AluOpType

D = 1024
L = 2048
DEPTH = 4
DFF = 2816
NJ = DFF // 128
EPS = 1e-6
G = 256
ARENA_BYTES = 204 * 1024
POOL_W = (2, 4, 8, 16)
TC = 256


def dsz(dt):
    return 2 if dt == BF16 else 4


class V:
    def __init__(self, ap, lo, hi, shape, es, off):
        self.ap, self.lo, self.hi, self.shape, self.es, self.off = ap, lo, hi, shape, es, off

    def keys(self):
        return [("s", i) for i in range(self.lo // G, (self.hi - 1) // G + 1)]

    def __getitem__(self, key):
        if not isinstance(key, tuple):
            key = (key,)
        sh = self.shape
        if len(sh) == 1:
            (k,) = key
            a, b = (k.start or 0, sh[0] if k.stop is None else k.stop) if isinstance(k, slice) else (k, k + 1)
            ap = self.ap[:, a:b]
            return V(ap, self.off + a * self.es, self.off + b * self.es, (b - a,), self.es, self.off + a * self.es)
        k0 = key[0]
        k1 = key[1] if len(key) > 1 else slice(None)
        if isinstance(k0, slice):
            i0, i1 = k0.start or 0, sh[0] if k0.stop is None else k0.stop
        else:
            i0, i1 = k0, k0 + 1
        c0, c1 = k1.start or 0, sh[1] if k1.stop is None else k1.stop
        lo = self.off + (i0 * sh[1] + c0) * self.es
        hi = self.off + ((i1 - 1) * sh[1] + c1) * self.es
        if isinstance(k0, slice):
            ap = self.ap[:, i0:i1, c0:c1]
            return V(ap, lo, hi, (i1 - i0, c1 - c0), self.es, lo) if (c0 == 0 and c1 == sh[1]) else V(ap, lo, hi, None, self.es, lo)
        ap = self.ap[:, i0, c0:c1]
        return V(ap, lo, hi, (c1 - c0,), self.es, lo)

    def p(self, p0, p1):
        key = (slice(p0, p1),) + (slice(None),) * (len(self.ap.shape) - 1)
        return V(self.ap[key], self.lo, self.hi, self.shape, self.es, self.off)

    def with_ap(self, ap):
        return V(ap, self.lo, self.hi, None, self.es, self.off)


class PV:
    def __init__(self, ap, bank):
        self.ap, self.bank = ap, bank

    def keys(self):
        return [("p", self.bank)]

    def __getitem__(self, key):
        return PV(self.ap[key], self.bank)


class Op:
    __slots__ = ("id", "eng", "fn", "deps", "dma", "marked", "sem", "val", "qidx")


class Sched:
    ENGS = ("pe", "act", "dve", "pool", "sp")
    EPOCH = 20000
    KDMA = 8

    def __init__(self):
        self.ops = []
        self.W = {}
        self.R = {}
        self.by_eng = {e: [] for e in self.ENGS}
        self.ndma = {"pool": 0, "sp": 0}
        self.out_dmas = []

    def add(self, eng, fn, reads=(), writes=(), dma=False, is_out=False):
        op = Op()
        op.id = len(self.ops)
        op.eng, op.fn, op.dma, op.marked = eng, fn, dma, False
        op.sem = op.val = None
        deps = set()
        W, R = self.W, self.R
        for r in reads:
            for k in r.keys():
                w = W.get(k)
                if w is not None:
                    o = self.ops[w]
                    if o.dma or dma or o.eng != eng or eng != "pe":
                        deps.add(w)
        for wv in writes:
            for k in wv.keys():
                w = W.get(k)
                if w is not None:
                    o = self.ops[w]
                    if o.dma or dma or o.eng != eng or eng != "pe":
                        deps.add(w)
                rd = R.get(k)
                if rd:
                    for rk, rid in rd.items():
                        o = self.ops[rid]
                        if o.dma or dma or o.eng != eng or eng != "pe":
                            deps.add(rid)
                W[k] = op.id
                R[k] = {}
        for r in reads:
            for k in r.keys():
                R.setdefault(k, {})[("d", op.id) if dma else eng] = op.id
        deps.discard(op.id)
        if dma:
            q = self.ndma[eng]
            self.ndma[eng] = q + 1
            op.qidx = q
        op.deps = deps
        self.ops.append(op)
        self.by_eng[eng].append(op)
        if is_out:
            self.out_dmas.append(op)
        return op

    def emit(self, nc, block_engines, sems, dsems):
        for op in self.ops:
            for d in op.deps:
                self.ops[d].marked = True
        for e in self.ENGS:
            cnt = 0
            for op in self.by_eng[e]:
                if op.dma:
                    op.sem = dsems[e][op.qidx % self.KDMA]
                    op.val = 16 * (op.qidx // self.KDMA + 1)
                elif op.marked:
                    ep = cnt // self.EPOCH
                    op.sem = sems[e][ep]
                    op.val = cnt % self.EPOCH + 1
                    cnt += 1
        for e in self.ENGS:
            eh = block_engines[e]
            waited = {}

            def wait(sem, val):
                key = id(sem)
                if waited.get(key, 0) < val:
                    eh.wait_ge(sem, val)
                    waited[key] = val

            for op in self.by_eng[e]:
                for d in sorted(op.deps):
                    o = self.ops[d]
                    wait(o.sem, o.val)
                if op.dma and op.qidx >= self.KDMA:
                    wait(op.sem, op.val - 16)
                ins = op.fn(eh)
                if op.dma:
                    ins.then_inc(op.sem, 16)
                elif op.marked:
                    ins.then_inc(op.sem, 1)
            if e == "sp":
                for o in self.out_dmas:
                    wait(o.sem, o.val)


class Builder:
    def __init__(self, nc, n_sub, dbg):
        self.nc = nc
        self.S = Sched()
        self.n_sub = n_sub
        self.dbg = dbg
        self.off = 0
        self.bank_rr = 0

    def alloc(self, shape, dt):
        es = dsz(dt)
        n = int(np.prod(shape))
        nbytes = (n * es + 255) // 256 * 256
        off = self.off
        self.off += nbytes
        assert self.off <= ARENA_BYTES, f"arena overflow {self.off}"
        w0 = off // 4
        ap = self.arena[:, w0:w0 + nbytes // 4]
        if dt != F32:
            ap = ap.bitcast(dt)
        ap = ap[:, 0:n]
        if len(shape) == 2:
            ap = ap.rearrange("p (a b) -> p a b", a=shape[0])
        return V(ap, off, off + n * es, tuple(shape), es, off)

    def mark(self):
        return self.off

    def release(self, m):
        self.off = m

    def bank(self, pool=None):
        pool = pool or self.rot
        b = pool[self.bank_rr % len(pool)]
        self.bank_rr += 1
        return self.banks[b]

    def mm(self, out, lhsT, rhs, start, stop):
        self.S.add("pe", lambda e: e.matmul(out.ap, lhsT=lhsT.ap, rhs=rhs.ap, start=start, stop=stop, skip_group_check=True),
                   reads=[lhsT, rhs], writes=[out])

    def tr(self, out, in_, ident):
        self.S.add("pe", lambda e: e.transpose(out.ap, in_.ap, ident.ap), reads=[in_, ident], writes=[out])

    def act(self, out, in_, func, scale=1.0, bias=None, extra_reads=()):
        kw = {}
        rd = [in_] + list(extra_reads)
        if isinstance(scale, V):
            rd.append(scale)
            sc = scale.ap
        else:
            sc = scale
        if bias is not None:
            rd.append(bias)
            kw["bias"] = bias.ap
        self.S.add("act", lambda e: e.activation(out=out.ap, in_=in_.ap, func=func, scale=sc, **kw), reads=rd, writes=[out])

    def tt(self, out, a, b, op, eng="dve"):
        self.S.add(eng, lambda e: e.tensor_tensor(out=out.ap, in0=a.ap, in1=b.ap, op=op), reads=[a, b], writes=[out])

    def ts(self, out, a, s1, op0, s2=None, op1=None, eng="dve"):
        rd = [a]
        s1a = s1.ap if isinstance(s1, V) else s1
        s2a = s2.ap if isinstance(s2, V) else s2
        if isinstance(s1, V):
            rd.append(s1)
        if isinstance(s2, V):
            rd.append(s2)
        if op1 is None:
            self.S.add(eng, lambda e: e.tensor_scalar(out=out.ap, in0=a.ap, scalar1=s1a, scalar2=None, op0=op0), reads=rd, writes=[out])
        else:
            self.S.add(eng, lambda e: e.tensor_scalar(out=out.ap, in0=a.ap, scalar1=s1a, scalar2=s2a, op0=op0, op1=op1), reads=rd, writes=[out])

    def stt(self, out, a, s, b, op0, op1):
        rd = [a, b]
        sa = s.ap if isinstance(s, V) else s
        if isinstance(s, V):
            rd.append(s)
        self.S.add("dve", lambda e: e.scalar_tensor_tensor(out=out.ap, in0=a.ap, scalar=sa, in1=b.ap, op0=op0, op1=op1), reads=rd, writes=[out])

    def copy(self, out, in_, eng="dve"):
        self.S.add(eng, lambda e: e.tensor_copy(out=out.ap, in_=in_.ap), reads=[in_], writes=[out])

    def scan(self, out, d0, d1, init, extra_reads=()):
        rd = [d0, d1] + list(extra_reads)
        ia = init.ap if isinstance(init, V) else init
        if isinstance(init, V):
            rd.append(init)
        self.S.add("dve", lambda e: e.tensor_tensor_scan(out=out.ap, data0=d0.ap, data1=d1.ap, initial=ia, op0=ALU.mult, op1=ALU.add), reads=rd, writes=[out])

    def recip(self, out, in_):
        self.S.add("dve", lambda e: e.reciprocal(out=out.ap, in_=in_.ap), reads=[in_], writes=[out])

    def memset(self, out, val, eng="dve"):
        self.S.add(eng, lambda e: e.memset(out.ap, val), writes=[out])

    def dma(self, out, in_ap, q="pool", reads=(), **kw):
        self.S.add(q, lambda e: e.dma_start(out=out.ap, in_=in_ap, **kw), reads=list(reads), writes=[out], dma=True)

    def dma_out(self, out_ap, in_):
        self.S.add("sp", lambda e: e.dma_start(out=out_ap, in_=in_.ap), reads=[in_], dma=True, is_out=True)

    def wslot(self, kc, ncols):
        s = self.ring[self.ring_i % len(self.ring)]
        self.ring_i += 1
        ap = s.ap[:, 0:kc * ncols].rearrange("p (k c) -> p k c", k=kc)
        return V(ap, s.lo, s.lo + kc * ncols * 2, (kc, ncols), 2, s.lo)

    def wload(self, w2d, kc, c0, ncols, slot=None, scol=0):
        if slot is None:
            slot = self.wslot(kc, ncols)
            dst = slot
        else:
            dst = slot[0:kc, scol:scol + ncols]
        src = w2d[:, c0:c0 + ncols].rearrange("(k p) c -> p k c", p=128)
        self.dma(dst, src, q="pool")
        return slot

    def sumsq_rstd(self, srcs, rstd):
        ps = self.bank()
        for k in range(8):
            sq = self.sqb[k % 2]
            self.act(sq, srcs[k], AF.Square)
            self.mm(ps, self.ones_bf, sq, k == 0, k == 7)
        self.act(rstd, ps, AF.Sqrt, scale=1.0 / D, bias=self.epsb)
        self.recip(rstd, rstd)

    def pre_norm(self, sub, hT, t0, ntt):
        for tt in range(ntt):
            ts_ = t0 + tt * 512
            rstd = self.rstd[tt % 2]
            self.sumsq_rstd([self.xT[k, ts_:ts_ + 512] for k in range(8)], rstd)
            for k in range(8):
                tmp = self.tmpf[k % 2]
                self.tt(tmp, self.xT[k, ts_:ts_ + 512], rstd, ALU.mult)
                self.act(hT[k, tt * 512:(tt + 1) * 512], tmp, AF.Identity, scale=self.coefA[sub][k:k + 1], bias=self.coefB[sub][k:k + 1])

    def post_norm_res(self, sub, fT, t0, ntt):
        for tt in range(ntt):
            ts_ = t0 + tt * 512
            rstd = self.rstd[tt % 2]
            self.sumsq_rstd([fT[k, tt * 512:(tt + 1) * 512] for k in range(8)], rstd)
            for k in range(8):
                tmp = self.tmpf[k % 2]
                self.tt(tmp, fT[k, tt * 512:(tt + 1) * 512], rstd, ALU.mult)
                xs = self.xT[k, ts_:ts_ + 512]
                self.stt(xs, tmp, self.coefG[sub][k:k + 1], xs, ALU.mult, ALU.add)

    def ffn(self, l, sub, fi):
        w_in = self.w["w_ff_in"][l, fi]
        w_out = self.w["w_ff_out"][l, fi]
        for hf in range(2):
            t0 = hf * 1024
            m = self.mark()
            hT = self.alloc([8, 1024], BF16)
            gT = self.alloc([NJ, 1024], BF16)
            fT = self.alloc([8, 1024], F32)
            self.pre_norm(sub, hT, t0, 2)
            for jg in range(NJ // 2):
                slot = self.wslot(8, 512)
                self.wload(w_in, 8, jg * 256, 256, slot, 0)
                self.wload(w_in, 8, DFF + jg * 256, 256, slot, 256)
                for jj in range(2):
                    j = jg * 2 + jj
                    for tt in range(2):
                        pa = self.bank()
                        pb = self.bank()
                        for k in range(8):
                            self.mm(pa, slot[k, jj * 128:(jj + 1) * 128], hT[k, tt * 512:(tt + 1) * 512], k == 0, k == 7)
                        for k in range(8):
                            self.mm(pb, slot[k, 256 + jj * 128:256 + (jj + 1) * 128], hT[k, tt * 512:(tt + 1) * 512], k == 0, k == 7)
                        sa = self.tmpf[(j * 2 + tt) % 2]
                        self.act(sa, pa, AF.Silu)
                        self.tt(gT[j, tt * 512:(tt + 1) * 512], sa, pb, ALU.mult)
            for n in range(8):
                slot = self.wload(w_out, NJ, n * 128, 128)
                for tt in range(2):
                    pf = self.bank()
                    for k in range(NJ):
                        self.mm(pf, slot[k, 0:128], gT[k, tt * 512:(tt + 1) * 512], k == 0, k == NJ - 1)
                    self.act(fT[n, tt * 512:(tt + 1) * 512], pf, AF.Identity)
            self.post_norm_res(sub, fT, t0, 2)
            self.release(m)

    def layer_params(self, l):
        w = self.w
        vr = self.vrows
        self.dma(vr.p(0, 72), w["b_ada"][l].rearrange("(j p) -> j p", p=128), q="sp")
        self.dma(vr.p(72, 96), w["g_pre"][l].rearrange("s (k p) -> (s k) p", p=128), q="sp")
        self.dma(vr.p(96, 120), w["g_post"][l].rearrange("s (k p) -> (s k) p", p=128), q="sp")
        ps = self.bank()
        self.tr(ps[:, 0:120], vr.p(0, 120), self.ident.p(0, 120)[0:120])
        self.copy(self.vecs[0:120], ps[:, 0:120])
        ada = self.ada
        for cg in range(18):
            slot = self.wload(w["w_ada"][l], 8, cg * 512, 512)
            for jj in range(4):
                j = cg * 4 + jj
                pa = self.bank()
                for k in range(8):
                    self.mm(pa[:, 0:2], slot[k, jj * 128:(jj + 1) * 128], self.cT2[k * 2:k * 2 + 2], k == 0, k == 7)
                self.tt(ada[j:j + 1], pa[:, 0:1], self.vecs[j:j + 1], ALU.add)
        for s in range(3):
            resw = 1.0 if s == 1 else 0.5
            self.stt(self.coefA[s], ada[s * 24 + 8:s * 24 + 16], 1.0, self.vecs[72 + s * 8:72 + s * 8 + 8], ALU.add, ALU.mult)
            self.copy(self.coefB[s], ada[s * 24:s * 24 + 8])
            self.ts(self.tmp8, ada[s * 24 + 16:s * 24 + 24], 1.0, ALU.add, resw, ALU.mult)
            self.tt(self.coefG[s], self.tmp8, self.vecs[96 + s * 8:96 + s * 8 + 8], ALU.mult)

    def setup(self):
        nc = self.nc
        al = self.alloc
        self.xT = al([8, L], F32)
        self.ident = al([128], F32)
        self.ones_bf = al([128], BF16)
        self.onesf = al([128], F32)
        self.zerosf = al([512], F32)
        self.epsb = al([1], F32)
        self.one1 = al([1], F32)
        self.cT = al([8], BF16)
        self.cT2 = al([16], BF16)
        self.cTf = al([8], F32)
        self.vrows = al([128], F32)
        self.vecs = al([128], F32)
        self.ada = al([72], F32)
        self.coefA = [al([8], F32) for _ in range(3)]
        self.coefB = [al([8], F32) for _ in range(3)]
        self.coefG = [al([8], F32) for _ in range(3)]
        self.tmp8 = al([8], F32)
        self.ring = [al([4096], BF16) for _ in range(3)]
        self.ring_i = 0
        self.sqb = [al([512], BF16) for _ in range(2)]
        self.rstd = [al([512], F32) for _ in range(2)]
        self.tmpf = [al([512], F32) for _ in range(2)]
        self.memset(self.onesf, 1.0, "pool")
        self.memset(self.zerosf, 0.0, "pool")
        self.memset(self.epsb, EPS, "pool")
        self.memset(self.one1, 1.0, "pool")
        self.memset(self.vrows, 0.0, "pool")
        self.S.add("pool", lambda e: e.affine_select(out=self.ident.ap, in_=self.onesf.ap, pattern=[[-1, 128]], compare_op=ALU.is_equal,
                                                     fill=0.0, base=0, channel_multiplier=1), reads=[self.onesf], writes=[self.ident])
        self.copy(self.ones_bf, self.onesf)
        import os
        crow = self.vrows
        if not os.environ.get("SKIPC"):
            self.dma(crow.p(0, 8), self.w["c"].rearrange("(k p) -> k p", p=128), q="sp")
            ps = self.bank()
            self.tr(ps[:, 0:8], crow.p(0, 8), self.ident.p(0, 8)[0:8])
            self.act(self.cTf, ps[:, 0:8], AF.Silu)
            self.copy(self.cT, self.cTf)
            self.copy(self.cT2.with_ap(self.cT2.ap.rearrange("p (k t) -> p k t", t=2)), self.cTf.with_ap(self.cTf.ap.unsqueeze(2).to_broadcast([128, 8, 2])))
        mk = self.mark()
        self.xrow = [self.alloc([1024], F32) for _ in range(2)]
        xin = self.w["x"]
        for tb in range(16):
            xr = self.xrow[tb % 2]
            self.dma(xr, xin[tb * 128:(tb + 1) * 128, :], q="sp")
            for k in range(8):
                ps = self.bank()
                self.tr(ps[:, 0:128], xr[k * 128:(k + 1) * 128], self.ident)
                if k % 2 == 0:
                    self.copy(self.xT[k, tb * 128:(tb + 1) * 128], ps[:, 0:128])
                else:
                    self.act(self.xT[k, tb * 128:(tb + 1) * 128], ps[:, 0:128], AF.Identity)
        self.release(mk)

    def store_out(self):
        out = self.out_ap
        self.xrow = [self.alloc([1024], F32) for _ in range(2)]
        for tb in range(16):
            xr = self.xrow[tb % 2]
            for k in range(8):
                ps = self.bank()
                self.tr(ps[:, 0:128], self.xT[k, tb * 128:(tb + 1) * 128], self.ident)
                if k % 2 == 0:
                    self.copy(xr[k * 128:(k + 1) * 128], ps[:, 0:128])
                else:
                    self.act(xr[k * 128:(k + 1) * 128], ps[:, 0:128], AF.Identity)
            self.dma_out(out[tb * 128:(tb + 1) * 128, :], xr)

    def bc3(self, v, n, m):
        return v.with_ap(v.ap.unsqueeze(2).to_broadcast([128, n, m]))

    def v3(self, v, b):
        if isinstance(v, PV):
            return PV(v.ap.rearrange("p (a b) -> p a b", b=b), v.bank)
        return v.with_ap(v.ap.rearrange("p (a b) -> p a b", b=b))

    def tb3(self, v, n, b):
        return v.with_ap(v.ap.unsqueeze(1).to_broadcast([128, n, b]))

    def pool_op(self, fn, reads, writes):
        self.S.add("pool", fn, reads=reads, writes=writes)

    def mixer_consts(self, l):
        al = self.alloc
        w = self.w
        mc = {}
        mc["ident_bf"] = al([128], BF16)
        self.copy(mc["ident_bf"], self.ident)
        mc["tri"] = al([128], BF16)
        mc["negones"] = al([128], BF16)
        negf = self.tmpf[0][0:128]
        trif = self.tmpf[1][0:128]
        self.memset(negf, -1.0)
        self.copy(mc["negones"], negf)
        self.pool_op(lambda e: e.affine_select(out=trif.ap, in_=negf.ap, pattern=[[-1, 128]], compare_op=ALU.is_ge, fill=0.0, base=0,
                                               channel_multiplier=1), [negf], [trif])
        self.copy(mc["tri"], trif)
        vr = self.vrows
        self.dma(vr.p(0, 8), w["lam_re"][l].rearrange("(t g) p -> t (g p)", g=2), q="sp")
        self.dma(vr.p(8, 16), w["lam_im"][l].rearrange("(t g) p -> t (g p)", g=2), q="sp")
        self.dma(vr.p(16, 22), w["conv_w"][l].rearrange("k (c p) -> (k c) p", p=128), q="sp")
        self.dma(vr.p(22, 24), w["ssm_d"][l].rearrange("(c p) -> c p", p=128), q="sp")
        self.dma(vr.p(24, 26), w["pool_scale"][l].rearrange("(c p) -> c p", p=128), q="sp")
        ps = self.bank()
        self.tr(ps[:, 0:26], vr.p(0, 26), self.ident.p(0, 26)[0:26])
        mc["vecs2"] = al([32], F32)
        self.copy(mc["vecs2"][0:26], ps[:, 0:26])
        return mc

    def sincos(self, dsin, dcos, y, yi, ta, tb):
        self.copy(yi, y)
        self.copy(ta, yi)
        self.tt(tb, y, ta, ALU.subtract)
        self.act(dsin, tb, AF.Sin, scale=6.283185)
        self.ts(tb, y, 0.25, ALU.add)
        self.copy(yi, tb)
        self.copy(ta, yi)
        self.tt(tb, tb, ta, ALU.subtract)
        self.act(dcos, tb, AF.Sin, scale=6.283185)

    def ssm_branch(self, l, hT, ub, mc):
        w = self.w
        al = self.alloc
        MUL, ADD, SUB = ALU.mult, ALU.add, ALU.subtract
        m = self.mark()
        v2 = mc["vecs2"]
        lrT, liT = v2[0:8], v2[8:16]
        dcol = v2[22:24]
        sp = al([20 * 8], F32)
        c8 = [sp[i * 8:(i + 1) * 8] for i in range(20)]
        lr, dt, a_, th, mag, thn, ta8, tb8, sn, cs, abr, abi, den, nr, fre, fim, t8, nEi, inre, inim = c8
        yi8 = al([8], I32)
        NT = TC + 1
        lb = al([16, 128], BF16)
        lc = al([16, 128], BF16)
        sinT = al([8, NT], F32)
        cosT = al([8, NT], F32)
        ms_ = self.mark()
        rows = al([272], F32)
        E0 = rows[0:128].p(0, 1)
        E1 = rows[128:256].p(0, 1)
        ld = rows[256:272].p(0, 1)
        self.memset(rows.p(0, 1), 0.0)
        self.memset(rows[0:64].p(0, 1), 1.0)
        self.memset(rows[192:256].p(0, 1), 1.0)
        self.dma(ld, w["log_dt"][l].unsqueeze(0), q="sp")
        ps = self.bank()
        self.mm(ps[:, 0:8], E0, ld.with_ap(ld.ap[:, 0:16:2]), True, False)
        self.mm(ps[:, 0:8], E1, ld.with_ap(ld.ap[:, 1:16:2]), False, True)
        self.act(dt, ps[:, 0:8], AF.Exp)
        self.ts(lr, lrT, -1e-4, ALU.min)
        self.tt(a_, lr, dt, MUL)
        self.tt(th, liT, dt, MUL)
        self.act(mag, a_, AF.Exp)
        self.ts(thn, th, 1.0 / (2 * math.pi), MUL)
        self.sincos(sn, cs, thn, yi8, ta8, tb8)
        self.tt(abr, mag, cs, MUL)
        self.tt(abi, mag, sn, MUL)
        self.tt(den, lr, lr, MUL)
        self.tt(t8, liT, liT, MUL)
        self.tt(den, den, t8, ADD)
        self.recip(den, den)
        self.ts(nr, abr, -1.0, ADD)
        self.tt(fre, nr, lr, MUL)
        self.tt(t8, abi, liT, MUL)
        self.tt(fre, fre, t8, ADD)
        self.tt(fre, fre, den, MUL)
        self.tt(fim, abi, lr, MUL)
        self.tt(t8, nr, liT, MUL)
        self.tt(fim, fim, t8, SUB)
        self.tt(fim, fim, den, MUL)
        if int(os.environ.get("SSMSTOP", "9")) <= 1:
            self.memset(ub, 0.0)
            self.release(m)
            return
        pidx = al([1], I32)
        pj = al([1], I32)
        pf = al([1], F32)
        rm = al([4], F32)
        par = al([2], F32)
        self.pool_op(lambda e: e.iota(pidx.ap, pattern=[[0, 1]], base=0, channel_multiplier=1), [], [pidx])
        self.S.add("dve", lambda e: e.tensor_scalar(out=pj.ap, in0=pidx.ap, scalar1=5, scalar2=None, op0=ALU.logical_shift_right), reads=[pidx], writes=[pj])
        self.copy(pf, pj)
        for q in range(4):
            self.ts(rm[q:q + 1], pf, float(q), ALU.is_equal)
        self.S.add("dve", lambda e: e.tensor_scalar(out=pj.ap, in0=pidx.ap, scalar1=4, scalar2=1, op0=ALU.logical_shift_right, op1=ALU.bitwise_and),
                   reads=[pidx], writes=[pj])
        self.copy(par[1:2], pj)
        self.ts(par[0:1], par[1:2], -1.0, MUL, 1.0, ADD)
        Bre = al([8, 32], F32)
        Bim = al([8, 32], F32)
        bbr = al([8, 32], F32)
        bbi = al([8, 32], F32)
        tb1 = al([8, 32], F32)
        for Bt, nm in ((Bre, "ssm_b_re"), (Bim, "ssm_b_im")):
            self.memset(Bt, 0.0)
            src = w[nm][l].rearrange("(t g) p h -> g p t h", g=2)
            self.dma(Bt.with_ap(Bt.ap[0:64, :, 0:16]), src[0], q="sp")
            self.dma(Bt.with_ap(Bt.ap[64:128, :, 16:32]), src[1], q="sp")
        fr3 = self.bc3(fre, 8, 32)
        fi3 = self.bc3(fim, 8, 32)
        self.tt(bbr, Bre, fr3, MUL)
        self.tt(tb1, Bim, fi3, MUL)
        self.tt(bbr, bbr, tb1, SUB)
        self.tt(bbi, Bim, fr3, MUL)
        self.tt(tb1, Bre, fi3, MUL)
        self.tt(bbi, bbi, tb1, ADD)
        for ri, bb in enumerate((bbr, bbi)):
            for c in range(2):
                ps = self.bank()
                src = bb.with_ap(bb.ap[:, 4 * c:4 * c + 4, :].rearrange("p a b -> p (a b)"))
                self.tr(ps[:, 0:128], src, self.ident)
                for q in range(4):
                    self.ts(lb[ri * 8 + 4 * c + q], ps[:, 0:128], rm[q:q + 1], MUL)
        if int(os.environ.get("SSMSTOP", "9")) <= 2:
            self.memset(ub, 0.0)
            self.release(m)
            return
        self.memset(lc, 0.0)
        InC = al([128], F32)
        for ri, nm in enumerate(("ssm_c_re", "ssm_c_im")):
            for c in range(2):
                src = w[nm][l].rearrange("g h p -> (g h) p")[c * 128:(c + 1) * 128, :]
                self.dma(InC[0:64], src, q="sp")
                self.dma(InC[64:128], src, q="sp")
                self.ts(InC[0:64], InC[0:64], par[0:1], MUL)
                self.ts(InC[64:128], InC[64:128], par[1:2], MUL)
                ps = self.bank()
                self.tr(ps[:, 0:128], InC, self.ident)
                for q in range(4):
                    dst = lc[ri * 8 + 4 * c + q][32 * q:32 * q + 32]
                    if ri == 0:
                        self.copy(dst, ps[:, 32 * q:32 * q + 32])
                    else:
                        self.ts(dst, ps[:, 32 * q:32 * q + 32], -1.0, MUL)
        if int(os.environ.get("SSMSTOP", "9")) <= 3:
            self.memset(ub, 0.0)
            self.release(m)
            return
        iot_i = al([NT], I32)
        iot_f = al([NT], F32)
        self.pool_op(lambda e: e.iota(iot_i.ap, pattern=[[1, NT]], base=0, channel_multiplier=0), [], [iot_i])
        self.copy(iot_f, iot_i)
        mt = self.mark()
        yt = al([4, NT], F32)
        yi = al([4, NT], I32)
        ta = al([4, NT], F32)
        tb = al([4, NT], F32)
        for h4 in range(2):
            for T in range(4):
                self.ts(yt[T], iot_f, thn[4 * h4 + T:4 * h4 + T + 1], MUL)
            self.sincos(sinT[4 * h4:4 * h4 + 4], cosT[4 * h4:4 * h4 + 4], yt, yi, ta, tb)
        self.release(mt)
        for T in range(8):
            self.ts(nEi[T:T + 1], sinT[T, TC:TC + 1], -1.0, MUL)
        if int(os.environ.get("SSMSTOP", "9")) <= 4:
            self.memset(ub, 0.0)
            self.release(m)
            return
        self.release(ms_)
        uS = al([1024], F32)
        uSb = al([1024], BF16)
        sre = al([1024], F32)
        sim = al([1024], F32)
        sbre = al([1024], BF16)
        sbim = al([1024], BF16)
        sre2 = al([1024], F32)
        sim2 = al([1024], F32)
        tA = al([512], F32)
        tB = al([512], F32)
        yv = tA
        y2 = tB
        t1c = al([1], F32)
        wsl = self.wload(w["w_in"][l], 8, 768, 256)
        ybanks = [self.banks[6], self.banks[7]]
        rot = [0, 1, 2, 3, 4, 5]
        nb = 512 // TC
        NCH = 1024 // TC
        for c in range(2):
            for hf in range(2):
                t0 = hf * 1024
                for tt in range(2):
                    lt = slice(tt * 512, (tt + 1) * 512)
                    ps = self.bank(rot)
                    for k in range(8):
                        self.mm(ps, wsl[k, c * 128:(c + 1) * 128], hT[k, t0 + tt * 512:t0 + (tt + 1) * 512], k == 0, k == 7)
                    self.act(uS[lt], ps, AF.Identity)
                    self.copy(uSb[lt], ps)
                for q in range(4):
                    T = 4 * c + q
                    cb = self.tb3(cosT[T, 0:TC], nb, TC)
                    sb_ = self.tb3(sinT[T, 0:TC], nb, TC)
                    Er = cosT[T, TC:TC + 1]
                    Ei = sinT[T, TC:TC + 1]
                    magb = mag[T:T + 1].with_ap(mag[T:T + 1].ap.to_broadcast([128, TC]))
                    tA3, tB3 = self.v3(tA, TC), self.v3(tB, TC)
                    for tt in range(2):
                        lt = slice(tt * 512, (tt + 1) * 512)
                        pr = self.bank(rot)
                        pi = self.bank(rot)
                        self.mm(pr, lb[T], uSb[lt], True, True)
                        self.mm(pi, lb[8 + T], uSb[lt], True, True)
                        self.act(sre[lt], pr, AF.Identity)
                        self.act(sim[lt], pi, AF.Identity)
                        r3, i3 = self.v3(sre[lt], TC), self.v3(sim[lt], TC)
                        self.tt(tA3, r3, cb, MUL)
                        self.tt(tB3, i3, sb_, MUL)
                        self.tt(sre2[lt], tA, tB, ADD)
                        self.tt(tA3, i3, cb, MUL)
                        self.tt(tB3, r3, sb_, MUL)
                        self.tt(sim2[lt], tA, tB, SUB)
                    for ch in range(NCH):
                        gch = hf * NCH + ch
                        seg = slice(ch * TC, (ch + 1) * TC)
                        self.scan(sre[seg], magb, sre2[seg], 0.0 if gch == 0 else inre[T:T + 1])
                        self.scan(sim[seg], magb, sim2[seg], 0.0 if gch == 0 else inim[T:T + 1])
                        if gch < 2 * NCH - 1:
                            lre = sre[(ch + 1) * TC - 1:(ch + 1) * TC]
                            lim = sim[(ch + 1) * TC - 1:(ch + 1) * TC]
                            self.ts(t1c, lre, Er, MUL)
                            self.ts(inre[T:T + 1], lim, nEi[T:T + 1], MUL, t1c, ADD)
                            self.ts(t1c, lim, Er, MUL)
                            self.ts(inim[T:T + 1], lre, Ei, MUL, t1c, ADD)
                    for tt in range(2):
                        lt = slice(tt * 512, (tt + 1) * 512)
                        sr3, si3 = self.v3(sre[lt], TC), self.v3(sim[lt], TC)
                        self.tt(tA3, sr3, cb, MUL)
                        self.tt(tB3, si3, sb_, MUL)
                        self.tt(sbre[lt], tA, tB, SUB)
                        self.tt(tA3, si3, cb, MUL)
                        self.tt(tB3, sr3, sb_, MUL)
                        self.tt(sbim[lt], tA, tB, ADD)
                        self.mm(ybanks[tt], lc[T], sbre[lt], q == 0, False)
                        self.mm(ybanks[tt], lc[8 + T], sbim[lt], False, q == 3)
                for tt in range(2):
                    lt = slice(tt * 512, (tt + 1) * 512)
                    self.ts(yv, uS[lt], dcol[c:c + 1], MUL)
                    self.tt(yv, yv, ybanks[tt], ADD)
                    self.tt(y2, yv, yv, MUL)
                    self.ts(y2, y2, 1.5957691216 * 0.044715, MUL, 1.5957691216, ADD)
                    self.tt(y2, y2, yv, MUL)
                    self.act(y2, y2, AF.Sigmoid)
                    self.tt(ub[c, t0 + tt * 512:t0 + (tt + 1) * 512], yv, y2, MUL)
        self.release(m)

    def conv_branch(self, l, hT, ua, mc):
        al = self.alloc
        win = self.w["w_in"][l]
        MUL, ADD = ALU.mult, ALU.add
        m = self.mark()
        v2 = mc["vecs2"]
        upad = al([2 + L], F32)
        csb = al([512], F32)
        yt = al([512], F32)
        s1 = self.wload(win, 8, 0, 512)
        s2 = self.wload(win, 8, 512, 256)
        for i in range(2):
            self.memset(upad[0:2], 0.0)
            for tt in range(4):
                tok = slice(tt * 512, (tt + 1) * 512)
                pc = self.bank()
                pv = self.bank()
                pb = self.bank()
                for k in range(8):
                    self.mm(pc, s1[k, 256 + i * 128:256 + (i + 1) * 128], hT[k, tok], k == 0, k == 7)
                for k in range(8):
                    self.mm(pv, s2[k, i * 128:(i + 1) * 128], hT[k, tok], k == 0, k == 7)
                for k in range(8):
                    self.mm(pb, s1[k, i * 128:(i + 1) * 128], hT[k, tok], k == 0, k == 7)
                self.act(csb, pc, AF.Identity)
                b0 = tt * 512
                self.tt(upad[2 + b0:2 + b0 + 512], csb, pv, MUL)
                self.ts(yt, upad[b0:b0 + 512], v2[16 + i:17 + i], MUL)
                self.stt(yt, upad[1 + b0:1 + b0 + 512], v2[18 + i:19 + i], yt, MUL, ADD)
                self.stt(yt, upad[2 + b0:2 + b0 + 512], v2[20 + i:21 + i], yt, MUL, ADD)
                self.tt(ua[i, tok], yt, pb, MUL)
        self.release(m)

    def pool_branch(self, l, hT, uc, mc):
        al = self.alloc
        w = self.w
        win = w["w_in"][l]
        MUL, ADD, SUB = ALU.mult, ALU.add, ALU.subtract
        m = self.mark()
        v2 = mc["vecs2"]
        uP = al([L], F32)
        csp = al([16 + L], F32)
        dP = al([L], F32)
        pl = al([L], BF16)
        lp = al([2, 128], BF16)
        invc = al([2, 16], F32)
        t16 = al([16], F32)
        io_i = al([16], I32)
        io_f = al([16], F32)
        self.pool_op(lambda e: e.iota(io_i.ap, pattern=[[1, 16]], base=1, channel_multiplier=0), [], [io_i])
        self.copy(io_f, io_i)
        for c in range(2):
            for h2 in range(2):
                self.ts(invc[c].p(64 * h2, 64 * h2 + 64), io_f.p(64 * h2, 64 * h2 + 64), float(POOL_W[2 * c + h2]), ALU.min)
        self.recip(invc, invc)
        self.memset(lp, 0.0)
        for c in range(2):
            for h2 in range(2):
                self.dma(lp[c].p(64 * h2, 64 * h2 + 64)[64 * h2:64 * h2 + 64], w["w_pool"][l, 2 * c + h2], q="pool")
        sl = self.wload(win, 8, 1024, 256)
        onesb = self.onesf[0:1].with_ap(self.onesf[0:1].ap.to_broadcast([128, 512]))
        self.memset(csp[0:16], 0.0)
        for c in range(2):
            for tt in range(4):
                tok = slice(tt * 512, (tt + 1) * 512)
                ps = self.bank()
                for k in range(8):
                    self.mm(ps, sl[k, c * 128:(c + 1) * 128], hT[k, tok], k == 0, k == 7)
                self.act(uP[tok], ps, AF.Identity)
                self.scan(csp[16 + tt * 512:16 + (tt + 1) * 512], onesb, uP[tok], csp[15 + tt * 512:16 + tt * 512])
            for h2 in range(2):
                w_ = POOL_W[2 * c + h2]
                P = (64 * h2, 64 * h2 + 64)
                self.tt(dP.p(*P), csp[16:16 + L].p(*P), csp[16 - w_:16 - w_ + L].p(*P), SUB)
                self.stt(pl.p(*P), dP.p(*P), 1.0 / w_, uP.p(*P), MUL, SUB)
            self.tt(t16, dP[0:16], invc[c], MUL)
            self.tt(pl[0:16], t16, uP[0:16], SUB)
            for tt in range(4):
                tok = slice(tt * 512, (tt + 1) * 512)
                ps = self.bank()
                self.mm(ps, lp[c], pl[tok], True, True)
                self.act(uc[c, tok], ps, AF.Identity, scale=v2[24 + c:25 + c])
        self.release(m)

    def attn_branch(self, l, hT, ud, mc):
        al = self.alloc
        win = self.w["w_in"][l]
        m = self.mark()
        qT = al([L], BF16)
        kT = al([L], BF16)
        vS = al([16, 128], BF16)
        e_ = al([512], F32)
        spb = [al([512], BF16) for _ in range(2)]
        At = [al([512], F32) for _ in range(2)]
        Gts = [al([512], F32) for _ in range(2)]
        Ab = [al([512], BF16) for _ in range(2)]
        maskb = []
        for i in range(4):
            mb = al([512], BF16)
            tmp = self.rstd[i % 2]
            self.pool_op((lambda tmp, i: lambda e: e.affine_select(out=tmp.ap, in_=self.zerosf.ap, pattern=[[1, 512]], compare_op=ALU.is_gt,
                                                                  fill=-30000.0, base=-128 * i, channel_multiplier=-1))(tmp, i),
                         [self.zerosf], [tmp])
            self.copy(mb, tmp)
            maskb.append(mb)
        mc = dict(mc, maskb=maskb)
        rot = [0, 1, 2, 3]
        ui = 0
        for c in range(2):
            sq = self.wload(win, 8, 1280 + c * 128, 128)
            sk = self.wload(win, 8, 1536 + c * 128, 128)
            sv = self.wload(win, 8, 1792 + c * 128, 128)
            for tt in range(4):
                tok = slice(tt * 512, (tt + 1) * 512)
                ps = self.bank(rot)
                for k in range(8):
                    self.mm(ps, sq[k, 0:128], hT[k, tok], k == 0, k == 7)
                self.act(qT[tok], ps, AF.Identity, scale=0.125)
                ps = self.bank(rot)
                for k in range(8):
                    self.mm(ps, sk[k, 0:128], hT[k, tok], k == 0, k == 7)
                self.copy(kT[tok], ps)
            for tb in range(16):
                ps = self.bank(rot)
                for k in range(8):
                    self.mm(ps[:, 0:128], hT[k, tb * 128:(tb + 1) * 128], sv[k, 0:128], k == 0, k == 7)
                if tb % 2 == 0:
                    self.copy(vS[tb], ps[:, 0:128])
                else:
                    self.act(vS[tb], ps[:, 0:128], AF.Identity)
            units = []
            for hh in range(2):
                P = (64 * hh, 64 * hh + 64)
                for qt in range(4):
                    chain = hh * 4 + qt
                    Ob, Rb = (self.banks[4], self.banks[5]) if chain % 2 == 0 else (self.banks[6], self.banks[7])
                    ObP = PV(Ob.ap[P[0]:P[1], :], Ob.bank)
                    nkb = 4 * (qt + 1)
                    qv = qT[qt * 512:(qt + 1) * 512].p(*P)
                    for kb in range(nkb - 1, -1, -1):
                        units.append(dict(hh=hh, qt=qt, kb=kb, first=(kb == nkb - 1), last=(kb == 0), Rb=Rb, ObP=ObP, qv=qv, P=P,
                                          diag=(kb >= 4 * qt)))

            def stageA0(u, i):
                zb = self.bank(rot)
                u["zb"] = zb
                kb, P = u["kb"], u["P"]
                self.mm(zb, kT[kb * 128:(kb + 1) * 128].p(*P), u["qv"], True, not u["diag"])
                if u["diag"]:
                    self.mm(zb, mc["ident_bf"], mc["maskb"][kb - 4 * u["qt"]], False, True)

            def stageA1(u, i):
                zb = u["zb"]
                sp_ = spb[i % 2]
                G = Gts[i % 2]
                u["G"] = G
                kb = u["kb"]
                self.act(e_, zb, AF.Exp)
                self.act(sp_, e_, AF.Ln, bias=self.one1)
                if not u["first"]:
                    self.act(G, u["Rb"], AF.Exp)
                self.mm(zb, mc["tri"], sp_, False, True)
                if kb > 0:
                    self.mm(u["Rb"], mc["negones"], sp_, u["first"], True)

            def stageB(u, i):
                A = Ab[i % 2]
                at = At[i % 2]
                if u["first"]:
                    self.act(A, u["zb"], AF.Exp)
                else:
                    self.act(at, u["zb"], AF.Exp)
                    self.tt(A, at, u["G"], ALU.mult)
                hh, kb = u["hh"], u["kb"]
                self.mm(u["ObP"], vS[kb, 64 * hh:64 * hh + 64], A, u["first"], u["last"])
                if u["last"]:
                    qt = u["qt"]
                    self.act(ud[c, qt * 512:(qt + 1) * 512].p(*u["P"]), u["ObP"], AF.Identity)

            nu = len(units)
            stageA0(units[0], 0)
            for i in range(nu):
                if i + 1 < nu:
                    stageA0(units[i + 1], i + 1)
                stageA1(units[i], i)
                if i >= 1:
                    stageB(units[i - 1], i - 1)
            stageB(units[-1], nu - 1)
        self.release(m)

    def merge_out(self, l, hT, us):
        al = self.alloc
        w = self.w
        win = w["w_in"][l]
        MUL, ADD = ALU.mult, ALU.add
        ynames = ["w_conv_out", None, "w_pool_out", "w_sb_out"]
        rot = [0, 1, 2, 3, 4, 5, 6]
        pss = self.banks[7]
        for hf in range(2):
            t0 = hf * 1024
            m = self.mark()
            mg = al([8, 1024], BF16)
            acc = al([1024], F32)
            sg = al([512], F32)
            sg2 = al([512], F32)
            yt = al([512], F32)
            pr = al([512], F32)
            for n in range(8):
                for b in range(4):
                    sgw = self.wload(win, 8, 2048 + b * 1024 + n * 128, 128)
                    if b == 1:
                        sy = self.wslot(2, 256)
                        self.wload(w["w_glu"][l], 2, n * 128, 128, sy, 0)
                        self.wload(w["w_glu"][l], 2, 1024 + n * 128, 128, sy, 128)
                    else:
                        sy = self.wload(w[ynames[b]][l], 2, n * 128, 128)
                    u = us[b]
                    for tt in range(2):
                        tok = slice(t0 + tt * 512, t0 + (tt + 1) * 512)
                        lt = slice(tt * 512, (tt + 1) * 512)
                        pg = self.bank(rot)
                        for k in range(8):
                            self.mm(pg, sgw[k, 0:128], hT[k, tok], k == 0, k == 7)
                        py = self.bank(rot)
                        for k in range(2):
                            self.mm(py, sy[k, 0:128], u[k, tok], k == 0, k == 1)
                        self.act(sg, pg, AF.Sigmoid)
                        dst = acc[lt] if b == 0 else pr
                        if b == 1:
                            pyg = self.bank(rot)
                            for k in range(2):
                                self.mm(pyg, sy[k, 128:256], u[k, tok], k == 0, k == 1)
                            self.act(sg2, pyg, AF.Sigmoid)
                            self.tt(yt, sg2, py, MUL)
                            self.tt(dst, sg, yt, MUL)
                        else:
                            self.tt(dst, sg, py, MUL)
                        if b in (1, 2):
                            self.tt(acc[lt], acc[lt], pr, ADD)
                        elif b == 3:
                            self.tt(mg[n, lt], acc[lt], pr, ADD)
            so = [self.wload(w["w_out"][l], 8, 0, 512), self.wload(w["w_out"][l], 8, 512, 512)]
            for tt in range(2):
                lt = slice(tt * 512, (tt + 1) * 512)
                ts_ = t0 + tt * 512
                rstd = self.rstd[tt % 2]
                for n in range(8):
                    ps = self.bank(rot)
                    for k in range(8):
                        self.mm(ps, so[n // 4][k, (n % 4) * 128:(n % 4 + 1) * 128], mg[k, lt], k == 0, k == 7)
                    sq = self.sqb[n % 2]
                    self.act(sq, ps, AF.Square)
                    self.mm(pss, self.ones_bf, sq, n == 0, n == 7)
                self.act(rstd, pss, AF.Sqrt, scale=1.0 / D, bias=self.epsb)
                self.recip(rstd, rstd)
                for n in range(8):
                    ps = self.bank(rot)
                    for k in range(8):
                        self.mm(ps, so[n // 4][k, (n % 4) * 128:(n % 4 + 1) * 128], mg[k, lt], k == 0, k == 7)
                    tmp = self.tmpf[n % 2]
                    self.tt(tmp, ps, rstd, MUL)
                    xs = self.xT[n, ts_:ts_ + 512]
                    self.stt(xs, tmp, self.coefG[1][n:n + 1], xs, MUL, ADD)
            self.release(m)

    def dump_u(self, i, u):
        if self.dbg:
            dst = self.dbg_ap[i]
            self.S.add("pool", lambda e: e.dma_start(out=dst, in_=u.ap), reads=[u], dma=True, is_out=True)

    def mixer(self, l):
        al = self.alloc
        m0 = self.mark()
        hT = al([8, L], BF16)
        self.pre_norm(1, hT, 0, 4)
        mc = self.mixer_consts(l)
        br = os.environ.get("BR", "scpam")
        ub = al([2, L], BF16)
        if "s" in br:
            self.ssm_branch(l, hT, ub, mc)
        else:
            self.memset(ub, 0.0)
        ua = al([2, L], BF16)
        if "c" in br:
            self.conv_branch(l, hT, ua, mc)
        else:
            self.memset(ua, 0.0)
        uc = al([2, L], BF16)
        if "p" in br:
            self.pool_branch(l, hT, uc, mc)
        else:
            self.memset(uc, 0.0)
        ud = al([2, L], BF16)
        if "a" in br:
            self.attn_branch(l, hT, ud, mc)
        else:
            self.memset(ud, 0.0)
        for i, u in enumerate((ua, ub, uc, ud)):
            self.dump_u(i, u)
        if "m" in br:
            self.merge_out(l, hT, (ua, ub, uc, ud))
        self.release(m0)


    def build(self):
        nc = self.nc
        w = {}
        shapes = dict(x=[L, D], c=[D], w_ada=[DEPTH, D, 9 * D], b_ada=[DEPTH, 9 * D], g_pre=[DEPTH, 3, D], g_post=[DEPTH, 3, D],
                      w_ff_in=[DEPTH, 2, D, 2 * DFF], w_ff_out=[DEPTH, 2, DFF, D], w_in=[DEPTH, D, 6144], conv_w=[DEPTH, 3, 256],
                      w_conv_out=[DEPTH, 256, D], lam_re=[DEPTH, 16, 64], lam_im=[DEPTH, 16, 64], log_dt=[DEPTH, 16],
                      ssm_b_re=[DEPTH, 16, 64, 16], ssm_b_im=[DEPTH, 16, 64, 16], ssm_c_re=[DEPTH, 16, 16, 64], ssm_c_im=[DEPTH, 16, 16, 64],
                      ssm_d=[DEPTH, 256], w_glu=[DEPTH, 256, 2 * D], w_pool=[DEPTH, 4, 64, 64], pool_scale=[DEPTH, 256],
                      w_pool_out=[DEPTH, 256, D], w_sb_out=[DEPTH, 256, D], w_out=[DEPTH, D, D])
        import os
        for k, s in shapes.items():
            if os.environ.get("ONLYX") and k not in ("x", "c"):
                continue
            w[k] = nc.dram_tensor(k, s, F32, kind="ExternalInput").ap()
        self.w = w
        self.out_ap = nc.dram_tensor("out", [L, D], F32, kind="ExternalOutput").ap()
        if self.dbg:
            self.dbg_ap = nc.dram_tensor("dbg_u", [4, 128, 2, L], F32, kind="ExternalOutput").ap()
        from contextlib import ExitStack
        with ExitStack() as es:
            arena = es.enter_context(nc.sbuf_tensor("arena", [128, ARENA_BYTES // 4], F32))
            self.arena = arena[:]
            self.banks = []
            for b in range(8):
                pt = es.enter_context(nc.psum_tensor(f"ps{b}", [128, 512], F32))
                self.banks.append(PV(pt[:], b))
            self.rot = list(range(8))
            sems = {e: [es.enter_context(nc.semaphore(f"s_{e}_{i}")) for i in range(4)] for e in ("pe", "act", "dve", "pool")}
            sems["sp"] = sems["pool"]
            dsems = {q: [es.enter_context(nc.semaphore(f"d_{q}_{i}")) for i in range(Sched.KDMA)] for q in ("pool", "sp")}
            m0 = self.mark()
            self.xrow = None
            self.xT_only = True
            self.setup_alloc_xrow = True
            self._setup_with_xrow()
            done = 0
            for l in range(DEPTH):
                if done >= self.n_sub:
                    break
                self.layer_params(l)
                self.ffn(l, 0, 0)
                done += 1
                if done >= self.n_sub:
                    break
                self.mixer(l)
                done += 1
                if done >= self.n_sub:
                    break
                self.ffn(l, 2, 1)
                done += 1
            self.store_out()
            block = es.enter_context(nc.Block())
            S = self.S

            @block.tensor
            def _(e):
                S._cur = e
                S.emit_one(nc, "pe", e, sems, dsems)

            @block.scalar
            def _(e):
                S.emit_one(nc, "act", e, sems, dsems)

            @block.vector
            def _(e):
                S.emit_one(nc, "dve", e, sems, dsems)

            @block.gpsimd
            def _(e):
                S.emit_one(nc, "pool", e, sems, dsems)

            @block.sync
            def _(e):
                S.emit_one(nc, "sp", e, sems, dsems)
        return nc

    def _setup_with_xrow(self):
        orig_alloc = self.alloc
        self.xrow = None
        self.setup()


def _emit_one(self, nc, e, eh, sems, dsems):
    if not getattr(self, "_prepared", False):
        for op in self.ops:
            for d in op.deps:
                self.ops[d].marked = True
        for en in self.ENGS:
            cnt = 0
            for op in self.by_eng[en]:
                if op.dma:
                    op.sem = dsems[en][op.qidx % self.KDMA]
                    op.val = 16 * (op.qidx // self.KDMA + 1)
                elif op.marked:
                    ep = cnt // self.EPOCH
                    op.sem = sems[en][ep] if en != "sp" else None
                    op.val = cnt % self.EPOCH + 1
                    cnt += 1
        self._prepared = True
    waited = {}

    def wait(sem, val):
        key = id(sem)
        if waited.get(key, 0) < val:
            eh.wait_ge(sem, val)
            waited[key] = val

    for op in self.by_eng[e]:
        for d in sorted(op.deps):
            o = self.ops[d]
            wait(o.sem, o.val)
        if op.dma and op.qidx >= self.KDMA:
            wait(op.sem, op.val - 16)
        ins = op.fn(eh)
        if op.dma:
            ins.then_inc(op.sem, 16)
        elif op.marked:
            ins.then_inc(op.sem, 1)
    if e == "sp":
        for o in self.out_dmas:
            wait(o.sem, o.val)


Sched.emit_one = _emit_one

_CACHE = {}


def build_nc(n_sub=12, dbg=False):
    nc = bass.Bass("TRN2", target_bir_lowering=False)
    b = Builder(nc, n_sub, dbg)
    b.build()
    return nc


def kernel(n_sub=12, **inputs):
    x = np.ascontiguousarray(inputs["x"], dtype=np.float32)
    c = np.ascontiguousarray(inputs["c"], dtype=np.float32)
    nc = build_nc(n_sub)
    shared = {k: np.ascontiguousarray(v, dtype=np.float32) for k, v in inputs.items() if k not in ("x", "c")}
    in_maps = []
    for b in range(8):
        m = dict(shared)
        m["x"] = x[b]
        m["c"] = c[b]
        in_maps.append(m)
    res = run_bass_kernel_spmd(nc, in_maps, core_ids=list(range(8)))
    out = np.stack([res.results[b]["out"] for b in range(8)], axis=0)
    return out.astype(np.float32)
```

```python
import math
import os
import numpy as np
import concourse.bass as bass
import concourse.mybir as mybir
from concourse.bass_utils import run_bass_kernel_spmd

F32 = mybir.dt.float32
BF16 = mybir.dt.bfloat16
I32 = mybir.dt.int32
AF = mybir.ActivationFunctionType
ALU = mybir.AluOpType

D = 1024
L = 2048
DEPTH = 4
DFF = 2816
NJ = DFF // 128
EPS = 1e-6
G = 256
ARENA_BYTES = 204 * 1024
POOL_W = (2, 4, 8, 16)
TC = 256


def dsz(dt):
    return 2 if dt == BF16 else 4


class V:
    def __init__(self, ap, lo, hi, shape, es, off):
        self.ap, self.lo, self.hi, self.shape, self.es, self.off = ap, lo, hi, shape, es, off

    def keys(self):
        return [("s", i) for i in range(self.lo // G, (self.hi - 1) // G + 1)]

    def __getitem__(self, key):
        if not isinstance(key, tuple):
            key = (key,)
        sh = self.shape
        if len(sh) == 1:
            (k,) = key
            a, b = (k.start or 0, sh[0] if k.stop is None else k.stop) if isinstance(k, slice) else (k, k + 1)
            ap = self.ap[:, a:b]
            return V(ap, self.off + a * self.es, self.off + b * self.es, (b - a,), self.es, self.off + a * self.es)
        k0 = key[0]
        k1 = key[1] if len(key) > 1 else slice(None)
        if isinstance(k0, slice):
            i0, i1 = k0.start or 0, sh[0] if k0.stop is None else k0.stop
        else:
            i0, i1 = k0, k0 + 1
        c0, c1 = k1.start or 0, sh[1] if k1.stop is None else k1.stop
        lo = self.off + (i0 * sh[1] + c0) * self.es
        hi = self.off + ((i1 - 1) * sh[1] + c1) * self.es
        if isinstance(k0, slice):
            ap = self.ap[:, i0:i1, c0:c1]
            return V(ap, lo, hi, (i1 - i0, c1 - c0), self.es, lo) if (c0 == 0 and c1 == sh[1]) else V(ap, lo, hi, None, self.es, lo)
        ap = self.ap[:, i0, c0:c1]
        return V(ap, lo, hi, (c1 - c0,), self.es, lo)

    def p(self, p0, p1):
        key = (slice(p0, p1),) + (slice(None),) * (len(self.ap.shape) - 1)
        return V(self.ap[key], self.lo, self.hi, self.shape, self.es, self.off)

    def with_ap(self, ap):
        return V(ap, self.lo, self.hi, None, self.es, self.off)


class PV:
    def __init__(self, ap, bank):
        self.ap, self.bank = ap, bank

    def keys(self):
        return [("p", self.bank)]

    def __getitem__(self, key):
        return PV(self.ap[key], self.bank)


class Op:
    __slots__ = ("id", "eng", "fn", "deps", "dma", "marked", "sem", "val", "qidx")


class Sched:
    ENGS = ("pe", "act", "dve", "pool", "sp")
    EPOCH = 20000
    KDMA = 8

    def __init__(self):
        self.ops = []
        self.W = {}
        self.R = {}
        self.by_eng = {e: [] for e in self.ENGS}
        self.ndma = {"pool": 0, "sp": 0}
        self.out_dmas = []

    def add(self, eng, fn, reads=(), writes=(), dma=False, is_out=False):
        op = Op()
        op.id = len(self.ops)
        op.eng, op.fn, op.dma, op.marked = eng, fn, dma, False
        op.sem = op.val = None
        deps = set()
        W, R = self.W, self.R
        for r in reads:
            for k in r.keys():
                w = W.get(k)
                if w is not None:
                    o = self.ops[w]
                    if o.dma or dma or o.eng != eng or eng != "pe":
                        deps.add(w)
        for wv in writes:
            for k in wv.keys():
                w = W.get(k)
                if w is not None:
                    o = self.ops[w]
                    if o.dma or dma or o.eng != eng or eng != "pe":
                        deps.add(w)
                rd = R.get(k)
                if rd:
                    for rk, rid in rd.items():
                        o = self.ops[rid]
                        if o.dma or dma or o.eng != eng or eng != "pe":
                            deps.add(rid)
                W[k] = op.id
                R[k] = {}
        for r in reads:
            for k in r.keys():
                R.setdefault(k, {})[("d", op.id) if dma else eng] = op.id
        deps.discard(op.id)
        if dma:
            q = self.ndma[eng]
            self.ndma[eng] = q + 1
            op.qidx = q
        op.deps = deps
        self.ops.append(op)
        self.by_eng[eng].append(op)
        if is_out:
            self.out_dmas.append(op)
        return op

    def emit(self, nc, block_engines, sems, dsems):
        for op in self.ops:
            for d in op.deps:
                self.ops[d].marked = True
        for e in self.ENGS:
            cnt = 0
            for op in self.by_eng[e]:
                if op.dma:
                    op.sem = dsems[e][op.qidx % self.KDMA]
                    op.val = 16 * (op.qidx // self.KDMA + 1)
                elif op.marked:
                    ep = cnt // self.EPOCH
                    op.sem = sems[e][ep]
                    op.val = cnt % self.EPOCH + 1
                    cnt += 1
        for e in self.ENGS:
            eh = block_engines[e]
            waited = {}

            def wait(sem, val):
                key = id(sem)
                if waited.get(key, 0) < val:
                    eh.wait_ge(sem, val)
                    waited[key] = val

            for op in self.by_eng[e]:
                for d in sorted(op.deps):
                    o = self.ops[d]
                    wait(o.sem, o.val)
                if op.dma and op.qidx >= self.KDMA:
                    wait(op.sem, op.val - 16)
                ins = op.fn(eh)
                if op.dma:
                    ins.then_inc(op.sem, 16)
                elif op.marked:
                    ins.then_inc(op.sem, 1)
            if e == "sp":
                for o in self.out_dmas:
                    wait(o.sem, o.val)


class Builder:
    def __init__(self, nc, n_sub, dbg):
        self.nc = nc
        self.S = Sched()
        self.n_sub = n_sub
        self.dbg = dbg
        self.off = 0
        self.bank_rr = 0

    def alloc(self, shape, dt):
        es = dsz(dt)
        n = int(np.prod(shape))
        nbytes = (n * es + 255) // 256 * 256
        off = self.off
        self.off += nbytes
        assert self.off <= ARENA_BYTES, f"arena overflow {self.off}"
        w0 = off // 4
        ap = self.arena[:, w0:w0 + nbytes // 4]
        if dt != F32:
            ap = ap.bitcast(dt)
        ap = ap[:, 0:n]
        if len(shape) == 2:
            ap = ap.rearrange("p (a b) -> p a b", a=shape[0])
        return V(ap, off, off + n * es, tuple(shape), es, off)

    def mark(self):
        return self.off

    def release(self, m):
        self.off = m

    def bank(self, pool=None):
        pool = pool or self.rot
        b = pool[self.bank_rr % len(pool)]
        self.bank_rr += 1
        return self.banks[b]

    def mm(self, out, lhsT, rhs, start, stop):
        self.S.add("pe", lambda e: e.matmul(out.ap, lhsT=lhsT.ap, rhs=rhs.ap, start=start, stop=stop, skip_group_check=True),
                   reads=[lhsT, rhs], writes=[out])

    def tr(self, out, in_, ident):
        self.S.add("pe", lambda e: e.transpose(out.ap, in_.ap, ident.ap), reads=[in_, ident], writes=[out])

    def act(self, out, in_, func, scale=1.0, bias=None, extra_reads=()):
        kw = {}
        rd = [in_] + list(extra_reads)
        if isinstance(scale, V):
            rd.append(scale)
            sc = scale.ap
        else:
            sc = scale
        if bias is not None:
            rd.append(bias)
            kw["bias"] = bias.ap
        self.S.add("act", lambda e: e.activation(out=out.ap, in_=in_.ap, func=func, scale=sc, **kw), reads=rd, writes=[out])

    def tt(self, out, a, b, op, eng="dve"):
        self.S.add(eng, lambda e: e.tensor_tensor(out=out.ap, in0=a.ap, in1=b.ap, op=op), reads=[a, b], writes=[out])

    def ts(self, out, a, s1, op0, s2=None, op1=None, eng="dve"):
        rd = [a]
        s1a = s1.ap if isinstance(s1, V) else s1
        s2a = s2.ap if isinstance(s2, V) else s2
        if isinstance(s1, V):
            rd.append(s1)
        if isinstance(s2, V):
            rd.append(s2)
        if op1 is None:
            self.S.add(eng, lambda e: e.tensor_scalar(out=out.ap, in0=a.ap, scalar1=s1a, scalar2=None, op0=op0), reads=rd, writes=[out])
        else:
            self.S.add(eng, lambda e: e.tensor_scalar(out=out.ap, in0=a.ap, scalar1=s1a, scalar2=s2a, op0=op0, op1=op1), reads=rd, writes=[out])

    def stt(self, out, a, s, b, op0, op1):
        rd = [a, b]
        sa = s.ap if isinstance(s, V) else s
        if isinstance(s, V):
            rd.append(s)
        self.S.add("dve", lambda e: e.scalar_tensor_tensor(out=out.ap, in0=a.ap, scalar=sa, in1=b.ap, op0=op0, op1=op1), reads=rd, writes=[out])

    def copy(self, out, in_, eng="dve"):
        self.S.add(eng, lambda e: e.tensor_copy(out=out.ap, in_=in_.ap), reads=[in_], writes=[out])

    def scan(self, out, d0, d1, init, extra_reads=()):
        rd = [d0, d1] + list(extra_reads)
        ia = init.ap if isinstance(init, V) else init
        if isinstance(init, V):
            rd.append(init)
        self.S.add("dve", lambda e: e.tensor_tensor_scan(out=out.ap, data0=d0.ap, data1=d1.ap, initial=ia, op0=ALU.mult, op1=ALU.add), reads=rd, writes=[out])

    def recip(self, out, in_):
        self.S.add("dve", lambda e: e.reciprocal(out=out.ap, in_=in_.ap), reads=[in_], writes=[out])

    def memset(self, out, val, eng="dve"):
        self.S.add(eng, lambda e: e.memset(out.ap, val), writes=[out])

    def dma(self, out, in_ap, q="pool", reads=(), **kw):
        self.S.add(q, lambda e: e.dma_start(out=out.ap, in_=in_ap, **kw), reads=list(reads), writes=[out], dma=True)

    def dma_out(self, out_ap, in_):
        self.S.add("sp", lambda e: e.dma_start(out=out_ap, in_=in_.ap), reads=[in_], dma=True, is_out=True)

    def wslot(self, kc, ncols):
        s = self.ring[self.ring_i % len(self.ring)]
        self.ring_i += 1
        ap = s.ap[:, 0:kc * ncols].rearrange("p (k c) -> p k c", k=kc)
        return V(ap, s.lo, s.lo + kc * ncols * 2, (kc, ncols), 2, s.lo)

    def wload(self, w2d, kc, c0, ncols, slot=None, scol=0):
        if slot is None:
            slot = self.wslot(kc, ncols)
            dst = slot
        else:
            dst = slot[0:kc, scol:scol + ncols]
        src = w2d[:, c0:c0 + ncols].rearrange("(k p) c -> p k c", p=128)
        self.dma(dst, src, q="pool")
        return slot

    def sumsq_rstd(self, srcs, rstd):
        ps = self.bank()
        for k in range(8):
            sq = self.sqb[k % 2]
            self.act(sq, srcs[k], AF.Square)
            self.mm(ps, self.ones_bf, sq, k == 0, k == 7)
        self.act(rstd, ps, AF.Sqrt, scale=1.0 / D, bias=self.epsb)
        self.recip(rstd, rstd)

    def pre_norm(self, sub, hT, t0, ntt):
        for tt in range(ntt):
            ts_ = t0 + tt * 512
            rstd = self.rstd[tt % 2]
            self.sumsq_rstd([self.xT[k, ts_:ts_ + 512] for k in range(8)], rstd)
            for k in range(8):
                tmp = self.tmpf[k % 2]
                self.tt(tmp, self.xT[k, ts_:ts_ + 512], rstd, ALU.mult)
                self.act(hT[k, tt * 512:(tt + 1) * 512], tmp, AF.Identity, scale=self.coefA[sub][k:k + 1], bias=self.coefB[sub][k:k + 1])

    def post_norm_res(self, sub, fT, t0, ntt):
        for tt in range(ntt):
            ts_ = t0 + tt * 512
            rstd = self.rstd[tt % 2]
            self.sumsq_rstd([fT[k, tt * 512:(tt + 1) * 512] for k in range(8)], rstd)
            for k in range(8):
                tmp = self.tmpf[k % 2]
                self.tt(tmp, fT[k, tt * 512:(tt + 1) * 512], rstd, ALU.mult)
                xs = self.xT[k, ts_:ts_ + 512]
                self.stt(xs, tmp, self.coefG[sub][k:k + 1], xs, ALU.mult, ALU.add)

    def ffn(self, l, sub, fi, bg=None):
        w_in = self.w["w_ff_in"][l, fi]
        w_out = self.w["w_ff_out"][l, fi]
        for hf in range(2):
            t0 = hf * 1024
            m = self.mark()
            hT = self.alloc([8, 1024], BF16)
            gT = self.alloc([NJ, 1024], BF16)
            fT = self.alloc([8, 1024], F32)
            self.pre_norm(sub, hT, t0, 2)
            for jg in range(NJ // 2):
                slot = self.wslot(8, 512)
                self.wload(w_in, 8, jg * 256, 256, slot, 0)
                self.wload(w_in, 8, DFF + jg * 256, 256, slot, 256)
                for jj in range(2):
                    j = jg * 2 + jj
                    for tt in range(2):
                        pa = self.bank()
                        pb = self.bank()
                        for k in range(8):
                            self.mm(pa, slot[k, jj * 128:(jj + 1) * 128], hT[k, tt * 512:(tt + 1) * 512], k == 0, k == 7)
                        for k in range(8):
                            self.mm(pb, slot[k, 256 + jj * 128:256 + (jj + 1) * 128], hT[k, tt * 512:(tt + 1) * 512], k == 0, k == 7)
                        sa = self.tmpf[(j * 2 + tt) % 2]
                        self.act(sa, pa, AF.Silu)
                        self.tt(gT[j, tt * 512:(tt + 1) * 512], sa, pb, ALU.mult)
                if bg is not None:
                    next(bg, None)
            for n in range(8):
                slot = self.wload(w_out, NJ, n * 128, 128)
                for tt in range(2):
                    pf = self.bank()
                    for k in range(NJ):
                        self.mm(pf, slot[k, 0:128], gT[k, tt * 512:(tt + 1) * 512], k == 0, k == NJ - 1)
                    self.act(fT[n, tt * 512:(tt + 1) * 512], pf, AF.Identity)
            self.post_norm_res(sub, fT, t0, 2)
            self.release(m)
        if bg is not None:
            for _ in bg:
                pass

    def layer_params(self, l):
        w = self.w
        vr = self.vrows
        cA, cB, cG = self.coef_sets[l % 2]
        self.dma(vr.p(0, 72), w["b_ada"][l].rearrange("(j p) -> j p", p=128), q="sp")
        self.dma(vr.p(72, 96), w["g_pre"][l].rearrange("s (k p) -> (s k) p", p=128), q="sp")
        self.dma(vr.p(96, 120), w["g_post"][l].rearrange("s (k p) -> (s k) p", p=128), q="sp")
        ps = self.bank()
        self.tr(ps[:, 0:120], vr.p(0, 120), self.ident.p(0, 120)[0:120])
        self.copy(self.vecs[0:120], ps[:, 0:120])
        yield
        ada = self.ada
        for cg in range(18):
            slot = self.wload(w["w_ada"][l], 8, cg * 512, 512)
            for jj in range(4):
                j = cg * 4 + jj
                pa = self.bank()
                for k in range(8):
                    self.mm(pa[:, 0:2], slot[k, jj * 128:(jj + 1) * 128], self.cT2[k * 2:k * 2 + 2], k == 0, k == 7)
                self.tt(ada[j:j + 1], pa[:, 0:1], self.vecs[j:j + 1], ALU.add)
            yield
        for s in range(3):
            resw = 1.0 if s == 1 else 0.5
            self.stt(cA[s], ada[s * 24 + 8:s * 24 + 16], 1.0, self.vecs[72 + s * 8:72 + s * 8 + 8], ALU.add, ALU.mult)
            self.copy(cB[s], ada[s * 24:s * 24 + 8])
            self.ts(self.tmp8, ada[s * 24 + 16:s * 24 + 24], 1.0, ALU.add, resw, ALU.mult)
            self.tt(cG[s], self.tmp8, self.vecs[96 + s * 8:96 + s * 8 + 8], ALU.mult)

    def setup(self):
        nc = self.nc
        al = self.alloc
        self.xT = al([8, L], F32)
        self.ident = al([128], F32)
        self.ones_bf = al([128], BF16)
        self.onesf = al([128], F32)
        self.zerosf = al([512], F32)
        self.epsb = al([1], F32)
        self.one1 = al([1], F32)
        self.cT = al([8], BF16)
        self.cT2 = al([16], BF16)
        self.cTf = al([8], F32)
        self.vrows = al([128], F32)
        self.vecs = al([128], F32)
        self.ada = al([72], F32)
        self.coef_sets = []
        for _ in range(2):
            cb = al([72], F32)
            self.coef_sets.append(tuple([cb[(g * 3 + s_) * 8:(g * 3 + s_ + 1) * 8] for s_ in range(3)] for g in range(3)))
        self.coefA, self.coefB, self.coefG = self.coef_sets[0]
        self.tmp8 = al([8], F32)
        self.ring = [al([4096], BF16) for _ in range(3)]
        self.ring_i = 0
        self.sqb = [al([512], BF16) for _ in range(2)]
        self.rstd = [al([512], F32) for _ in range(2)]
        self.tmpf = [al([512], F32) for _ in range(2)]
        self.memset(self.onesf, 1.0, "pool")
        self.memset(self.zerosf, 0.0, "pool")
        self.memset(self.epsb, EPS, "pool")
        self.memset(self.one1, 1.0, "pool")
        self.memset(self.vrows, 0.0, "pool")
        self.S.add("pool", lambda e: e.affine_select(out=self.ident.ap, in_=self.onesf.ap, pattern=[[-1, 128]], compare_op=ALU.is_equal,
                                                     fill=0.0, base=0, channel_multiplier=1), reads=[self.onesf], writes=[self.ident])
        self.copy(self.ones_bf, self.onesf)
        import os
        crow = self.vrows
        if not os.environ.get("SKIPC"):
            self.dma(crow.p(0, 8), self.w["c"].rearrange("(k p) -> k p", p=128), q="sp")
            ps = self.bank()
            self.tr(ps[:, 0:8], crow.p(0, 8), self.ident.p(0, 8)[0:8])
            self.act(self.cTf, ps[:, 0:8], AF.Silu)
            self.copy(self.cT, self.cTf)
            self.copy(self.cT2.with_ap(self.cT2.ap.rearrange("p (k t) -> p k t", t=2)), self.cTf.with_ap(self.cTf.ap.unsqueeze(2).to_broadcast([128, 8, 2])))
        mk = self.mark()
        self.xrow = [self.alloc([1024], F32) for _ in range(2)]
        xin = self.w["x"]
        for tb in range(16):
            xr = self.xrow[tb % 2]
            self.dma(xr, xin[tb * 128:(tb + 1) * 128, :], q="sp")
            for k in range(8):
                ps = self.bank()
                self.tr(ps[:, 0:128], xr[k * 128:(k + 1) * 128], self.ident)
                if k % 2 == 0:
                    self.copy(self.xT[k, tb * 128:(tb + 1) * 128], ps[:, 0:128])
                else:
                    self.act(self.xT[k, tb * 128:(tb + 1) * 128], ps[:, 0:128], AF.Identity)
        self.release(mk)

    def store_out(self):
        out = self.out_ap
        self.xrow = [self.alloc([1024], F32) for _ in range(2)]
        for tb in range(16):
            xr = self.xrow[tb % 2]
            for k in range(8):
                ps = self.bank()
                self.tr(ps[:, 0:128], self.xT[k, tb * 128:(tb + 1) * 128], self.ident)
                if k % 2 == 0:
                    self.copy(xr[k * 128:(k + 1) * 128], ps[:, 0:128])
                else:
                    self.act(xr[k * 128:(k + 1) * 128], ps[:, 0:128], AF.Identity)
            self.dma_out(out[tb * 128:(tb + 1) * 128, :], xr)

    def bc3(self, v, n, m):
        return v.with_ap(v.ap.unsqueeze(2).to_broadcast([128, n, m]))

    def v3(self, v, b):
        if isinstance(v, PV):
            return PV(v.ap.rearrange("p (a b) -> p a b", b=b), v.bank)
        return v.with_ap(v.ap.rearrange("p (a b) -> p a b", b=b))

    def tb3(self, v, n, b):
        return v.with_ap(v.ap.unsqueeze(1).to_broadcast([128, n, b]))

    def pool_op(self, fn, reads, writes):
        self.S.add("pool", fn, reads=reads, writes=writes)

    def mixer_consts(self, l):
        al = self.alloc
        w = self.w
        mc = {}
        mc["ident_bf"] = al([128], BF16)
        self.copy(mc["ident_bf"], self.ident)
        mc["tri"] = al([128], BF16)
        mc["negones"] = al([128], BF16)
        negf = self.tmpf[0][0:128]
        trif = self.tmpf[1][0:128]
        self.memset(negf, -1.0)
        self.copy(mc["negones"], negf)
        self.pool_op(lambda e: e.affine_select(out=trif.ap, in_=negf.ap, pattern=[[-1, 128]], compare_op=ALU.is_ge, fill=0.0, base=0,
                                               channel_multiplier=1), [negf], [trif])
        self.copy(mc["tri"], trif)
        vr = self.vrows
        self.dma(vr.p(0, 8), w["lam_re"][l].rearrange("(t g) p -> t (g p)", g=2), q="sp")
        self.dma(vr.p(8, 16), w["lam_im"][l].rearrange("(t g) p -> t (g p)", g=2), q="sp")
        self.dma(vr.p(16, 22), w["conv_w"][l].rearrange("k (c p) -> (k c) p", p=128), q="sp")
        self.dma(vr.p(22, 24), w["ssm_d"][l].rearrange("(c p) -> c p", p=128), q="sp")
        self.dma(vr.p(24, 26), w["pool_scale"][l].rearrange("(c p) -> c p", p=128), q="sp")
        ps = self.bank()
        self.tr(ps[:, 0:26], vr.p(0, 26), self.ident.p(0, 26)[0:26])
        mc["vecs2"] = al([32], F32)
        self.copy(mc["vecs2"][0:26], ps[:, 0:26])
        return mc

    def sincos(self, dsin, dcos, y, yi, ta, tb):
        self.copy(yi, y)
        self.copy(ta, yi)
        self.tt(tb, y, ta, ALU.subtract)
        self.act(dsin, tb, AF.Sin, scale=6.283185)
        self.ts(tb, y, 0.25, ALU.add)
        self.copy(yi, tb)
        self.copy(ta, yi)
        self.tt(tb, tb, ta, ALU.subtract)
        self.act(dcos, tb, AF.Sin, scale=6.283185)

    def ssm_branch(self, l, hT, ub, mc):
        w = self.w
        al = self.alloc
        MUL, ADD, SUB = ALU.mult, ALU.add, ALU.subtract
        m = self.mark()
        v2 = mc["vecs2"]
        lrT, liT = v2[0:8], v2[8:16]
        dcol = v2[22:24]
        sp = al([20 * 8], F32)
        c8 = [sp[i * 8:(i + 1) * 8] for i in range(20)]
        lr, dt, a_, th, mag, thn, ta8, tb8, sn, cs, abr, abi, den, nr, fre, fim, t8, nEi, inre, inim = c8
        yi8 = al([8], I32)
        NT = TC + 1
        lb = al([16, 128], BF16)
        lc = al([16, 128], BF16)
        sinT = al([8, NT], F32)
        cosT = al([8, NT], F32)
        ms_ = self.mark()
        rows = al([272], F32)
        E0 = rows[0:128].p(0, 1)
        E1 = rows[128:256].p(0, 1)
        ld = rows[256:272].p(0, 1)
        self.memset(rows.p(0, 1), 0.0)
        self.memset(rows[0:64].p(0, 1), 1.0)
        self.memset(rows[192:256].p(0, 1), 1.0)
        self.dma(ld, w["log_dt"][l].unsqueeze(0), q="sp")
        ps = self.bank()
        self.mm(ps[:, 0:8], E0, ld.with_ap(ld.ap[:, 0:16:2]), True, False)
        self.mm(ps[:, 0:8], E1, ld.with_ap(ld.ap[:, 1:16:2]), False, True)
        self.act(dt, ps[:, 0:8], AF.Exp)
        self.ts(lr, lrT, -1e-4, ALU.min)
        self.tt(a_, lr, dt, MUL)
        self.tt(th, liT, dt, MUL)
        self.act(mag, a_, AF.Exp)
        self.ts(thn, th, 1.0 / (2 * math.pi), MUL)
        self.sincos(sn, cs, thn, yi8, ta8, tb8)
        self.tt(abr, mag, cs, MUL)
        self.tt(abi, mag, sn, MUL)
        self.tt(den, lr, lr, MUL)
        self.tt(t8, liT, liT, MUL)
        self.tt(den, den, t8, ADD)
        self.recip(den, den)
        self.ts(nr, abr, -1.0, ADD)
        self.tt(fre, nr, lr, MUL)
        self.tt(t8, abi, liT, MUL)
        self.tt(fre, fre, t8, ADD)
        self.tt(fre, fre, den, MUL)
        self.tt(fim, abi, lr, MUL)
        self.tt(t8, nr, liT, MUL)
        self.tt(fim, fim, t8, SUB)
        self.tt(fim, fim, den, MUL)
        if int(os.environ.get("SSMSTOP", "9")) <= 1:
            self.memset(ub, 0.0)
            self.release(m)
            return
        pidx = al([1], I32)
        pj = al([1], I32)
        pf = al([1], F32)
        rm = al([4], F32)
        par = al([2], F32)
        self.pool_op(lambda e: e.iota(pidx.ap, pattern=[[0, 1]], base=0, channel_multiplier=1), [], [pidx])
        self.S.add("dve", lambda e: e.tensor_scalar(out=pj.ap, in0=pidx.ap, scalar1=5, scalar2=None, op0=ALU.logical_shift_right), reads=[pidx], writes=[pj])
        self.copy(pf, pj)
        for q in range(4):
            self.ts(rm[q:q + 1], pf, float(q), ALU.is_equal)
        self.S.add("dve", lambda e: e.tensor_scalar(out=pj.ap, in0=pidx.ap, scalar1=4, scalar2=1, op0=ALU.logical_shift_right, op1=ALU.bitwise_and),
                   reads=[pidx], writes=[pj])
        self.copy(par[1:2], pj)
        self.ts(par[0:1], par[1:2], -1.0, MUL, 1.0, ADD)
        Bre = al([8, 32], F32)
        Bim = al([8, 32], F32)
        bbr = al([8, 32], F32)
        bbi = al([8, 32], F32)
        tb1 = al([8, 32], F32)
        for Bt, nm in ((Bre, "ssm_b_re"), (Bim, "ssm_b_im")):
            self.memset(Bt, 0.0)
            src = w[nm][l].rearrange("(t g) p h -> g p t h", g=2)
            self.dma(Bt.with_ap(Bt.ap[0:64, :, 0:16]), src[0], q="sp")
            self.dma(Bt.with_ap(Bt.ap[64:128, :, 16:32]), src[1], q="sp")
        fr3 = self.bc3(fre, 8, 32)
        fi3 = self.bc3(fim, 8, 32)
        self.tt(bbr, Bre, fr3, MUL)
        self.tt(tb1, Bim, fi3, MUL)
        self.tt(bbr, bbr, tb1, SUB)
        self.tt(bbi, Bim, fr3, MUL)
        self.tt(tb1, Bre, fi3, MUL)
        self.tt(bbi, bbi, tb1, ADD)
        for ri, bb in enumerate((bbr, bbi)):
            for c in range(2):
                ps = self.bank()
                src = bb.with_ap(bb.ap[:, 4 * c:4 * c + 4, :].rearrange("p a b -> p (a b)"))
                self.tr(ps[:, 0:128], src, self.ident)
                for q in range(4):
                    self.ts(lb[ri * 8 + 4 * c + q], ps[:, 0:128], rm[q:q + 1], MUL)
        if int(os.environ.get("SSMSTOP", "9")) <= 2:
            self.memset(ub, 0.0)
            self.release(m)
            return
        self.memset(lc, 0.0)
        InC = al([128], F32)
        for ri, nm in enumerate(("ssm_c_re", "ssm_c_im")):
            for c in range(2):
                src = w[nm][l].rearrange("g h p -> (g h) p")[c * 128:(c + 1) * 128, :]
                self.dma(InC[0:64], src, q="sp")
                self.dma(InC[64:128], src, q="sp")
                self.ts(InC[0:64], InC[0:64], par[0:1], MUL)
                self.ts(InC[64:128], InC[64:128], par[1:2], MUL)
                ps = self.bank()
                self.tr(ps[:, 0:128], InC, self.ident)
                for q in range(4):
                    dst = lc[ri * 8 + 4 * c + q][32 * q:32 * q + 32]
                    if ri == 0:
                        self.copy(dst, ps[:, 32 * q:32 * q + 32])
                    else:
                        self.ts(dst, ps[:, 32 * q:32 * q + 32], -1.0, MUL)
        if int(os.environ.get("SSMSTOP", "9")) <= 3:
            self.memset(ub, 0.0)
            self.release(m)
            return
        iot_i = al([NT], I32)
        iot_f = al([NT], F32)
        self.pool_op(lambda e: e.iota(iot_i.ap, pattern=[[1, NT]], base=0, channel_multiplier=0), [], [iot_i])
        self.copy(iot_f, iot_i)
        mt = self.mark()
        yt = al([4, NT], F32)
        yi = al([4, NT], I32)
        ta = al([4, NT], F32)
        tb = al([4, NT], F32)
        for h4 in range(2):
            for T in range(4):
                self.ts(yt[T], iot_f, thn[4 * h4 + T:4 * h4 + T + 1], MUL)
            self.sincos(sinT[4 * h4:4 * h4 + 4], cosT[4 * h4:4 * h4 + 4], yt, yi, ta, tb)
        self.release(mt)
        for T in range(8):
            self.ts(nEi[T:T + 1], sinT[T, TC:TC + 1], -1.0, MUL)
        if int(os.environ.get("SSMSTOP", "9")) <= 4:
            self.memset(ub, 0.0)
            self.release(m)
            return
        self.release(ms_)
        uS = al([1024], F32)
        uSb = al([1024], BF16)
        sre = al([1024], F32)
        sim = al([1024], F32)
        sbre = al([1024], BF16)
        sbim = al([1024], BF16)
        sre2 = al([1024], F32)
        sim2 = al([1024], F32)
        tA = al([512], F32)
        tB = al([512], F32)
        yv = tA
        y2 = tB
        t1c = al([1], F32)
        wsl = self.wload(w["w_in"][l], 8, 768, 256)
        ybanks = [self.banks[6], self.banks[7]]
        rot = [0, 1, 2, 3, 4, 5]
        nb = 512 // TC
        NCH = 1024 // TC
        for c in range(2):
            for hf in range(2):
                t0 = hf * 1024
                for tt in range(2):
                    lt = slice(tt * 512, (tt + 1) * 512)
                    ps = self.bank(rot)
                    for k in range(8):
                        self.mm(ps, wsl[k, c * 128:(c + 1) * 128], hT[k, t0 + tt * 512:t0 + (tt + 1) * 512], k == 0, k == 7)
                    self.act(uS[lt], ps, AF.Identity)
                    self.copy(uSb[lt], ps)
                for q in range(4):
                    T = 4 * c + q
                    cb = self.tb3(cosT[T, 0:TC], nb, TC)
                    sb_ = self.tb3(sinT[T, 0:TC], nb, TC)
                    Er = cosT[T, TC:TC + 1]
                    Ei = sinT[T, TC:TC + 1]
                    magb = mag[T:T + 1].with_ap(mag[T:T + 1].ap.to_broadcast([128, TC]))
                    tA3, tB3 = self.v3(tA, TC), self.v3(tB, TC)
                    for tt in range(2):
                        lt = slice(tt * 512, (tt + 1) * 512)
                        pr = self.bank(rot)
                        pi = self.bank(rot)
                        self.mm(pr, lb[T], uSb[lt], True, True)
                        self.mm(pi, lb[8 + T], uSb[lt], True, True)
                        self.act(sre[lt], pr, AF.Identity)
                        self.act(sim[lt], pi, AF.Identity)
                        r3, i3 = self.v3(sre[lt], TC), self.v3(sim[lt], TC)
                        self.tt(tA3, r3, cb, MUL)
                        self.tt(tB3, i3, sb_, MUL)
                        self.tt(sre2[lt], tA, tB, ADD)
                        self.tt(tA3, i3, cb, MUL)
                        self.tt(tB3, r3, sb_, MUL)
                        self.tt(sim2[lt], tA, tB, SUB)
                    for ch in range(NCH):
                        gch = hf * NCH + ch
                        seg = slice(ch * TC, (ch + 1) * TC)
                        self.scan(sre[seg], magb, sre2[seg], 0.0 if gch == 0 else inre[T:T + 1])
                        self.scan(sim[seg], magb, sim2[seg], 0.0 if gch == 0 else inim[T:T + 1])
                        if gch < 2 * NCH - 1:
                            lre = sre[(ch + 1) * TC - 1:(ch + 1) * TC]
                            lim = sim[(ch + 1) * TC - 1:(ch + 1) * TC]
                            self.ts(t1c, lre, Er, MUL)
                            self.ts(inre[T:T + 1], lim, nEi[T:T + 1], MUL, t1c, ADD)
                            self.ts(t1c, lim, Er, MUL)
                            self.ts(inim[T:T + 1], lre, Ei, MUL, t1c, ADD)
                    for tt in range(2):
                        lt = slice(tt * 512, (tt + 1) * 512)
                        sr3, si3 = self.v3(sre[lt], TC), self.v3(sim[lt], TC)
                        self.tt(tA3, sr3, cb, MUL)
                        self.tt(tB3, si3, sb_, MUL)
                        self.tt(sbre[lt], tA, tB, SUB)
                        self.tt(tA3, si3, cb, MUL)
                        self.tt(tB3, sr3, sb_, MUL)
                        self.tt(sbim[lt], tA, tB, ADD)
                        self.mm(ybanks[tt], lc[T], sbre[lt], q == 0, False)
                        self.mm(ybanks[tt], lc[8 + T], sbim[lt], False, q == 3)
                for tt in range(2):
                    lt = slice(tt * 512, (tt + 1) * 512)
                    self.ts(yv, uS[lt], dcol[c:c + 1], MUL)
                    self.tt(yv, yv, ybanks[tt], ADD)
                    self.tt(y2, yv, yv, MUL)
                    self.ts(y2, y2, 1.5957691216 * 0.044715, MUL, 1.5957691216, ADD)
                    self.tt(y2, y2, yv, MUL)
                    self.act(y2, y2, AF.Sigmoid)
                    self.tt(ub[c, t0 + tt * 512:t0 + (tt + 1) * 512], yv, y2, MUL)
        self.release(m)

    def conv_branch(self, l, hT, ua, mc):
        al = self.alloc
        win = self.w["w_in"][l]
        MUL, ADD = ALU.mult, ALU.add
        m = self.mark()
        v2 = mc["vecs2"]
        upad = al([2 + L], F32)
        csb = al([512], F32)
        yt = al([512], F32)
        s1 = self.wload(win, 8, 0, 512)
        s2 = self.wload(win, 8, 512, 256)
        for i in range(2):
            self.memset(upad[0:2], 0.0)
            for tt in range(4):
                tok = slice(tt * 512, (tt + 1) * 512)
                pc = self.bank()
                pv = self.bank()
                pb = self.bank()
                for k in range(8):
                    self.mm(pc, s1[k, 256 + i * 128:256 + (i + 1) * 128], hT[k, tok], k == 0, k == 7)
                for k in range(8):
                    self.mm(pv, s2[k, i * 128:(i + 1) * 128], hT[k, tok], k == 0, k == 7)
                for k in range(8):
                    self.mm(pb, s1[k, i * 128:(i + 1) * 128], hT[k, tok], k == 0, k == 7)
                self.act(csb, pc, AF.Identity)
                b0 = tt * 512
                self.tt(upad[2 + b0:2 + b0 + 512], csb, pv, MUL)
                self.ts(yt, upad[b0:b0 + 512], v2[16 + i:17 + i], MUL)
                self.stt(yt, upad[1 + b0:1 + b0 + 512], v2[18 + i:19 + i], yt, MUL, ADD)
                self.stt(yt, upad[2 + b0:2 + b0 + 512], v2[20 + i:21 + i], yt, MUL, ADD)
                self.tt(ua[i, tok], yt, pb, MUL)
        self.release(m)

    def pool_branch(self, l, hT, uc, mc):
        al = self.alloc
        w = self.w
        win = w["w_in"][l]
        MUL, ADD, SUB = ALU.mult, ALU.add, ALU.subtract
        m = self.mark()
        v2 = mc["vecs2"]
        uP = al([L], F32)
        csp = al([16 + L], F32)
        dP = al([L], F32)
        pl = al([L], BF16)
        lp = al([2, 128], BF16)
        invc = al([2, 16], F32)
        t16 = al([16], F32)
        io_i = al([16], I32)
        io_f = al([16], F32)
        self.pool_op(lambda e: e.iota(io_i.ap, pattern=[[1, 16]], base=1, channel_multiplier=0), [], [io_i])
        self.copy(io_f, io_i)
        for c in range(2):
            for h2 in range(2):
                self.ts(invc[c].p(64 * h2, 64 * h2 + 64), io_f.p(64 * h2, 64 * h2 + 64), float(POOL_W[2 * c + h2]), ALU.min)
        self.recip(invc, invc)
        self.memset(lp, 0.0)
        for c in range(2):
            for h2 in range(2):
                self.dma(lp[c].p(64 * h2, 64 * h2 + 64)[64 * h2:64 * h2 + 64], w["w_pool"][l, 2 * c + h2], q="pool")
        sl = self.wload(win, 8, 1024, 256)
        onesb = self.onesf[0:1].with_ap(self.onesf[0:1].ap.to_broadcast([128, 512]))
        self.memset(csp[0:16], 0.0)
        for c in range(2):
            for tt in range(4):
                tok = slice(tt * 512, (tt + 1) * 512)
                ps = self.bank()
                for k in range(8):
                    self.mm(ps, sl[k, c * 128:(c + 1) * 128], hT[k, tok], k == 0, k == 7)
                self.act(uP[tok], ps, AF.Identity)
                self.scan(csp[16 + tt * 512:16 + (tt + 1) * 512], onesb, uP[tok], csp[15 + tt * 512:16 + tt * 512])
            for h2 in range(2):
                w_ = POOL_W[2 * c + h2]
                P = (64 * h2, 64 * h2 + 64)
                self.tt(dP.p(*P), csp[16:16 + L].p(*P), csp[16 - w_:16 - w_ + L].p(*P), SUB)
                self.stt(pl.p(*P), dP.p(*P), 1.0 / w_, uP.p(*P), MUL, SUB)
            self.tt(t16, dP[0:16], invc[c], MUL)
            self.tt(pl[0:16], t16, uP[0:16], SUB)
            for tt in range(4):
                tok = slice(tt * 512, (tt + 1) * 512)
                ps = self.bank()
                self.mm(ps, lp[c], pl[tok], True, True)
                self.act(uc[c, tok], ps, AF.Identity, scale=v2[24 + c:25 + c])
        self.release(m)

    def attn_branch(self, l, hT, ud, mc):
        al = self.alloc
        win = self.w["w_in"][l]
        m = self.mark()
        qT = al([L], BF16)
        kT = al([L], BF16)
        vS = al([16, 128], BF16)
        e_ = al([512], F32)
        spb = [al([512], BF16) for _ in range(2)]
        At = [al([512], F32) for _ in range(2)]
        Gts = [al([512], F32) for _ in range(2)]
        Ab = [al([512], BF16) for _ in range(2)]
        maskb = []
        for i in range(4):
            mb = al([512], BF16)
            tmp = self.rstd[i % 2]
            self.pool_op((lambda tmp, i: lambda e: e.affine_select(out=tmp.ap, in_=self.zerosf.ap, pattern=[[1, 512]], compare_op=ALU.is_gt,
                                                                  fill=-30000.0, base=-128 * i, channel_multiplier=-1))(tmp, i),
                         [self.zerosf], [tmp])
            self.copy(mb, tmp)
            maskb.append(mb)
        mc = dict(mc, maskb=maskb)
        rot = [0, 1, 2, 3]
        ui = 0
        for c in range(2):
            sq = self.wload(win, 8, 1280 + c * 128, 128)
            sk = self.wload(win, 8, 1536 + c * 128, 128)
            sv = self.wload(win, 8, 1792 + c * 128, 128)
            for tt in range(4):
                tok = slice(tt * 512, (tt + 1) * 512)
                ps = self.bank(rot)
                for k in range(8):
                    self.mm(ps, sq[k, 0:128], hT[k, tok], k == 0, k == 7)
                self.act(qT[tok], ps, AF.Identity, scale=0.125)
                ps = self.bank(rot)
                for k in range(8):
                    self.mm(ps, sk[k, 0:128], hT[k, tok], k == 0, k == 7)
                self.copy(kT[tok], ps)
            for tb in range(16):
                ps = self.bank(rot)
                for k in range(8):
                    self.mm(ps[:, 0:128], hT[k, tb * 128:(tb + 1) * 128], sv[k, 0:128], k == 0, k == 7)
                if tb % 2 == 0:
                    self.copy(vS[tb], ps[:, 0:128])
                else:
                    self.act(vS[tb], ps[:, 0:128], AF.Identity)
            units = []
            for hh in range(2):
                P = (64 * hh, 64 * hh + 64)
                for qt in range(4):
                    chain = hh * 4 + qt
                    Ob, Rb = (self.banks[4], self.banks[5]) if chain % 2 == 0 else (self.banks[6], self.banks[7])
                    ObP = PV(Ob.ap[P[0]:P[1], :], Ob.bank)
                    nkb = 4 * (qt + 1)
                    qv = qT[qt * 512:(qt + 1) * 512].p(*P)
                    for kb in range(nkb - 1, -1, -1):
                        units.append(dict(hh=hh, qt=qt, kb=kb, first=(kb == nkb - 1), last=(kb == 0), Rb=Rb, ObP=ObP, qv=qv, P=P,
                                          diag=(kb >= 4 * qt)))

            def stageA0(u, i):
                zb = self.bank(rot)
                u["zb"] = zb
                kb, P = u["kb"], u["P"]
                self.mm(zb, kT[kb * 128:(kb + 1) * 128].p(*P), u["qv"], True, not u["diag"])
                if u["diag"]:
                    self.mm(zb, mc["ident_bf"], mc["maskb"][kb - 4 * u["qt"]], False, True)

            def stageA1(u, i):
                zb = u["zb"]
                sp_ = spb[i % 2]
                G = Gts[i % 2]
                u["G"] = G
                kb = u["kb"]
                self.act(e_, zb, AF.Exp)
                self.act(sp_, e_, AF.Ln, bias=self.one1)
                if not u["first"]:
                    self.act(G, u["Rb"], AF.Exp)
                self.mm(zb, mc["tri"], sp_, False, True)
                if kb > 0:
                    self.mm(u["Rb"], mc["negones"], sp_, u["first"], True)

            def stageB(u, i):
                A = Ab[i % 2]
                at = At[i % 2]
                if u["first"]:
                    self.act(A, u["zb"], AF.Exp)
                else:
                    self.act(at, u["zb"], AF.Exp)
                    self.tt(A, at, u["G"], ALU.mult)
                hh, kb = u["hh"], u["kb"]
                self.mm(u["ObP"], vS[kb, 64 * hh:64 * hh + 64], A, u["first"], u["last"])
                if u["last"]:
                    qt = u["qt"]
                    self.act(ud[c, qt * 512:(qt + 1) * 512].p(*u["P"]), u["ObP"], AF.Identity)

            nu = len(units)
            stageA0(units[0], 0)
            for i in range(nu):
                if i + 1 < nu:
                    stageA0(units[i + 1], i + 1)
                stageA1(units[i], i)
                if i >= 1:
                    stageB(units[i - 1], i - 1)
            stageB(units[-1], nu - 1)
        self.release(m)

    def merge_out(self, l, hT, us):
        al = self.alloc
        w = self.w
        win = w["w_in"][l]
        MUL, ADD = ALU.mult, ALU.add
        ynames = ["w_conv_out", None, "w_pool_out", "w_sb_out"]
        rot = [0, 1, 2, 3, 4, 5, 6]
        pss = self.banks[7]
        for hf in range(2):
            t0 = hf * 1024
            m = self.mark()
            mg = al([8, 1024], BF16)
            acc = al([1024], F32)
            sg = al([512], F32)
            sg2 = al([512], F32)
            yt = al([512], F32)
            pr = al([512], F32)
            for n in range(8):
                for b in range(4):
                    sgw = self.wload(win, 8, 2048 + b * 1024 + n * 128, 128)
                    if b == 1:
                        sy = self.wslot(2, 256)
                        self.wload(w["w_glu"][l], 2, n * 128, 128, sy, 0)
                        self.wload(w["w_glu"][l], 2, 1024 + n * 128, 128, sy, 128)
                    else:
                        sy = self.wload(w[ynames[b]][l], 2, n * 128, 128)
                    u = us[b]
                    for tt in range(2):
                        tok = slice(t0 + tt * 512, t0 + (tt + 1) * 512)
                        lt = slice(tt * 512, (tt + 1) * 512)
                        pg = self.bank(rot)
                        for k in range(8):
                            self.mm(pg, sgw[k, 0:128], hT[k, tok], k == 0, k == 7)
                        py = self.bank(rot)
                        for k in range(2):
                            self.mm(py, sy[k, 0:128], u[k, tok], k == 0, k == 1)
                        self.act(sg, pg, AF.Sigmoid)
                        dst = acc[lt] if b == 0 else pr
                        if b == 1:
                            pyg = self.bank(rot)
                            for k in range(2):
                                self.mm(pyg, sy[k, 128:256], u[k, tok], k == 0, k == 1)
                            self.act(sg2, pyg, AF.Sigmoid)
                            self.tt(yt, sg2, py, MUL)
                            self.tt(dst, sg, yt, MUL)
                        else:
                            self.tt(dst, sg, py, MUL)
                        if b in (1, 2):
                            self.tt(acc[lt], acc[lt], pr, ADD)
                        elif b == 3:
                            self.tt(mg[n, lt], acc[lt], pr, ADD)
            so = [self.wload(w["w_out"][l], 8, 0, 512), self.wload(w["w_out"][l], 8, 512, 512)]
            for tt in range(2):
                lt = slice(tt * 512, (tt + 1) * 512)
                ts_ = t0 + tt * 512
                rstd = self.rstd[tt % 2]
                for n in range(8):
                    ps = self.bank(rot)
                    for k in range(8):
                        self.mm(ps, so[n // 4][k, (n % 4) * 128:(n % 4 + 1) * 128], mg[k, lt], k == 0, k == 7)
                    sq = self.sqb[n % 2]
                    self.act(sq, ps, AF.Square)
                    self.mm(pss, self.ones_bf, sq, n == 0, n == 7)
                self.act(rstd, pss, AF.Sqrt, scale=1.0 / D, bias=self.epsb)
                self.recip(rstd, rstd)
                for n in range(8):
                    ps = self.bank(rot)
                    for k in range(8):
                        self.mm(ps, so[n // 4][k, (n % 4) * 128:(n % 4 + 1) * 128], mg[k, lt], k == 0, k == 7)
                    tmp = self.tmpf[n % 2]
                    self.tt(tmp, ps, rstd, MUL)
                    xs = self.xT[n, ts_:ts_ + 512]
                    self.stt(xs, tmp, self.coefG[1][n:n + 1], xs, MUL, ADD)
            self.release(m)

    def dump_u(self, i, u):
        if self.dbg:
            dst = self.dbg_ap[i]
            self.S.add("pool", lambda e: e.dma_start(out=dst, in_=u.ap), reads=[u], dma=True, is_out=True)

    def mixer(self, l):
        al = self.alloc
        m0 = self.mark()
        hT = al([8, L], BF16)
        self.pre_norm(1, hT, 0, 4)
        mc = self.mixer_consts(l)
        br = os.environ.get("BR", "scpam")
        ub = al([2, L], BF16)
        if "s" in br:
            self.ssm_branch(l, hT, ub, mc)
        else:
            self.memset(ub, 0.0)
        ua = al([2, L], BF16)
        if "c" in br:
            self.conv_branch(l, hT, ua, mc)
        else:
            self.memset(ua, 0.0)
        uc = al([2, L], BF16)
        if "p" in br:
            self.pool_branch(l, hT, uc, mc)
        else:
            self.memset(uc, 0.0)
        ud = al([2, L], BF16)
        if "a" in br:
            self.attn_branch(l, hT, ud, mc)
        else:
            self.memset(ud, 0.0)
        for i, u in enumerate((ua, ub, uc, ud)):
            self.dump_u(i, u)
        if "m" in br:
            self.merge_out(l, hT, (ua, ub, uc, ud))
        self.release(m0)


    def build(self):
        nc = self.nc
        w = {}
        shapes = dict(x=[L, D], c=[D], w_ada=[DEPTH, D, 9 * D], b_ada=[DEPTH, 9 * D], g_pre=[DEPTH, 3, D], g_post=[DEPTH, 3, D],
                      w_ff_in=[DEPTH, 2, D, 2 * DFF], w_ff_out=[DEPTH, 2, DFF, D], w_in=[DEPTH, D, 6144], conv_w=[DEPTH, 3, 256],
                      w_conv_out=[DEPTH, 256, D], lam_re=[DEPTH, 16, 64], lam_im=[DEPTH, 16, 64], log_dt=[DEPTH, 16],
                      ssm_b_re=[DEPTH, 16, 64, 16], ssm_b_im=[DEPTH, 16, 64, 16], ssm_c_re=[DEPTH, 16, 16, 64], ssm_c_im=[DEPTH, 16, 16, 64],
                      ssm_d=[DEPTH, 256], w_glu=[DEPTH, 256, 2 * D], w_pool=[DEPTH, 4, 64, 64], pool_scale=[DEPTH, 256],
                      w_pool_out=[DEPTH, 256, D], w_sb_out=[DEPTH, 256, D], w_out=[DEPTH, D, D])
        import os
        for k, s in shapes.items():
            if os.environ.get("ONLYX") and k not in ("x", "c"):
                continue
            w[k] = nc.dram_tensor(k, s, F32, kind="ExternalInput").ap()
        self.w = w
        self.out_ap = nc.dram_tensor("out", [L, D], F32, kind="ExternalOutput").ap()
        if self.dbg:
            self.dbg_ap = nc.dram_tensor("dbg_u", [4, 128, 2, L], F32, kind="ExternalOutput").ap()
        from contextlib import ExitStack
        with ExitStack() as es:
            arena = es.enter_context(nc.sbuf_tensor("arena", [128, ARENA_BYTES // 4], F32))
            self.arena = arena[:]
            self.banks = []
            for b in range(8):
                pt = es.enter_context(nc.psum_tensor(f"ps{b}", [128, 512], F32))
                self.banks.append(PV(pt[:], b))
            self.rot = list(range(8))
            sems = {e: [es.enter_context(nc.semaphore(f"s_{e}_{i}")) for i in range(4)] for e in ("pe", "act", "dve", "pool")}
            sems["sp"] = sems["pool"]
            dsems = {q: [es.enter_context(nc.semaphore(f"d_{q}_{i}")) for i in range(Sched.KDMA)] for q in ("pool", "sp")}
            m0 = self.mark()
            self.xrow = None
            self.xT_only = True
            self.setup_alloc_xrow = True
            self._setup_with_xrow()
            done = 0
            for l in range(DEPTH):
                if done >= self.n_sub:
                    break
                self.coefA, self.coefB, self.coefG = self.coef_sets[l % 2]
                if l == 0:
                    for _ in self.layer_params(l):
                        pass
                self.ffn(l, 0, 0)
                done += 1
                if done >= self.n_sub:
                    break
                self.mixer(l)
                done += 1
                if done >= self.n_sub:
                    break
                self.ffn(l, 2, 1, bg=self.layer_params(l + 1) if l + 1 < DEPTH else None)
                done += 1
            self.store_out()
            block = es.enter_context(nc.Block())
            S = self.S

            @block.tensor
            def _(e):
                S._cur = e
                S.emit_one(nc, "pe", e, sems, dsems)

            @block.scalar
            def _(e):
                S.emit_one(nc, "act", e, sems, dsems)

            @block.vector
            def _(e):
                S.emit_one(nc, "dve", e, sems, dsems)

            @block.gpsimd
            def _(e):
                S.emit_one(nc, "pool", e, sems, dsems)

            @block.sync
            def _(e):
                S.emit_one(nc, "sp", e, sems, dsems)
        return nc

    def _setup_with_xrow(self):
        orig_alloc = self.alloc
        self.xrow = None
        self.setup()


def _emit_one(self, nc, e, eh, sems, dsems):
    if not getattr(self, "_prepared", False):
        for op in self.ops:
            for d in op.deps:
                self.ops[d].marked = True
        for en in self.ENGS:
            cnt = 0
            for op in self.by_eng[en]:
                if op.dma:
                    op.sem = dsems[en][op.qidx % self.KDMA]
                    op.val = 16 * (op.qidx // self.KDMA + 1)
                elif op.marked:
                    ep = cnt // self.EPOCH
                    op.sem = sems[en][ep] if en != "sp" else None
                    op.val = cnt % self.EPOCH + 1
                    cnt += 1
        self._prepared = True
    waited = {}

    def wait(sem, val):
        key = id(sem)
        if waited.get(key, 0) < val:
            eh.wait_ge(sem, val)
            waited[key] = val

    for op in self.by_eng[e]:
        for d in sorted(op.deps):
            o = self.ops[d]
            wait(o.sem, o.val)
        if op.dma and op.qidx >= self.KDMA:
            wait(op.sem, op.val - 16)
        ins = op.fn(eh)
        if op.dma:
            ins.then_inc(op.sem, 16)
        elif op.marked:
            ins.then_inc(op.sem, 1)
    if e == "sp":
        for o in self.out_dmas:
            wait(o.sem, o.val)


Sched.emit_one = _emit_one

_CACHE = {}


def build_nc(n_sub=12, dbg=False):
    nc = bass.Bass("TRN2", target_bir_lowering=False)
    b = Builder(nc, n_sub, dbg)
    b.build()
    return nc


def kernel(n_sub=12, **inputs):
    x = np.ascontiguousarray(inputs["x"], dtype=np.float32)
    c = np.ascontiguousarray(inputs["c"], dtype=np.float32)
    nc = build_nc(n_sub)
    shared = {k: np.ascontiguousarray(v, dtype=np.float32) for k, v in inputs.items() if k not in ("x", "c")}
    in_maps = []
    for b in range(8):
        m = dict(shared)
        m["x"] = x[b]
        m["c"] = c[b]
        in_maps.append(m)
    res = run_bass_kernel_spmd(nc, in_maps, core_ids=list(range(8)))
    out = np.stack([res.results[b]["out"] for b in range(8)], axis=0)
    return out.astype(np.float32)
```

```python
import math
import os
import numpy as np
import concourse.bass as bass
import concourse.mybir as mybir
from concourse.bass_utils import run_bass_kernel_spmd

F32 = mybir.dt.float32
BF16 = mybir.dt.bfloat16
I32 = mybir.dt.int32
AF = mybir.ActivationFunctionType
ALU = mybir.AluOpType

D = 1024
L = 2048
DEPTH = 4
DFF = 2816
NJ = DFF // 128
EPS = 1e-6
G = 256
ARENA_BYTES = 204 * 1024
POOL_W = (2, 4, 8, 16)
TC = 256


def dsz(dt):
    return 2 if dt == BF16 else 4


class V:
    def __init__(self, ap, lo, hi, shape, es, off):
        self.ap, self.lo, self.hi, self.shape, self.es, self.off = ap, lo, hi, shape, es, off

    def keys(self):
        return [("s", i) for i in range(self.lo // G, (self.hi - 1) // G + 1)]

    def __getitem__(self, key):
        if not isinstance(key, tuple):
            key = (key,)
        sh = self.shape
        if len(sh) == 1:
            (k,) = key
            a, b = (k.start or 0, sh[0] if k.stop is None else k.stop) if isinstance(k, slice) else (k, k + 1)
            ap = self.ap[:, a:b]
            return V(ap, self.off + a * self.es, self.off + b * self.es, (b - a,), self.es, self.off + a * self.es)
        k0 = key[0]
        k1 = key[1] if len(key) > 1 else slice(None)
        if isinstance(k0, slice):
            i0, i1 = k0.start or 0, sh[0] if k0.stop is None else k0.stop
        else:
            i0, i1 = k0, k0 + 1
        c0, c1 = k1.start or 0, sh[1] if k1.stop is None else k1.stop
        lo = self.off + (i0 * sh[1] + c0) * self.es
        hi = self.off + ((i1 - 1) * sh[1] + c1) * self.es
        if isinstance(k0, slice):
            ap = self.ap[:, i0:i1, c0:c1]
            return V(ap, lo, hi, (i1 - i0, c1 - c0), self.es, lo) if (c0 == 0 and c1 == sh[1]) else V(ap, lo, hi, None, self.es, lo)
        ap = self.ap[:, i0, c0:c1]
        return V(ap, lo, hi, (c1 - c0,), self.es, lo)

    def p(self, p0, p1):
        key = (slice(p0, p1),) + (slice(None),) * (len(self.ap.shape) - 1)
        return V(self.ap[key], self.lo, self.hi, self.shape, self.es, self.off)

    def with_ap(self, ap):
        return V(ap, self.lo, self.hi, None, self.es, self.off)


class PV:
    def __init__(self, ap, bank):
        self.ap, self.bank = ap, bank

    def keys(self):
        return [("p", self.bank)]

    def __getitem__(self, key):
        return PV(self.ap[key], self.bank)


class Op:
    __slots__ = ("id", "eng", "fn", "deps", "dma", "marked", "sem", "val", "qidx")


class Sched:
    ENGS = ("pe", "act", "dve", "pool", "sp")
    EPOCH = 20000
    KDMA = 8

    def __init__(self):
        self.ops = []
        self.W = {}
        self.R = {}
        self.by_eng = {e: [] for e in self.ENGS}
        self.ndma = {"pool": 0, "sp": 0}
        self.out_dmas = []

    def add(self, eng, fn, reads=(), writes=(), dma=False, is_out=False):
        op = Op()
        op.id = len(self.ops)
        op.eng, op.fn, op.dma, op.marked = eng, fn, dma, False
        op.sem = op.val = None
        deps = set()
        W, R = self.W, self.R
        for r in reads:
            for k in r.keys():
                w = W.get(k)
                if w is not None:
                    o = self.ops[w]
                    if o.dma or dma or o.eng != eng or eng != "pe":
                        deps.add(w)
        for wv in writes:
            for k in wv.keys():
                w = W.get(k)
                if w is not None:
                    o = self.ops[w]
                    if o.dma or dma or o.eng != eng or eng != "pe":
                        deps.add(w)
                rd = R.get(k)
                if rd:
                    for rk, rid in rd.items():
                        o = self.ops[rid]
                        if o.dma or dma or o.eng != eng or eng != "pe":
                            deps.add(rid)
                W[k] = op.id
                R[k] = {}
        for r in reads:
            for k in r.keys():
                R.setdefault(k, {})[("d", op.id) if dma else eng] = op.id
        deps.discard(op.id)
        if dma:
            q = self.ndma[eng]
            self.ndma[eng] = q + 1
            op.qidx = q
        op.deps = deps
        self.ops.append(op)
        self.by_eng[eng].append(op)
        if is_out:
            self.out_dmas.append(op)
        return op

    def emit(self, nc, block_engines, sems, dsems):
        for op in self.ops:
            for d in op.deps:
                self.ops[d].marked = True
        for e in self.ENGS:
            cnt = 0
            for op in self.by_eng[e]:
                if op.dma:
                    op.sem = dsems[e][op.qidx % self.KDMA]
                    op.val = 16 * (op.qidx // self.KDMA + 1)
                elif op.marked:
                    ep = cnt // self.EPOCH
                    op.sem = sems[e][ep]
                    op.val = cnt % self.EPOCH + 1
                    cnt += 1
        for e in self.ENGS:
            eh = block_engines[e]
            waited = {}

            def wait(sem, val):
                key = id(sem)
                if waited.get(key, 0) < val:
                    eh.wait_ge(sem, val)
                    waited[key] = val

            for op in self.by_eng[e]:
                for d in sorted(op.deps):
                    o = self.ops[d]
                    wait(o.sem, o.val)
                if op.dma and op.qidx >= self.KDMA:
                    wait(op.sem, op.val - 16)
                ins = op.fn(eh)
                if op.dma:
                    ins.then_inc(op.sem, 16)
                elif op.marked:
                    ins.then_inc(op.sem, 1)
            if e == "sp":
                for o in self.out_dmas:
                    wait(o.sem, o.val)


class Builder:
    def __init__(self, nc, n_sub, dbg):
        self.nc = nc
        self.S = Sched()
        self.n_sub = n_sub
        self.dbg = dbg
        self.off = 0
        self.bank_rr = 0

    def alloc(self, shape, dt):
        es = dsz(dt)
        n = int(np.prod(shape))
        nbytes = (n * es + 255) // 256 * 256
        off = self.off
        self.off += nbytes
        assert self.off <= ARENA_BYTES, f"arena overflow {self.off}"
        w0 = off // 4
        ap = self.arena[:, w0:w0 + nbytes // 4]
        if dt != F32:
            ap = ap.bitcast(dt)
        ap = ap[:, 0:n]
        if len(shape) == 2:
            ap = ap.rearrange("p (a b) -> p a b", a=shape[0])
        return V(ap, off, off + n * es, tuple(shape), es, off)

    def mark(self):
        return self.off

    def release(self, m):
        self.off = m

    def bank(self, pool=None):
        pool = pool or self.rot
        b = pool[self.bank_rr % len(pool)]
        self.bank_rr += 1
        return self.banks[b]

    def mm(self, out, lhsT, rhs, start, stop):
        self.S.add("pe", lambda e: e.matmul(out.ap, lhsT=lhsT.ap, rhs=rhs.ap, start=start, stop=stop, skip_group_check=True),
                   reads=[lhsT, rhs], writes=[out])

    def tr(self, out, in_, ident):
        self.S.add("pe", lambda e: e.transpose(out.ap, in_.ap, ident.ap), reads=[in_, ident], writes=[out])

    def act(self, out, in_, func, scale=1.0, bias=None, extra_reads=()):
        kw = {}
        rd = [in_] + list(extra_reads)
        if isinstance(scale, V):
            rd.append(scale)
            sc = scale.ap
        else:
            sc = scale
        if bias is not None:
            rd.append(bias)
            kw["bias"] = bias.ap
        self.S.add("act", lambda e: e.activation(out=out.ap, in_=in_.ap, func=func, scale=sc, **kw), reads=rd, writes=[out])

    def tt(self, out, a, b, op, eng="dve"):
        self.S.add(eng, lambda e: e.tensor_tensor(out=out.ap, in0=a.ap, in1=b.ap, op=op), reads=[a, b], writes=[out])

    def ts(self, out, a, s1, op0, s2=None, op1=None, eng="dve"):
        rd = [a]
        s1a = s1.ap if isinstance(s1, V) else s1
        s2a = s2.ap if isinstance(s2, V) else s2
        if isinstance(s1, V):
            rd.append(s1)
        if isinstance(s2, V):
            rd.append(s2)
        if op1 is None:
            self.S.add(eng, lambda e: e.tensor_scalar(out=out.ap, in0=a.ap, scalar1=s1a, scalar2=None, op0=op0), reads=rd, writes=[out])
        else:
            self.S.add(eng, lambda e: e.tensor_scalar(out=out.ap, in0=a.ap, scalar1=s1a, scalar2=s2a, op0=op0, op1=op1), reads=rd, writes=[out])

    def stt(self, out, a, s, b, op0, op1):
        rd = [a, b]
        sa = s.ap if isinstance(s, V) else s
        if isinstance(s, V):
            rd.append(s)
        self.S.add("dve", lambda e: e.scalar_tensor_tensor(out=out.ap, in0=a.ap, scalar=sa, in1=b.ap, op0=op0, op1=op1), reads=rd, writes=[out])

    def copy(self, out, in_, eng="dve"):
        self.S.add(eng, lambda e: e.tensor_copy(out=out.ap, in_=in_.ap), reads=[in_], writes=[out])

    def scan(self, out, d0, d1, init, extra_reads=()):
        rd = [d0, d1] + list(extra_reads)
        ia = init.ap if isinstance(init, V) else init
        if isinstance(init, V):
            rd.append(init)
        self.S.add("dve", lambda e: e.tensor_tensor_scan(out=out.ap, data0=d0.ap, data1=d1.ap, initial=ia, op0=ALU.mult, op1=ALU.add), reads=rd, writes=[out])

    def recip(self, out, in_):
        self.S.add("dve", lambda e: e.reciprocal(out=out.ap, in_=in_.ap), reads=[in_], writes=[out])

    def memset(self, out, val, eng="dve"):
        self.S.add(eng, lambda e: e.memset(out.ap, val), writes=[out])

    def dma(self, out, in_ap, q="pool", reads=(), **kw):
        self.S.add(q, lambda e: e.dma_start(out=out.ap, in_=in_ap, **kw), reads=list(reads), writes=[out], dma=True)

    def dma_out(self, out_ap, in_):
        self.S.add("sp", lambda e: e.dma_start(out=out_ap, in_=in_.ap), reads=[in_], dma=True, is_out=True)

    def wslot(self, kc, ncols):
        s = self.ring[self.ring_i % len(self.ring)]
        self.ring_i += 1
        ap = s.ap[:, 0:kc * ncols].rearrange("p (k c) -> p k c", k=kc)
        return V(ap, s.lo, s.lo + kc * ncols * 2, (kc, ncols), 2, s.lo)

    def wslot_sub(self, kc, ncols):
        assert kc * ncols <= 1024
        i = self.sub_i % (4 * len(self.ring))
        self.sub_i += 1
        s = self.ring[i // 4][(i % 4) * 1024:(i % 4) * 1024 + kc * ncols]
        ap = s.ap.rearrange("p (k c) -> p k c", k=kc)
        return V(ap, s.lo, s.lo + kc * ncols * 2, (kc, ncols), 2, s.lo)

    def wload(self, w2d, kc, c0, ncols, slot=None, scol=0):
        if slot is None:
            slot = self.wslot(kc, ncols)
            dst = slot
        else:
            dst = slot[0:kc, scol:scol + ncols]
        src = w2d[:, c0:c0 + ncols].rearrange("(k p) c -> p k c", p=128)
        self.dma(dst, src, q="pool")
        return slot

    def sumsq_rstd(self, srcs, rstd):
        ps = self.bank()
        for k in range(8):
            sq = self.sqb[k % 2]
            self.act(sq, srcs[k], AF.Square)
            self.mm(ps, self.ones_bf, sq, k == 0, k == 7)
        self.act(rstd, ps, AF.Sqrt, scale=1.0 / D, bias=self.epsb)
        self.recip(rstd, rstd)

    def pre_norm(self, sub, hT, t0, ntt):
        for tt in range(ntt):
            ts_ = t0 + tt * 512
            rstd = self.rstd[tt % 2]
            self.sumsq_rstd([self.xT[k, ts_:ts_ + 512] for k in range(8)], rstd)
            for k in range(8):
                tmp = self.tmpf[k % 2]
                self.tt(tmp, self.xT[k, ts_:ts_ + 512], rstd, ALU.mult)
                self.act(hT[k, tt * 512:(tt + 1) * 512], tmp, AF.Identity, scale=self.coefA[sub][k:k + 1], bias=self.coefB[sub][k:k + 1])

    def post_norm_res(self, sub, fT, t0, ntt):
        for tt in range(ntt):
            ts_ = t0 + tt * 512
            rstd = self.rstd[tt % 2]
            self.sumsq_rstd([fT[k, tt * 512:(tt + 1) * 512] for k in range(8)], rstd)
            for k in range(8):
                tmp = self.tmpf[k % 2]
                self.tt(tmp, fT[k, tt * 512:(tt + 1) * 512], rstd, ALU.mult)
                xs = self.xT[k, ts_:ts_ + 512]
                self.stt(xs, tmp, self.coefG[sub][k:k + 1], xs, ALU.mult, ALU.add)

    def ffn(self, l, sub, fi, bg=None):
        w_in = self.w["w_ff_in"][l, fi]
        w_out = self.w["w_ff_out"][l, fi]
        for hf in range(2):
            t0 = hf * 1024
            m = self.mark()
            hT = self.alloc([8, 1024], BF16)
            gT = self.alloc([NJ, 1024], BF16)
            fT = self.alloc([8, 1024], F32)
            self.pre_norm(sub, hT, t0, 2)
            for jg in range(NJ // 2):
                slot = self.wslot(8, 512)
                self.wload(w_in, 8, jg * 256, 256, slot, 0)
                self.wload(w_in, 8, DFF + jg * 256, 256, slot, 256)
                for jj in range(2):
                    j = jg * 2 + jj
                    for tt in range(2):
                        pa = self.bank()
                        pb = self.bank()
                        for k in range(8):
                            self.mm(pa, slot[k, jj * 128:(jj + 1) * 128], hT[k, tt * 512:(tt + 1) * 512], k == 0, k == 7)
                        for k in range(8):
                            self.mm(pb, slot[k, 256 + jj * 128:256 + (jj + 1) * 128], hT[k, tt * 512:(tt + 1) * 512], k == 0, k == 7)
                        sa = self.tmpf[(j * 2 + tt) % 2]
                        self.act(sa, pa, AF.Silu)
                        self.tt(gT[j, tt * 512:(tt + 1) * 512], sa, pb, ALU.mult)
                if bg is not None:
                    next(bg, None)
            for n in range(8):
                slot = self.wload(w_out, NJ, n * 128, 128)
                for tt in range(2):
                    pf = self.bank()
                    for k in range(NJ):
                        self.mm(pf, slot[k, 0:128], gT[k, tt * 512:(tt + 1) * 512], k == 0, k == NJ - 1)
                    self.act(fT[n, tt * 512:(tt + 1) * 512], pf, AF.Identity)
            self.post_norm_res(sub, fT, t0, 2)
            self.release(m)
        if bg is not None:
            for _ in bg:
                pass

    def layer_params(self, l):
        w = self.w
        vr = self.vrows
        cA, cB, cG = self.coef_sets[l % 2]
        self.dma(vr.p(0, 72), w["b_ada"][l].rearrange("(j p) -> j p", p=128), q="sp")
        self.dma(vr.p(72, 96), w["g_pre"][l].rearrange("s (k p) -> (s k) p", p=128), q="sp")
        self.dma(vr.p(96, 120), w["g_post"][l].rearrange("s (k p) -> (s k) p", p=128), q="sp")
        ps = self.bank()
        self.tr(ps[:, 0:120], vr.p(0, 120), self.ident.p(0, 120)[0:120])
        self.copy(self.vecs[0:120], ps[:, 0:120])
        yield
        ada = self.ada
        for cg in range(18):
            slot = self.wload(w["w_ada"][l], 8, cg * 512, 512)
            for jj in range(4):
                j = cg * 4 + jj
                pa = self.bank()
                for k in range(8):
                    self.mm(pa[:, 0:2], slot[k, jj * 128:(jj + 1) * 128], self.cT2[k * 2:k * 2 + 2], k == 0, k == 7)
                self.tt(ada[j:j + 1], pa[:, 0:1], self.vecs[j:j + 1], ALU.add)
            yield
        for s in range(3):
            resw = 1.0 if s == 1 else 0.5
            self.stt(cA[s], ada[s * 24 + 8:s * 24 + 16], 1.0, self.vecs[72 + s * 8:72 + s * 8 + 8], ALU.add, ALU.mult)
            self.copy(cB[s], ada[s * 24:s * 24 + 8])
            self.ts(self.tmp8, ada[s * 24 + 16:s * 24 + 24], 1.0, ALU.add, resw, ALU.mult)
            self.tt(cG[s], self.tmp8, self.vecs[96 + s * 8:96 + s * 8 + 8], ALU.mult)

    def setup(self):
        nc = self.nc
        al = self.alloc
        self.xT = al([8, L], F32)
        self.ident = al([128], F32)
        self.ones_bf = al([128], BF16)
        self.onesf = al([128], F32)
        self.zerosf = al([512], F32)
        self.epsb = al([1], F32)
        self.one1 = al([1], F32)
        self.cT = al([8], BF16)
        self.cT2 = al([16], BF16)
        self.cTf = al([8], F32)
        self.vrows = al([128], F32)
        self.vecs = al([128], F32)
        self.ada = al([72], F32)
        self.coef_sets = []
        for _ in range(2):
            cb = al([72], F32)
            self.coef_sets.append(tuple([cb[(g * 3 + s_) * 8:(g * 3 + s_ + 1) * 8] for s_ in range(3)] for g in range(3)))
        self.coefA, self.coefB, self.coefG = self.coef_sets[0]
        self.tmp8 = al([8], F32)
        self.ring = [al([4096], BF16) for _ in range(3)]
        self.ring_i = 0
        self.sub_i = 0
        self.sqb = [al([512], BF16) for _ in range(2)]
        self.rstd = [al([512], F32) for _ in range(2)]
        self.tmpf = [al([512], F32) for _ in range(2)]
        self.memset(self.onesf, 1.0, "pool")
        self.memset(self.zerosf, 0.0, "pool")
        self.memset(self.epsb, EPS, "pool")
        self.memset(self.one1, 1.0, "pool")
        self.memset(self.vrows, 0.0, "pool")
        self.S.add("pool", lambda e: e.affine_select(out=self.ident.ap, in_=self.onesf.ap, pattern=[[-1, 128]], compare_op=ALU.is_equal,
                                                     fill=0.0, base=0, channel_multiplier=1), reads=[self.onesf], writes=[self.ident])
        self.copy(self.ones_bf, self.onesf)
        import os
        crow = self.vrows
        if not os.environ.get("SKIPC"):
            self.dma(crow.p(0, 8), self.w["c"].rearrange("(k p) -> k p", p=128), q="sp")
            ps = self.bank()
            self.tr(ps[:, 0:8], crow.p(0, 8), self.ident.p(0, 8)[0:8])
            self.act(self.cTf, ps[:, 0:8], AF.Silu)
            self.copy(self.cT, self.cTf)
            self.copy(self.cT2.with_ap(self.cT2.ap.rearrange("p (k t) -> p k t", t=2)), self.cTf.with_ap(self.cTf.ap.unsqueeze(2).to_broadcast([128, 8, 2])))
        mk = self.mark()
        self.xrow = [self.alloc([1024], F32) for _ in range(2)]
        xin = self.w["x"]
        for tb in range(16):
            xr = self.xrow[tb % 2]
            self.dma(xr, xin[tb * 128:(tb + 1) * 128, :], q="sp")
            for k in range(8):
                ps = self.bank()
                self.tr(ps[:, 0:128], xr[k * 128:(k + 1) * 128], self.ident)
                if k % 2 == 0:
                    self.copy(self.xT[k, tb * 128:(tb + 1) * 128], ps[:, 0:128])
                else:
                    self.act(self.xT[k, tb * 128:(tb + 1) * 128], ps[:, 0:128], AF.Identity)
        self.release(mk)

    def store_out(self):
        out = self.out_ap
        self.xrow = [self.alloc([1024], F32) for _ in range(2)]
        for tb in range(16):
            xr = self.xrow[tb % 2]
            for k in range(8):
                ps = self.bank()
                self.tr(ps[:, 0:128], self.xT[k, tb * 128:(tb + 1) * 128], self.ident)
                if k % 2 == 0:
                    self.copy(xr[k * 128:(k + 1) * 128], ps[:, 0:128])
                else:
                    self.act(xr[k * 128:(k + 1) * 128], ps[:, 0:128], AF.Identity)
            self.dma_out(out[tb * 128:(tb + 1) * 128, :], xr)

    def bc3(self, v, n, m):
        return v.with_ap(v.ap.unsqueeze(2).to_broadcast([128, n, m]))

    def v3(self, v, b):
        if isinstance(v, PV):
            return PV(v.ap.rearrange("p (a b) -> p a b", b=b), v.bank)
        return v.with_ap(v.ap.rearrange("p (a b) -> p a b", b=b))

    def tb3(self, v, n, b):
        return v.with_ap(v.ap.unsqueeze(1).to_broadcast([128, n, b]))

    def pool_op(self, fn, reads, writes):
        self.S.add("pool", fn, reads=reads, writes=writes)

    def mixer_consts(self, l):
        al = self.alloc
        w = self.w
        mc = {}
        mc["ident_bf"] = al([128], BF16)
        self.copy(mc["ident_bf"], self.ident)
        mc["tri"] = al([128], BF16)
        mc["negones"] = al([128], BF16)
        negf = self.tmpf[0][0:128]
        trif = self.tmpf[1][0:128]
        self.memset(negf, -1.0)
        self.copy(mc["negones"], negf)
        self.pool_op(lambda e: e.affine_select(out=trif.ap, in_=negf.ap, pattern=[[-1, 128]], compare_op=ALU.is_ge, fill=0.0, base=0,
                                               channel_multiplier=1), [negf], [trif])
        self.copy(mc["tri"], trif)
        vr = self.vrows
        self.dma(vr.p(0, 8), w["lam_re"][l].rearrange("(t g) p -> t (g p)", g=2), q="sp")
        self.dma(vr.p(8, 16), w["lam_im"][l].rearrange("(t g) p -> t (g p)", g=2), q="sp")
        self.dma(vr.p(16, 22), w["conv_w"][l].rearrange("k (c p) -> (k c) p", p=128), q="sp")
        self.dma(vr.p(22, 24), w["ssm_d"][l].rearrange("(c p) -> c p", p=128), q="sp")
        self.dma(vr.p(24, 26), w["pool_scale"][l].rearrange("(c p) -> c p", p=128), q="sp")
        ps = self.bank()
        self.tr(ps[:, 0:26], vr.p(0, 26), self.ident.p(0, 26)[0:26])
        mc["vecs2"] = al([32], F32)
        self.copy(mc["vecs2"][0:26], ps[:, 0:26])
        return mc

    def sincos(self, dsin, dcos, y, yi, ta, tb):
        self.copy(yi, y)
        self.copy(ta, yi)
        self.tt(tb, y, ta, ALU.subtract)
        self.act(dsin, tb, AF.Sin, scale=6.283185)
        self.ts(tb, y, 0.25, ALU.add)
        self.copy(yi, tb)
        self.copy(ta, yi)
        self.tt(tb, tb, ta, ALU.subtract)
        self.act(dcos, tb, AF.Sin, scale=6.283185)

    def ssm_branch(self, l, hT, ub, mc):
        w = self.w
        al = self.alloc
        MUL, ADD, SUB = ALU.mult, ALU.add, ALU.subtract
        m = self.mark()
        v2 = mc["vecs2"]
        lrT, liT = v2[0:8], v2[8:16]
        dcol = v2[22:24]
        sp = al([20 * 8], F32)
        c8 = [sp[i * 8:(i + 1) * 8] for i in range(20)]
        lr, dt, a_, th, mag, thn, ta8, tb8, sn, cs, abr, abi, den, nr, fre, fim, t8, nEi, inre, inim = c8
        yi8 = al([8], I32)
        NT = TC + 1
        lb = al([16, 128], BF16)
        lc = al([16, 128], BF16)
        sinT = al([8, NT], F32)
        cosT = al([8, NT], F32)
        ms_ = self.mark()
        rows = al([272], F32)
        E0 = rows[0:128].p(0, 1)
        E1 = rows[128:256].p(0, 1)
        ld = rows[256:272].p(0, 1)
        self.memset(rows.p(0, 1), 0.0)
        self.memset(rows[0:64].p(0, 1), 1.0)
        self.memset(rows[192:256].p(0, 1), 1.0)
        self.dma(ld, w["log_dt"][l].unsqueeze(0), q="sp")
        ps = self.bank()
        self.mm(ps[:, 0:8], E0, ld.with_ap(ld.ap[:, 0:16:2]), True, False)
        self.mm(ps[:, 0:8], E1, ld.with_ap(ld.ap[:, 1:16:2]), False, True)
        self.act(dt, ps[:, 0:8], AF.Exp)
        self.ts(lr, lrT, -1e-4, ALU.min)
        self.tt(a_, lr, dt, MUL)
        self.tt(th, liT, dt, MUL)
        self.act(mag, a_, AF.Exp)
        self.ts(thn, th, 1.0 / (2 * math.pi), MUL)
        self.sincos(sn, cs, thn, yi8, ta8, tb8)
        self.tt(abr, mag, cs, MUL)
        self.tt(abi, mag, sn, MUL)
        self.tt(den, lr, lr, MUL)
        self.tt(t8, liT, liT, MUL)
        self.tt(den, den, t8, ADD)
        self.recip(den, den)
        self.ts(nr, abr, -1.0, ADD)
        self.tt(fre, nr, lr, MUL)
        self.tt(t8, abi, liT, MUL)
        self.tt(fre, fre, t8, ADD)
        self.tt(fre, fre, den, MUL)
        self.tt(fim, abi, lr, MUL)
        self.tt(t8, nr, liT, MUL)
        self.tt(fim, fim, t8, SUB)
        self.tt(fim, fim, den, MUL)
        if int(os.environ.get("SSMSTOP", "9")) <= 1:
            self.memset(ub, 0.0)
            self.release(m)
            return
        pidx = al([1], I32)
        pj = al([1], I32)
        pf = al([1], F32)
        rm = al([4], F32)
        par = al([2], F32)
        self.pool_op(lambda e: e.iota(pidx.ap, pattern=[[0, 1]], base=0, channel_multiplier=1), [], [pidx])
        self.S.add("dve", lambda e: e.tensor_scalar(out=pj.ap, in0=pidx.ap, scalar1=5, scalar2=None, op0=ALU.logical_shift_right), reads=[pidx], writes=[pj])
        self.copy(pf, pj)
        for q in range(4):
            self.ts(rm[q:q + 1], pf, float(q), ALU.is_equal)
        self.S.add("dve", lambda e: e.tensor_scalar(out=pj.ap, in0=pidx.ap, scalar1=4, scalar2=1, op0=ALU.logical_shift_right, op1=ALU.bitwise_and),
                   reads=[pidx], writes=[pj])
        self.copy(par[1:2], pj)
        self.ts(par[0:1], par[1:2], -1.0, MUL, 1.0, ADD)
        Bre = al([8, 32], F32)
        Bim = al([8, 32], F32)
        bbr = al([8, 32], F32)
        bbi = al([8, 32], F32)
        tb1 = al([8, 32], F32)
        for Bt, nm in ((Bre, "ssm_b_re"), (Bim, "ssm_b_im")):
            self.memset(Bt, 0.0)
            src = w[nm][l].rearrange("(t g) p h -> g p t h", g=2)
            self.dma(Bt.with_ap(Bt.ap[0:64, :, 0:16]), src[0], q="sp")
            self.dma(Bt.with_ap(Bt.ap[64:128, :, 16:32]), src[1], q="sp")
        fr3 = self.bc3(fre, 8, 32)
        fi3 = self.bc3(fim, 8, 32)
        self.tt(bbr, Bre, fr3, MUL)
        self.tt(tb1, Bim, fi3, MUL)
        self.tt(bbr, bbr, tb1, SUB)
        self.tt(bbi, Bim, fr3, MUL)
        self.tt(tb1, Bre, fi3, MUL)
        self.tt(bbi, bbi, tb1, ADD)
        for ri, bb in enumerate((bbr, bbi)):
            for c in range(2):
                ps = self.bank()
                src = bb.with_ap(bb.ap[:, 4 * c:4 * c + 4, :].rearrange("p a b -> p (a b)"))
                self.tr(ps[:, 0:128], src, self.ident)
                for q in range(4):
                    self.ts(lb[ri * 8 + 4 * c + q], ps[:, 0:128], rm[q:q + 1], MUL)
        if int(os.environ.get("SSMSTOP", "9")) <= 2:
            self.memset(ub, 0.0)
            self.release(m)
            return
        self.memset(lc, 0.0)
        InC = al([128], F32)
        for ri, nm in enumerate(("ssm_c_re", "ssm_c_im")):
            for c in range(2):
                src = w[nm][l].rearrange("g h p -> (g h) p")[c * 128:(c + 1) * 128, :]
                self.dma(InC[0:64], src, q="sp")
                self.dma(InC[64:128], src, q="sp")
                self.ts(InC[0:64], InC[0:64], par[0:1], MUL)
                self.ts(InC[64:128], InC[64:128], par[1:2], MUL)
                ps = self.bank()
                self.tr(ps[:, 0:128], InC, self.ident)
                for q in range(4):
                    dst = lc[ri * 8 + 4 * c + q][32 * q:32 * q + 32]
                    if ri == 0:
                        self.copy(dst, ps[:, 32 * q:32 * q + 32])
                    else:
                        self.ts(dst, ps[:, 32 * q:32 * q + 32], -1.0, MUL)
        if int(os.environ.get("SSMSTOP", "9")) <= 3:
            self.memset(ub, 0.0)
            self.release(m)
            return
        iot_i = al([NT], I32)
        iot_f = al([NT], F32)
        self.pool_op(lambda e: e.iota(iot_i.ap, pattern=[[1, NT]], base=0, channel_multiplier=0), [], [iot_i])
        self.copy(iot_f, iot_i)
        mt = self.mark()
        yt = al([4, NT], F32)
        yi = al([4, NT], I32)
        ta = al([4, NT], F32)
        tb = al([4, NT], F32)
        for h4 in range(2):
            for T in range(4):
                self.ts(yt[T], iot_f, thn[4 * h4 + T:4 * h4 + T + 1], MUL)
            self.sincos(sinT[4 * h4:4 * h4 + 4], cosT[4 * h4:4 * h4 + 4], yt, yi, ta, tb)
        self.release(mt)
        for T in range(8):
            self.ts(nEi[T:T + 1], sinT[T, TC:TC + 1], -1.0, MUL)
        if int(os.environ.get("SSMSTOP", "9")) <= 4:
            self.memset(ub, 0.0)
            self.release(m)
            return
        self.release(ms_)
        uS = al([1024], F32)
        uSb = al([1024], BF16)
        sre = al([1024], F32)
        sim = al([1024], F32)
        sbre = al([1024], BF16)
        sbim = al([1024], BF16)
        sre2 = al([1024], F32)
        sim2 = al([1024], F32)
        tA = al([512], F32)
        tB = al([512], F32)
        yv = tA
        y2 = tB
        t1c = al([1], F32)
        wsl = self.wload(w["w_in"][l], 8, 768, 256)
        ybanks = [self.banks[6], self.banks[7]]
        rot = [0, 1, 2, 3, 4, 5]
        nb = 512 // TC
        NCH = 1024 // TC
        for c in range(2):
            for hf in range(2):
                t0 = hf * 1024
                for tt in range(2):
                    lt = slice(tt * 512, (tt + 1) * 512)
                    ps = self.bank(rot)
                    for k in range(8):
                        self.mm(ps, wsl[k, c * 128:(c + 1) * 128], hT[k, t0 + tt * 512:t0 + (tt + 1) * 512], k == 0, k == 7)
                    self.act(uS[lt], ps, AF.Identity)
                    self.copy(uSb[lt], ps)
                for q in range(4):
                    T = 4 * c + q
                    cb = self.tb3(cosT[T, 0:TC], nb, TC)
                    sb_ = self.tb3(sinT[T, 0:TC], nb, TC)
                    Er = cosT[T, TC:TC + 1]
                    Ei = sinT[T, TC:TC + 1]
                    magb = mag[T:T + 1].with_ap(mag[T:T + 1].ap.to_broadcast([128, TC]))
                    tA3, tB3 = self.v3(tA, TC), self.v3(tB, TC)
                    for tt in range(2):
                        lt = slice(tt * 512, (tt + 1) * 512)
                        pr = self.bank(rot)
                        pi = self.bank(rot)
                        self.mm(pr, lb[T], uSb[lt], True, True)
                        self.mm(pi, lb[8 + T], uSb[lt], True, True)
                        self.act(sre[lt], pr, AF.Identity)
                        self.act(sim[lt], pi, AF.Identity)
                        r3, i3 = self.v3(sre[lt], TC), self.v3(sim[lt], TC)
                        self.tt(tA3, r3, cb, MUL)
                        self.tt(tB3, i3, sb_, MUL)
                        self.tt(sre2[lt], tA, tB, ADD)
                        self.tt(tA3, i3, cb, MUL)
                        self.tt(tB3, r3, sb_, MUL)
                        self.tt(sim2[lt], tA, tB, SUB)
                    for ch in range(NCH):
                        gch = hf * NCH + ch
                        seg = slice(ch * TC, (ch + 1) * TC)
                        self.scan(sre[seg], magb, sre2[seg], 0.0 if gch == 0 else inre[T:T + 1])
                        self.scan(sim[seg], magb, sim2[seg], 0.0 if gch == 0 else inim[T:T + 1])
                        if gch < 2 * NCH - 1:
                            lre = sre[(ch + 1) * TC - 1:(ch + 1) * TC]
                            lim = sim[(ch + 1) * TC - 1:(ch + 1) * TC]
                            self.ts(t1c, lre, Er, MUL)
                            self.ts(inre[T:T + 1], lim, nEi[T:T + 1], MUL, t1c, ADD)
                            self.ts(t1c, lim, Er, MUL)
                            self.ts(inim[T:T + 1], lre, Ei, MUL, t1c, ADD)
                    for tt in range(2):
                        lt = slice(tt * 512, (tt + 1) * 512)
                        sr3, si3 = self.v3(sre[lt], TC), self.v3(sim[lt], TC)
                        self.tt(tA3, sr3, cb, MUL)
                        self.tt(tB3, si3, sb_, MUL)
                        self.tt(sbre[lt], tA, tB, SUB)
                        self.tt(tA3, si3, cb, MUL)
                        self.tt(tB3, sr3, sb_, MUL)
                        self.tt(sbim[lt], tA, tB, ADD)
                        self.mm(ybanks[tt], lc[T], sbre[lt], q == 0, False)
                        self.mm(ybanks[tt], lc[8 + T], sbim[lt], False, q == 3)
                for tt in range(2):
                    lt = slice(tt * 512, (tt + 1) * 512)
                    self.ts(yv, uS[lt], dcol[c:c + 1], MUL)
                    self.tt(yv, yv, ybanks[tt], ADD)
                    self.tt(y2, yv, yv, MUL)
                    self.ts(y2, y2, 1.5957691216 * 0.044715, MUL, 1.5957691216, ADD)
                    self.tt(y2, y2, yv, MUL)
                    self.act(y2, y2, AF.Sigmoid)
                    self.tt(ub[c, t0 + tt * 512:t0 + (tt + 1) * 512], yv, y2, MUL)
        self.release(m)

    def conv_branch(self, l, hT, ua, mc):
        al = self.alloc
        win = self.w["w_in"][l]
        MUL, ADD = ALU.mult, ALU.add
        m = self.mark()
        v2 = mc["vecs2"]
        upad = al([2 + L], F32)
        csb = al([512], F32)
        yt = al([512], F32)
        s1 = self.wload(win, 8, 0, 512)
        s2 = self.wload(win, 8, 512, 256)
        for i in range(2):
            self.memset(upad[0:2], 0.0)
            for tt in range(4):
                tok = slice(tt * 512, (tt + 1) * 512)
                pc = self.bank()
                pv = self.bank()
                pb = self.bank()
                for k in range(8):
                    self.mm(pc, s1[k, 256 + i * 128:256 + (i + 1) * 128], hT[k, tok], k == 0, k == 7)
                for k in range(8):
                    self.mm(pv, s2[k, i * 128:(i + 1) * 128], hT[k, tok], k == 0, k == 7)
                for k in range(8):
                    self.mm(pb, s1[k, i * 128:(i + 1) * 128], hT[k, tok], k == 0, k == 7)
                self.act(csb, pc, AF.Identity)
                b0 = tt * 512
                self.tt(upad[2 + b0:2 + b0 + 512], csb, pv, MUL)
                self.ts(yt, upad[b0:b0 + 512], v2[16 + i:17 + i], MUL)
                self.stt(yt, upad[1 + b0:1 + b0 + 512], v2[18 + i:19 + i], yt, MUL, ADD)
                self.stt(yt, upad[2 + b0:2 + b0 + 512], v2[20 + i:21 + i], yt, MUL, ADD)
                self.tt(ua[i, tok], yt, pb, MUL)
        self.release(m)

    def pool_branch(self, l, hT, uc, mc):
        al = self.alloc
        w = self.w
        win = w["w_in"][l]
        MUL, ADD, SUB = ALU.mult, ALU.add, ALU.subtract
        m = self.mark()
        v2 = mc["vecs2"]
        uP = al([L], F32)
        csp = al([16 + L], F32)
        dP = al([L], F32)
        pl = al([L], BF16)
        lp = al([2, 128], BF16)
        invc = al([2, 16], F32)
        t16 = al([16], F32)
        io_i = al([16], I32)
        io_f = al([16], F32)
        self.pool_op(lambda e: e.iota(io_i.ap, pattern=[[1, 16]], base=1, channel_multiplier=0), [], [io_i])
        self.copy(io_f, io_i)
        for c in range(2):
            for h2 in range(2):
                self.ts(invc[c].p(64 * h2, 64 * h2 + 64), io_f.p(64 * h2, 64 * h2 + 64), float(POOL_W[2 * c + h2]), ALU.min)
        self.recip(invc, invc)
        self.memset(lp, 0.0)
        for c in range(2):
            for h2 in range(2):
                self.dma(lp[c].p(64 * h2, 64 * h2 + 64)[64 * h2:64 * h2 + 64], w["w_pool"][l, 2 * c + h2], q="pool")
        sl = self.wload(win, 8, 1024, 256)
        onesb = self.onesf[0:1].with_ap(self.onesf[0:1].ap.to_broadcast([128, 512]))
        self.memset(csp[0:16], 0.0)
        for c in range(2):
            for tt in range(4):
                tok = slice(tt * 512, (tt + 1) * 512)
                ps = self.bank()
                for k in range(8):
                    self.mm(ps, sl[k, c * 128:(c + 1) * 128], hT[k, tok], k == 0, k == 7)
                self.act(uP[tok], ps, AF.Identity)
                self.scan(csp[16 + tt * 512:16 + (tt + 1) * 512], onesb, uP[tok], csp[15 + tt * 512:16 + tt * 512])
            for h2 in range(2):
                w_ = POOL_W[2 * c + h2]
                P = (64 * h2, 64 * h2 + 64)
                self.tt(dP.p(*P), csp[16:16 + L].p(*P), csp[16 - w_:16 - w_ + L].p(*P), SUB)
                self.stt(pl.p(*P), dP.p(*P), 1.0 / w_, uP.p(*P), MUL, SUB)
            self.tt(t16, dP[0:16], invc[c], MUL)
            self.tt(pl[0:16], t16, uP[0:16], SUB)
            for tt in range(4):
                tok = slice(tt * 512, (tt + 1) * 512)
                ps = self.bank()
                self.mm(ps, lp[c], pl[tok], True, True)
                self.act(uc[c, tok], ps, AF.Identity, scale=v2[24 + c:25 + c])
        self.release(m)

    def attn_branch(self, l, hT, ud, mc):
        al = self.alloc
        win = self.w["w_in"][l]
        m = self.mark()
        qT = al([L], BF16)
        kT = al([L], BF16)
        vS = al([16, 128], BF16)
        e_ = al([512], F32)
        spb = [al([512], BF16) for _ in range(2)]
        At = [al([512], F32) for _ in range(2)]
        Gts = [al([512], F32) for _ in range(2)]
        Ab = [al([512], BF16) for _ in range(2)]
        maskb = []
        for i in range(4):
            mb = al([512], BF16)
            tmp = self.rstd[i % 2]
            self.pool_op((lambda tmp, i: lambda e: e.affine_select(out=tmp.ap, in_=self.zerosf.ap, pattern=[[1, 512]], compare_op=ALU.is_gt,
                                                                  fill=-30000.0, base=-128 * i, channel_multiplier=-1))(tmp, i),
                         [self.zerosf], [tmp])
            self.copy(mb, tmp)
            maskb.append(mb)
        mc = dict(mc, maskb=maskb)
        rot = [0, 1, 2, 3]
        ui = 0
        for c in range(2):
            sq = self.wload(win, 8, 1280 + c * 128, 128)
            sk = self.wload(win, 8, 1536 + c * 128, 128)
            sv = self.wload(win, 8, 1792 + c * 128, 128)
            for tt in range(4):
                tok = slice(tt * 512, (tt + 1) * 512)
                ps = self.bank(rot)
                for k in range(8):
                    self.mm(ps, sq[k, 0:128], hT[k, tok], k == 0, k == 7)
                self.act(qT[tok], ps, AF.Identity, scale=0.125)
                ps = self.bank(rot)
                for k in range(8):
                    self.mm(ps, sk[k, 0:128], hT[k, tok], k == 0, k == 7)
                self.copy(kT[tok], ps)
            for tb in range(16):
                ps = self.bank(rot)
                for k in range(8):
                    self.mm(ps[:, 0:128], hT[k, tb * 128:(tb + 1) * 128], sv[k, 0:128], k == 0, k == 7)
                if tb % 2 == 0:
                    self.copy(vS[tb], ps[:, 0:128])
                else:
                    self.act(vS[tb], ps[:, 0:128], AF.Identity)
            units = []
            for hh in range(2):
                P = (64 * hh, 64 * hh + 64)
                for qt in range(4):
                    chain = hh * 4 + qt
                    Ob, Rb = (self.banks[4], self.banks[5]) if chain % 2 == 0 else (self.banks[6], self.banks[7])
                    ObP = PV(Ob.ap[P[0]:P[1], :], Ob.bank)
                    nkb = 4 * (qt + 1)
                    qv = qT[qt * 512:(qt + 1) * 512].p(*P)
                    for kb in range(nkb - 1, -1, -1):
                        units.append(dict(hh=hh, qt=qt, kb=kb, first=(kb == nkb - 1), last=(kb == 0), Rb=Rb, ObP=ObP, qv=qv, P=P,
                                          diag=(kb >= 4 * qt)))

            def stageA0(u, i):
                zb = self.bank(rot)
                u["zb"] = zb
                kb, P = u["kb"], u["P"]
                self.mm(zb, kT[kb * 128:(kb + 1) * 128].p(*P), u["qv"], True, not u["diag"])
                if u["diag"]:
                    self.mm(zb, mc["ident_bf"], mc["maskb"][kb - 4 * u["qt"]], False, True)

            def stageA1(u, i):
                zb = u["zb"]
                sp_ = spb[i % 2]
                G = Gts[i % 2]
                u["G"] = G
                kb = u["kb"]
                self.act(e_, zb, AF.Exp)
                self.act(sp_, e_, AF.Ln, bias=self.one1)
                if not u["first"]:
                    self.act(G, u["Rb"], AF.Exp)
                self.mm(zb, mc["tri"], sp_, False, True)
                if kb > 0:
                    self.mm(u["Rb"], mc["negones"], sp_, u["first"], True)

            def stageB(u, i):
                A = Ab[i % 2]
                at = At[i % 2]
                if u["first"]:
                    self.act(A, u["zb"], AF.Exp)
                else:
                    self.act(at, u["zb"], AF.Exp)
                    self.tt(A, at, u["G"], ALU.mult)
                hh, kb = u["hh"], u["kb"]
                self.mm(u["ObP"], vS[kb, 64 * hh:64 * hh + 64], A, u["first"], u["last"])
                if u["last"]:
                    qt = u["qt"]
                    self.act(ud[c, qt * 512:(qt + 1) * 512].p(*u["P"]), u["ObP"], AF.Identity)

            nu = len(units)
            stageA0(units[0], 0)
            for i in range(nu):
                if i + 1 < nu:
                    stageA0(units[i + 1], i + 1)
                stageA1(units[i], i)
                if i >= 1:
                    stageB(units[i - 1], i - 1)
            stageB(units[-1], nu - 1)
        self.release(m)

    def merge_out(self, l, hT, us):
        al = self.alloc
        w = self.w
        win = w["w_in"][l]
        MUL, ADD = ALU.mult, ALU.add
        ynames = ["w_conv_out", None, "w_pool_out", "w_sb_out"]
        rot = [0, 1, 2, 3, 4, 5, 6]
        pss = self.banks[7]
        for hf in range(2):
            t0 = hf * 1024
            m = self.mark()
            mg = al([8, 1024], BF16)
            acc = al([1024], F32)
            sg = al([512], F32)
            sg2 = al([512], F32)
            yt = al([512], F32)
            pr = al([512], F32)
            for n in range(8):
                for b in range(4):
                    sgw = self.wload(win, 8, 2048 + b * 1024 + n * 128, 128, self.wslot_sub(8, 128), 0)
                    if b == 1:
                        sy = self.wslot_sub(2, 256)
                        self.wload(w["w_glu"][l], 2, n * 128, 128, sy, 0)
                        self.wload(w["w_glu"][l], 2, 1024 + n * 128, 128, sy, 128)
                    else:
                        sy = self.wload(w[ynames[b]][l], 2, n * 128, 128, self.wslot_sub(2, 128), 0)
                    u = us[b]
                    for tt in range(2):
                        tok = slice(t0 + tt * 512, t0 + (tt + 1) * 512)
                        lt = slice(tt * 512, (tt + 1) * 512)
                        pg = self.bank(rot)
                        for k in range(8):
                            self.mm(pg, sgw[k, 0:128], hT[k, tok], k == 0, k == 7)
                        py = self.bank(rot)
                        for k in range(2):
                            self.mm(py, sy[k, 0:128], u[k, tok], k == 0, k == 1)
                        self.act(sg, pg, AF.Sigmoid)
                        dst = acc[lt] if b == 0 else pr
                        if b == 1:
                            pyg = self.bank(rot)
                            for k in range(2):
                                self.mm(pyg, sy[k, 128:256], u[k, tok], k == 0, k == 1)
                            self.act(sg2, pyg, AF.Sigmoid)
                            self.tt(yt, sg2, py, MUL)
                            self.tt(dst, sg, yt, MUL)
                        else:
                            self.tt(dst, sg, py, MUL)
                        if b in (1, 2):
                            self.tt(acc[lt], acc[lt], pr, ADD)
                        elif b == 3:
                            self.tt(mg[n, lt], acc[lt], pr, ADD)
            so = [self.wload(w["w_out"][l], 8, 0, 512), self.wload(w["w_out"][l], 8, 512, 512)]
            for tt in range(2):
                lt = slice(tt * 512, (tt + 1) * 512)
                ts_ = t0 + tt * 512
                rstd = self.rstd[tt % 2]
                for n in range(8):
                    ps = self.bank(rot)
                    for k in range(8):
                        self.mm(ps, so[n // 4][k, (n % 4) * 128:(n % 4 + 1) * 128], mg[k, lt], k == 0, k == 7)
                    sq = self.sqb[n % 2]
                    self.act(sq, ps, AF.Square)
                    self.mm(pss, self.ones_bf, sq, n == 0, n == 7)
                self.act(rstd, pss, AF.Sqrt, scale=1.0 / D, bias=self.epsb)
                self.recip(rstd, rstd)
                for n in range(8):
                    ps = self.bank(rot)
                    for k in range(8):
                        self.mm(ps, so[n // 4][k, (n % 4) * 128:(n % 4 + 1) * 128], mg[k, lt], k == 0, k == 7)
                    tmp = self.tmpf[n % 2]
                    self.tt(tmp, ps, rstd, MUL)
                    xs = self.xT[n, ts_:ts_ + 512]
                    self.stt(xs, tmp, self.coefG[1][n:n + 1], xs, MUL, ADD)
            self.release(m)

    def dump_u(self, i, u):
        if self.dbg:
            dst = self.dbg_ap[i]
            self.S.add("pool", lambda e: e.dma_start(out=dst, in_=u.ap), reads=[u], dma=True, is_out=True)

    def mixer(self, l):
        al = self.alloc
        m0 = self.mark()
        hT = al([8, L], BF16)
        self.pre_norm(1, hT, 0, 4)
        mc = self.mixer_consts(l)
        br = os.environ.get("BR", "scpam")
        ub = al([2, L], BF16)
        if "s" in br:
            self.ssm_branch(l, hT, ub, mc)
        else:
            self.memset(ub, 0.0)
        ua = al([2, L], BF16)
        if "c" in br:
            self.conv_branch(l, hT, ua, mc)
        else:
            self.memset(ua, 0.0)
        uc = al([2, L], BF16)
        if "p" in br:
            self.pool_branch(l, hT, uc, mc)
        else:
            self.memset(uc, 0.0)
        ud = al([2, L], BF16)
        if "a" in br:
            self.attn_branch(l, hT, ud, mc)
        else:
            self.memset(ud, 0.0)
        for i, u in enumerate((ua, ub, uc, ud)):
            self.dump_u(i, u)
        if "m" in br:
            self.merge_out(l, hT, (ua, ub, uc, ud))
        self.release(m0)


    def build(self):
        nc = self.nc
        w = {}
        shapes = dict(x=[L, D], c=[D], w_ada=[DEPTH, D, 9 * D], b_ada=[DEPTH, 9 * D], g_pre=[DEPTH, 3, D], g_post=[DEPTH, 3, D],
                      w_ff_in=[DEPTH, 2, D, 2 * DFF], w_ff_out=[DEPTH, 2, DFF, D], w_in=[DEPTH, D, 6144], conv_w=[DEPTH, 3, 256],
                      w_conv_out=[DEPTH, 256, D], lam_re=[DEPTH, 16, 64], lam_im=[DEPTH, 16, 64], log_dt=[DEPTH, 16],
                      ssm_b_re=[DEPTH, 16, 64, 16], ssm_b_im=[DEPTH, 16, 64, 16], ssm_c_re=[DEPTH, 16, 16, 64], ssm_c_im=[DEPTH, 16, 16, 64],
                      ssm_d=[DEPTH, 256], w_glu=[DEPTH, 256, 2 * D], w_pool=[DEPTH, 4, 64, 64], pool_scale=[DEPTH, 256],
                      w_pool_out=[DEPTH, 256, D], w_sb_out=[DEPTH, 256, D], w_out=[DEPTH, D, D])
        import os
        for k, s in shapes.items():
            if os.environ.get("ONLYX") and k not in ("x", "c"):
                continue
            w[k] = nc.dram_tensor(k, s, F32, kind="ExternalInput").ap()
        self.w = w
        self.out_ap = nc.dram_tensor("out", [L, D], F32, kind="ExternalOutput").ap()
        if self.dbg:
            self.dbg_ap = nc.dram_tensor("dbg_u", [4, 128, 2, L], F32, kind="ExternalOutput").ap()
        from contextlib import ExitStack
        with ExitStack() as es:
            arena = es.enter_context(nc.sbuf_tensor("arena", [128, ARENA_BYTES // 4], F32))
            self.arena = arena[:]
            self.banks = []
            for b in range(8):
                pt = es.enter_context(nc.psum_tensor(f"ps{b}", [128, 512], F32))
                self.banks.append(PV(pt[:], b))
            self.rot = list(range(8))
            sems = {e: [es.enter_context(nc.semaphore(f"s_{e}_{i}")) for i in range(4)] for e in ("pe", "act", "dve", "pool")}
            sems["sp"] = sems["pool"]
            dsems = {q: [es.enter_context(nc.semaphore(f"d_{q}_{i}")) for i in range(Sched.KDMA)] for q in ("pool", "sp")}
            m0 = self.mark()
            self.xrow = None
            self.xT_only = True
            self.setup_alloc_xrow = True
            self._setup_with_xrow()
            done = 0
            for l in range(DEPTH):
                if done >= self.n_sub:
                    break
                self.coefA, self.coefB, self.coefG = self.coef_sets[l % 2]
                if l == 0:
                    for _ in self.layer_params(l):
                        pass
                self.ffn(l, 0, 0)
                done += 1
                if done >= self.n_sub:
                    break
                self.mixer(l)
                done += 1
                if done >= self.n_sub:
                    break
                self.ffn(l, 2, 1, bg=self.layer_params(l + 1) if l + 1 < DEPTH else None)
                done += 1
            self.store_out()
            block = es.enter_context(nc.Block())
            S = self.S

            @block.tensor
            def _(e):
                S._cur = e
                S.emit_one(nc, "pe", e, sems, dsems)

            @block.scalar
            def _(e):
                S.emit_one(nc, "act", e, sems, dsems)

            @block.vector
            def _(e):
                S.emit_one(nc, "dve", e, sems, dsems)

            @block.gpsimd
            def _(e):
                S.emit_one(nc, "pool", e, sems, dsems)

            @block.sync
            def _(e):
                S.emit_one(nc, "sp", e, sems, dsems)
        return nc

    def _setup_with_xrow(self):
        orig_alloc = self.alloc
        self.xrow = None
        self.setup()


def _emit_one(self, nc, e, eh, sems, dsems):
    if not getattr(self, "_prepared", False):
        for op in self.ops:
            for d in op.deps:
                self.ops[d].marked = True
        for en in self.ENGS:
            cnt = 0
            for op in self.by_eng[en]:
                if op.dma:
                    op.sem = dsems[en][op.qidx % self.KDMA]
                    op.val = 16 * (op.qidx // self.KDMA + 1)
                elif op.marked:
                    ep = cnt // self.EPOCH
                    op.sem = sems[en][ep] if en != "sp" else None
                    op.val = cnt % self.EPOCH + 1
                    cnt += 1
        self._prepared = True
    waited = {}

    def wait(sem, val):
        key = id(sem)
        if waited.get(key, 0) < val:
            eh.wait_ge(sem, val)
            waited[key] = val

    for op in self.by_eng[e]:
        for d in sorted(op.deps):
            o = self.ops[d]
            wait(o.sem, o.val)
        if op.dma and op.qidx >= self.KDMA:
            wait(op.sem, op.val - 16)
        ins = op.fn(eh)
        if op.dma:
            ins.then_inc(op.sem, 16)
        elif op.marked:
            ins.then_inc(op.sem, 1)
    if e == "sp":
        for o in self.out_dmas:
            wait(o.sem, o.val)


Sched.emit_one = _emit_one

_CACHE = {}


def build_nc(n_sub=12, dbg=False):
    nc = bass.Bass("TRN2", target_bir_lowering=False)
    b = Builder(nc, n_sub, dbg)
    b.build()
    return nc


def kernel(n_sub=12, **inputs):
    x = np.ascontiguousarray(inputs["x"], dtype=np.float32)
    c = np.ascontiguousarray(inputs["c"], dtype=np.float32)
    nc = build_nc(n_sub)
    shared = {k: np.ascontiguousarray(v, dtype=np.float32) for k, v in inputs.items() if k not in ("x", "c")}
    in_maps = []
    for b in range(8):
        m = dict(shared)
        m["x"] = x[b]
        m["c"] = c[b]
        in_maps.append(m)
    res = run_bass_kernel_spmd(nc, in_maps, core_ids=list(range(8)))
    out = np.stack([res.results[b]["out"] for b in range(8)], axis=0)
    return out.astype(np.float32)
```
